# Optimizing a Trainium2 kernel written in Bass

```python
import jax, jax.numpy as jnp
from jax import lax
import numpy as np

D_MODEL = 1024
BATCH = 16
SEQ = 256
DEPTH = 4
DEC_BATCH = 8
DEC_SEQ = 1024
PAST_LEN = 512

GRID_W = 64
HEAD_DIM = 64
NA_HEADS = 8
NA_WIN_R = 8
NA_WIN_C = 16
CONV_C = 512
CONV_W = 31
GQA_HEADS = 8
GQA_KV_HEADS = 2
GQA_GROUP = GQA_HEADS // GQA_KV_HEADS
LRU_W = 512
LRU_BLOCKS = 8
LRU_CONV_W = 4
LRU_C = 8.0
N_BRANCH = 4
BRANCH_W = 512
D_FF = 2816
Q_BLOCK = 128
ROPE_THETA = 10000.0
EPS = 1e-6
NEG_INF = -1e30
HALF = 0.5

NA_W = NA_HEADS * HEAD_DIM
GQA_QW = GQA_HEADS * HEAD_DIM
GQA_KVW = GQA_KV_HEADS * HEAD_DIM
OFF_NA = 0
OFF_CONV = OFF_NA + 3 * NA_W
OFF_GQA = OFF_CONV + 2 * CONV_C
OFF_LRU = OFF_GQA + GQA_QW + 2 * GQA_KVW
OFF_GATE = OFF_LRU + LRU_W
IN_COLS = OFF_GATE + N_BRANCH * D_MODEL

kernel_name = 'hybrid_na_conformer_gqa_rglru_prefix_dit'


def rmsnorm(x, g):
    xf = x.astype(jnp.float32)
    y = xf * lax.rsqrt(jnp.mean(xf * xf, axis=-1, keepdims=True) + EPS)
    return (y * g.astype(jnp.float32)).astype(x.dtype)


def layernorm(x, g, b):
    xf = x.astype(jnp.float32)
    mu = jnp.mean(xf, axis=-1, keepdims=True)
    var = jnp.mean(jnp.square(xf - mu), axis=-1, keepdims=True)
    y = (xf - mu) * lax.rsqrt(var + EPS)
    return (y * g.astype(jnp.float32) + b.astype(jnp.float32)).astype(x.dtype)


def adaln(cvec, w, b):
    m = jax.nn.silu(cvec) @ w + b
    return m.reshape(cvec.shape[0], 3, 3, D_MODEL)


def modulate(x, g_pre, mod_s):
    return rmsnorm(x, g_pre) * (1.0 + mod_s[:, None, 1]) + mod_s[:, None, 0]


def add_residual(x, y, g_post, mod_s, res_w):
    return x + res_w * mod_s[:, None, 2] * rmsnorm(y, g_post)


def swiglu(h, w1, w2):
    a, u = jnp.split(h @ w1, 2, axis=-1)
    return (jax.nn.silu(a) * u) @ w2


def depthwise_conv(x, w, pad):
    return lax.conv_general_dilated(x, w[:, None, :].astype(x.dtype), window_strides=(1,), padding=[pad],
                                    dimension_numbers=('NWC', 'WIO', 'NWC'), feature_group_count=x.shape[-1])


def rope_2d(n):
    t = jnp.arange(n)
    row = (t // GRID_W).astype(jnp.float32)
    col = (t % GRID_W).astype(jnp.float32)
    half = HEAD_DIM // 2
    freqs = ROPE_THETA ** (-jnp.arange(0, half, 2, dtype=jnp.float32) / half)
    ar = row[:, None] * freqs
    ac = col[:, None] * freqs
    return jnp.cos(ar), jnp.sin(ar), jnp.cos(ac), jnp.sin(ac)


def apply_rope_2d(x, tabs):
    cr, sr, cc, sc = tabs
    xf = x.astype(jnp.float32)

    def rot(xh, c, s):
        x1, x2 = jnp.split(xh, 2, axis=-1)
        c = c[None, :, None, :]
        s = s[None, :, None, :]
        return jnp.concatenate([x1 * c - x2 * s, x1 * s + x2 * c], axis=-1)

    xr, xc = jnp.split(xf, 2, axis=-1)
    return jnp.concatenate([rot(xr, cr, sr), rot(xc, cc, sc)], axis=-1).astype(x.dtype)


def block_attention(q, k, v):
    B, N, Hk, G, Dh = q.shape
    qb = jnp.moveaxis(q.reshape(B, N // Q_BLOCK, Q_BLOCK, Hk, G, Dh), 1, 0)
    scale = Dh ** -0.5

    def one(qi):
        s = jnp.einsum('bqhgd,bkhd->bhgqk', qi, k, preferred_element_type=jnp.float32) * scale
        p = jax.nn.softmax(s, axis=-1)
        return jnp.einsum('bhgqk,bkhd->bqhgd', p.astype(v.dtype), v)

    o = lax.map(one, qb)
    return jnp.moveaxis(o, 0, 1).reshape(B, N, Hk * G * Dh)


def na_latent(q, k, v, ck, cv, rpb):
    B, N, H, Dh = q.shape
    rows = N // GRID_W
    wr = min(NA_WIN_R, rows)
    r = jnp.arange(rows)
    r0 = jnp.clip(r - wr // 2, 0, rows - wr)
    row_idx = r0[:, None] + jnp.arange(wr)[None, :]
    cidx = jnp.arange(GRID_W)
    c0 = jnp.clip(cidx - NA_WIN_C // 2, 0, GRID_W - NA_WIN_C)
    col_ok = (cidx[None, :] >= c0[:, None]) & (cidx[None, :] < c0[:, None] + NA_WIN_C)
    qg = q.reshape(B, rows, GRID_W, H, Dh)
    kg = k.reshape(B, rows, GRID_W, H, Dh)[:, row_idx]
    vg = v.reshape(B, rows, GRID_W, H, Dh)[:, row_idx]
    scale = Dh ** -0.5
    s_lat = jnp.einsum('brqhd,brjkhd->bhrqjk', qg, kg, preferred_element_type=jnp.float32) * scale
    dr = row_idx - r[:, None]
    dc = jnp.clip(cidx[None, :] - cidx[:, None], -(NA_WIN_C - 1), NA_WIN_C - 1)
    bias = rpb[:, (dr + NA_WIN_R - 1)[:, None, :, None], (dc + NA_WIN_C - 1)[None, :, None, :]]
    s_lat = jnp.where(col_ok[:, None, :], s_lat + bias[None].astype(jnp.float32), NEG_INF)
    s_lat = s_lat.reshape(B, H, rows, GRID_W, wr * GRID_W)
    s_ctx = jnp.einsum('brqhd,bkhd->bhrqk', qg, ck, preferred_element_type=jnp.float32) * scale
    p = jax.nn.softmax(jnp.concatenate([s_lat, s_ctx], axis=-1), axis=-1).astype(v.dtype)
    p_lat, p_ctx = p[..., :wr * GRID_W], p[..., wr * GRID_W:]
    o = jnp.einsum('bhrqk,brkhd->brqhd', p_lat, vg.reshape(B, rows, wr * GRID_W, H, Dh))
    o = o + jnp.einsum('bhrqk,bkhd->brqhd', p_ctx, cv)
    return o.reshape(B, N, H * Dh)


def conv_branch(glu, dw, ln_g, ln_b):
    a, g = jnp.split(glu, 2, axis=-1)
    u = depthwise_conv(a * jax.nn.sigmoid(g), dw, (CONV_W // 2, CONV_W // 2))
    return jax.nn.silu(layernorm(u, ln_g, ln_b))


def lru_coeffs(xc, wr, br, wi, bi, lam):
    B, L, W = xc.shape
    xb = xc.reshape(B, L, LRU_BLOCKS, W // LRU_BLOCKS)
    r = jax.nn.sigmoid((jnp.einsum('blni,nij->blnj', xb, wr).reshape(B, L, W) + br).astype(jnp.float32))
    i = jax.nn.sigmoid((jnp.einsum('blni,nij->blnj', xb, wi).reshape(B, L, W) + bi).astype(jnp.float32))
    log_a = LRU_C * r * jax.nn.log_sigmoid(lam.astype(jnp.float32))
    a = jnp.exp(log_a)
    b = jnp.sqrt(-jnp.expm1(2.0 * log_a)) * (i * xc.astype(jnp.float32))
    return a, b


def linear_scan(a, b, h0):
    def comb(e1, e2):
        a1, b1 = e1
        a2, b2 = e2
        return a1 * a2, a2 * b1 + b2

    A, Bc = lax.associative_scan(comb, (a, b), axis=1)
    return A * h0[:, None, :] + Bc


def lru_branch(x, lp, h0):
    xc = depthwise_conv(x, lp['lru_conv_w'], (LRU_CONV_W // 2, LRU_CONV_W - 1 - LRU_CONV_W // 2)) + lp['lru_conv_b']
    a_f, b_f = lru_coeffs(xc, lp['lru_wr'][0], lp['lru_br'][0], lp['lru_wi'][0], lp['lru_bi'][0], lp['lru_lambda'][0])
    h_f = linear_scan(a_f, b_f, h0[:, 0])
    a_b, b_b = lru_coeffs(xc, lp['lru_wr'][1], lp['lru_br'][1], lp['lru_wi'][1], lp['lru_bi'][1], lp['lru_lambda'][1])
    h_b = jnp.flip(linear_scan(jnp.flip(a_b, 1), jnp.flip(b_b, 1), h0[:, 1]), 1)
    y = (h_f + h_b).astype(x.dtype)
    final = jnp.stack([h_f[:, -1], h_b[:, 0]], axis=1)
    return y, final


def split_proj(p):
    B, N, _ = p.shape

    def heads(lo, n_heads):
        return p[..., lo:lo + n_heads * HEAD_DIM].reshape(B, N, n_heads, HEAD_DIM)

    na_q = heads(OFF_NA, NA_HEADS)
    na_k = heads(OFF_NA + NA_W, NA_HEADS)
    na_v = heads(OFF_NA + 2 * NA_W, NA_HEADS)
    glu = p[..., OFF_CONV:OFF_GQA]
    gq_q = heads(OFF_GQA, GQA_HEADS)
    gq_k = heads(OFF_GQA + GQA_QW, GQA_KV_HEADS)
    gq_v = heads(OFF_GQA + GQA_QW + GQA_KVW, GQA_KV_HEADS)
    lru_x = p[..., OFF_LRU:OFF_GATE]
    gate_logits = p[..., OFF_GATE:]
    return na_q, na_k, na_v, glu, gq_q, gq_k, gq_v, lru_x, gate_logits


def merge_branches(branches, gate_logits, w_branch, w_out):
    B, N, _ = gate_logits.shape
    stacked = jnp.stack(branches, axis=2)
    proj = jnp.einsum('bnkc,kcd->bnkd', stacked, w_branch)
    g = jax.nn.sigmoid(gate_logits).reshape(B, N, N_BRANCH, D_MODEL)
    return jnp.sum(g * proj, axis=2) @ w_out


def mixer_context(h, lp):
    B, N, _ = h.shape
    na_q, na_k, na_v, glu, gq_q, gq_k, gq_v, lru_x, gl = split_proj(h @ lp['w_in'])
    o_na = block_attention(na_q[:, :, :, None, :], na_k, na_v)
    o_conv = conv_branch(glu, lp['conv_dw'], lp['conv_ln_g'], lp['conv_ln_b'])
    gq_q = rmsnorm(gq_q, lp['q_norm'])
    gq_k = rmsnorm(gq_k, lp['k_norm'])
    o_gqa = block_attention(gq_q.reshape(B, N, GQA_KV_HEADS, GQA_GROUP, HEAD_DIM), gq_k, gq_v)
    o_lru, lru_state = lru_branch(lru_x, lp, jnp.zeros((B, 2, LRU_W), jnp.float32))
    y = merge_branches((o_na, o_conv, o_gqa, o_lru), gl, lp['w_branch'], lp['w_out'])
    return y, na_k, na_v, gq_k, gq_v, lru_state


def mixer_latent(h, lp, tabs, ck_na, cv_na, ck_gq, cv_gq, h0):
    B, N, _ = h.shape
    na_q, na_k, na_v, glu, gq_q, gq_k, gq_v, lru_x, gl = split_proj(h @ lp['w_in'])
    o_na = na_latent(na_q, na_k, na_v, ck_na.astype(h.dtype), cv_na.astype(h.dtype), lp['na_rpb'])
    o_conv = conv_branch(glu, lp['conv_dw'], lp['conv_ln_g'], lp['conv_ln_b'])
    gq_q = apply_rope_2d(rmsnorm(gq_q, lp['q_norm']), tabs)
    gq_k = apply_rope_2d(rmsnorm(gq_k, lp['k_norm']), tabs)
    keys = jnp.concatenate([gq_k, ck_gq.astype(gq_k.dtype)], axis=1)
    vals = jnp.concatenate([gq_v, cv_gq.astype(gq_v.dtype)], axis=1)
    o_gqa = block_attention(gq_q.reshape(B, N, GQA_KV_HEADS, GQA_GROUP, HEAD_DIM), keys, vals)
    o_lru, _ = lru_branch(lru_x, lp, h0.astype(jnp.float32))
    return merge_branches((o_na, o_conv, o_gqa, o_lru), gl, lp['w_branch'], lp['w_out'])


def setup_inputs(seed: int = 0) -> dict:
    key = jax.random.key(seed)
    ks = jax.random.split(key, 32)
    f32 = jnp.float32

    def nrm(k, shape, scale):
        return jax.random.normal(k, shape, f32) * scale

    bw = LRU_W // LRU_BLOCKS
    u = jax.random.uniform(ks[27], (DEPTH, 2, LRU_W), f32, 0.9, 0.999)
    a0 = u ** (1.0 / LRU_C)
    lam = jnp.log(a0) - jnp.log1p(-a0)
    return {
        'x_prompt': nrm(ks[0], (BATCH, SEQ, D_MODEL), 1.0),
        'x_sample': nrm(ks[1], (DEC_BATCH, DEC_SEQ, D_MODEL), 1.0),
        'c': nrm(ks[2], (DEC_BATCH, D_MODEL), 1.0),
        'cache_na_k': nrm(ks[3], (DEC_BATCH, DEPTH, PAST_LEN, NA_HEADS, HEAD_DIM), 1.0),
        'cache_na_v': nrm(ks[4], (DEC_BATCH, DEPTH, PAST_LEN, NA_HEADS, HEAD_DIM), 1.0),
        'cache_gqa_k': nrm(ks[5], (DEC_BATCH, DEPTH, PAST_LEN, GQA_KV_HEADS, HEAD_DIM), 1.0),
        'cache_gqa_v': nrm(ks[6], (DEC_BATCH, DEPTH, PAST_LEN, GQA_KV_HEADS, HEAD_DIM), 1.0),
        'state_lru': nrm(ks[7], (DEC_BATCH, DEPTH, 2, LRU_W), 0.5),
        'c_ctx': nrm(ks[8], (D_MODEL,), 1.0),
        'w_ada': nrm(ks[9], (DEPTH, D_MODEL, 9 * D_MODEL), 0.3 * D_MODEL ** -0.5),
        'b_ada': nrm(ks[10], (DEPTH, 9 * D_MODEL), 0.02),
        'norm_g': 1.0 + nrm(ks[11], (DEPTH, 6, D_MODEL), 0.02),
        'ffn_w1': nrm(ks[12], (DEPTH, 2, D_MODEL, 2 * D_FF), D_MODEL ** -0.5),
        'ffn_w2': nrm(ks[13], (DEPTH, 2, D_FF, D_MODEL), D_FF ** -0.5),
        'w_in': nrm(ks[14], (DEPTH, D_MODEL, IN_COLS), D_MODEL ** -0.5),
        'na_rpb': nrm(ks[15], (DEPTH, NA_HEADS, 2 * NA_WIN_R - 1, 2 * NA_WIN_C - 1), 0.1),
        'conv_dw': nrm(ks[16], (DEPTH, CONV_W, CONV_C), CONV_W ** -0.5),
        'conv_ln_g': 1.0 + nrm(ks[17], (DEPTH, CONV_C), 0.02),
        'conv_ln_b': nrm(ks[18], (DEPTH, CONV_C), 0.02),
        'gqa_q_norm': 1.0 + nrm(ks[19], (DEPTH, HEAD_DIM), 0.02),
        'gqa_k_norm': 1.0 + nrm(ks[20], (DEPTH, HEAD_DIM), 0.02),
        'lru_conv_w': nrm(ks[21], (DEPTH, LRU_CONV_W, LRU_W), LRU_CONV_W ** -0.5),
        'lru_conv_b': nrm(ks[22], (DEPTH, LRU_W), 0.02),
        'lru_wr': nrm(ks[23], (DEPTH, 2, LRU_BLOCKS, bw, bw), bw ** -0.5),
        'lru_br': nrm(ks[24], (DEPTH, 2, LRU_W), 0.02),
        'lru_wi': nrm(ks[25], (DEPTH, 2, LRU_BLOCKS, bw, bw), bw ** -0.5),
        'lru_bi': nrm(ks[26], (DEPTH, 2, LRU_W), 0.02),
        'lru_lambda': lam,
        'w_branch': nrm(ks[28], (DEPTH, N_BRANCH, BRANCH_W, D_MODEL), BRANCH_W ** -0.5),
        'w_out': nrm(ks[29], (DEPTH, D_MODEL, D_MODEL), D_MODEL ** -0.5),
    }


def reference(x_prompt, x_sample, c, cache_na_k, cache_na_v, cache_gqa_k, cache_gqa_v, state_lru, c_ctx,
              w_ada, b_ada, norm_g, ffn_w1, ffn_w2, w_in, na_rpb, conv_dw, conv_ln_g, conv_ln_b,
              gqa_q_norm, gqa_k_norm, lru_conv_w, lru_conv_b, lru_wr, lru_br, lru_wi, lru_bi, lru_lambda,
              w_branch, w_out):
    tabs = rope_2d(x_sample.shape[1])
    xp, xs = x_prompt, x_sample
    nk_l, nv_l, gk_l, gv_l, st_l = [], [], [], [], []
    for l in range(DEPTH):
        lp = {'w_in': w_in[l], 'na_rpb': na_rpb[l], 'conv_dw': conv_dw[l], 'conv_ln_g': conv_ln_g[l],
              'conv_ln_b': conv_ln_b[l], 'q_norm': gqa_q_norm[l], 'k_norm': gqa_k_norm[l],
              'lru_conv_w': lru_conv_w[l], 'lru_conv_b': lru_conv_b[l], 'lru_wr': lru_wr[l], 'lru_br': lru_br[l],
              'lru_wi': lru_wi[l], 'lru_bi': lru_bi[l], 'lru_lambda': lru_lambda[l],
              'w_branch': w_branch[l], 'w_out': w_out[l]}
        g = norm_g[l]
        mod_p = adaln(c_ctx[None, :], w_ada[l], b_ada[l])
        mod_s = adaln(c, w_ada[l], b_ada[l])
        xp = add_residual(xp, swiglu(modulate(xp, g[0], mod_p[:, 0]), ffn_w1[l, 0], ffn_w2[l, 0]), g[1], mod_p[:, 0], HALF)
        y, nk, nv, gk, gv, st = mixer_context(modulate(xp, g[2], mod_p[:, 1]), lp)
        xp = add_residual(xp, y, g[3], mod_p[:, 1], 1.0)
        xp = add_residual(xp, swiglu(modulate(xp, g[4], mod_p[:, 2]), ffn_w1[l, 1], ffn_w2[l, 1]), g[5], mod_p[:, 2], HALF)
        nk_l.append(nk)
        nv_l.append(nv)
        gk_l.append(gk)
        gv_l.append(gv)
        st_l.append(st)
        xs = add_residual(xs, swiglu(modulate(xs, g[0], mod_s[:, 0]), ffn_w1[l, 0], ffn_w2[l, 0]), g[1], mod_s[:, 0], HALF)
        y = mixer_latent(modulate(xs, g[2], mod_s[:, 1]), lp, tabs, cache_na_k[:, l], cache_na_v[:, l],
                         cache_gqa_k[:, l], cache_gqa_v[:, l], state_lru[:, l])
        xs = add_residual(xs, y, g[3], mod_s[:, 1], 1.0)
        xs = add_residual(xs, swiglu(modulate(xs, g[4], mod_s[:, 2]), ffn_w1[l, 1], ffn_w2[l, 1]), g[5], mod_s[:, 2], HALF)
    new_na_k = jnp.stack(nk_l, axis=1)
    new_na_v = jnp.stack(nv_l, axis=1)
    new_gqa_k = jnp.stack(gk_l, axis=1)
    new_gqa_v = jnp.stack(gv_l, axis=1)
    new_lru_state = jnp.stack(st_l, axis=1)
    return (xp, xs, new_na_k, new_na_v, new_gqa_k, new_gqa_v, new_lru_state)
```

```python
import numpy as np
from contextlib import ExitStack
import concourse.bass as bass
import concourse.mybir as mybir
from concourse.bass_utils import run_bass_kernel_spmd

F32 = mybir.dt.float32
BF16 = mybir.dt.bfloat16
U8 = mybir.dt.uint8
AF = mybir.ActivationFunctionType
ALU = mybir.AluOpType

DEPTH = 4
D = 1024
NT = 1536
DFF = 2816
EPS = 1e-6
NCOL = 320
C_G = 0
C_BADA = 48
C_DW = 120
C_LNG = 244
C_LNB = 248
C_QN = 252
C_KN = 253
C_LCW = 254
C_LCB = 270
C_BR = 274
C_BI = 282
C_LAM = 290
C_H0 = 298
SEGS = [(0, 256), (256, 256), (512, 1024)]


class Eng:
    def __init__(self, name):
        self.name = name
        self.ops = []
        self.count = 0
        self.seen = {}
        self.semkey = "E_" + name


class Sched:
    def __init__(self):
        self.E = {n: Eng(n) for n in ("pe", "act", "dve", "pool", "sp")}
        self.recs = {}
        self.keys = {"S": [], "P": []}
        self.ov = {}
        self.dsem = {}
        self.marks = []

    def mark(self, label):
        self.marks.append((label, {n: e.count for n, e in self.E.items()}))

    def _reg(self, key):
        if key in self.ov:
            return
        sp, lo, hi = key
        o = []
        for k2 in self.keys[sp]:
            if k2[1] < hi and lo < k2[2]:
                o.append(k2)
                self.ov[k2].append(key)
        self.ov[key] = o
        self.keys[sp].append(key)
        self.recs[key] = [None, {}]

    def _deps(self, eng, reads, writes, is_dma=False, skip_sem=None):
        toks = {}

        def add(t, raw):
            if t is None:
                return
            if t[0] == skip_sem:
                return
            if (not is_dma) and t[2] == eng.name:
                if eng.name == "pe":
                    return
            if toks.get(t[0], 0) < t[1]:
                toks[t[0]] = t[1]

        for r in reads:
            self._reg(r)
            for k in [r] + self.ov[r]:
                add(self.recs[k][0], True)
        for w in writes:
            self._reg(w)
            for k in [w] + self.ov[w]:
                rec = self.recs[k]
                add(rec[0], False)
                for t in rec[1].values():
                    add(t, False)
        return toks

    def _push(self, eng, toks, fn, semkey, inc):
        waits = []
        for k, v in toks.items():
            if eng.seen.get(k, 0) < v:
                eng.seen[k] = v
                waits.append((k, v))
        eng.ops.append((waits, fn, semkey, inc))

    def _record(self, tok, reads, writes):
        for r in reads:
            self.recs[r][1][tok[0]] = tok
        for w in writes:
            rec = self.recs[w]
            rec[0] = tok
            rec[1] = {}

    def op(self, en, fn, reads=(), writes=()):
        eng = self.E[en]
        reads = list(reads)
        writes = list(writes)
        toks = self._deps(eng, reads, writes)
        eng.count += 1
        tok = (eng.semkey, eng.count, eng.name)
        self._push(eng, toks, fn, eng.semkey, 1)
        self._record(tok, reads, writes)

    def dma(self, en, fn, region, write, group=False, reads=(), writes=()):
        eng = self.E[en]
        reads = list(reads) + ([] if write else [region])
        writes = list(writes) + ([region] if write else [])
        dk = (region, en == "pool")
        if dk not in self.dsem:
            self.dsem[dk] = ["D%d" % len(self.dsem), 0]
        ds = self.dsem[dk]
        toks = self._deps(eng, reads, writes, is_dma=True, skip_sem=ds[0] if group else None)
        ds[1] += 16
        tok = (ds[0], ds[1], None)
        self._push(eng, toks, fn, ds[0], 16)
        self._record(tok, reads, writes)

    def finish(self):
        eng = self.E["sp"]
        toks = {ds[0]: ds[1] for ds in self.dsem.values()}
        for e in self.E.values():
            if e.count:
                toks[e.semkey] = e.count
        self._push(eng, toks, None, None, 0)

    def emit(self, nc, es):
        keys = [e.semkey for e in self.E.values()] + [ds[0] for ds in self.dsem.values()]
        sems = {k: es.enter_context(nc.semaphore(k)) for k in keys}
        with nc.Block() as block:
            regs = {"pe": block.tensor, "act": block.scalar, "dve": block.vector,
                    "pool": block.gpsimd, "sp": block.sync}
            for name, reg in regs.items():
                eng = self.E[name]

                def f(e, eng=eng):
                    for waits, fn, sk, inc in eng.ops:
                        for k, v in waits:
                            e.wait_ge(sems[k], v)
                        if fn is not None:
                            fn(e).then_inc(sems[sk], inc)
                reg(f)


def _esz(dt):
    return 4 if dt == F32 else (2 if dt == BF16 else 1)


class T:
    def __init__(self, arena_ap, off, shape, dtype):
        n = int(np.prod(shape))
        self.off = off
        self.esz = _esz(dtype)
        self.nb = n * self.esz
        self.shape = tuple(shape)
        ap = arena_ap[:, off:off + self.nb].bitcast(dtype)
        if len(shape) == 2:
            ap = ap.rearrange("p (a b) -> p a b", a=shape[0])
        elif len(shape) == 3:
            ap = ap.rearrange("p (a b c) -> p a b c", a=shape[0], b=shape[1])
        elif len(shape) == 4:
            ap = ap.rearrange("p (a b c d) -> p a b c d", a=shape[0], b=shape[1], c=shape[2])
        self.ap = ap

    def k(self, lo=None, hi=None):
        if lo is None:
            return ("S", self.off, self.off + self.nb)
        return ("S", self.off + lo * self.esz, self.off + hi * self.esz)


class Arena:
    def __init__(self, ap, size):
        self.ap = ap
        self.size = size
        self.top = 0

    def alloc(self, shape, dtype):
        off = (self.top + 63) // 64 * 64
        t = T(self.ap, off, shape, dtype)
        self.top = off + t.nb
        assert self.top <= self.size, ("arena overflow", self.top, self.size)
        return t


def R(ap, *keys):
    return (ap, list(keys))


def build_program(depth=DEPTH, do=("ffn0", "mixer", "ffn2"), adaln_on=True, mx=("pq", "pk", "pv", "pvo", "pc", "nactx", "nalat", "conv", "gqa", "lru", "merge")):
    nc = bass.Bass("TRN2", target_bir_lowering=False)
    S = Sched()
    es = ExitStack()

    def din(name, shape):
        return nc.dram_tensor(name, list(shape), F32, kind="ExternalInput").ap()

    def dout(name, shape):
        return nc.dram_tensor(name, list(shape), F32, kind="ExternalOutput").ap()

    x_in = din("x_in", [NT, D])
    cc = din("cc", [2, D])
    ck_na = din("ck_na", [DEPTH, 512, 512])
    cv_na = din("cv_na", [DEPTH, 512, 512])
    ck_gq = din("ck_gq", [DEPTH, 512, 128])
    cv_gq = din("cv_gq", [DEPTH, 512, 128])
    colp_d = din("colp", [DEPTH, 128, NCOL])
    w_ada = din("w_ada", [DEPTH, D, 9 * D])
    ffn_w1 = din("ffn_w1", [DEPTH, 2, D, 2 * DFF])
    ffn_w2 = din("ffn_w2", [DEPTH, 2, DFF, D])
    w_in = din("w_in_r", [DEPTH, D, 7936])
    rpb = din("rpb_pad", [DEPTH, 8, 15, 127])
    lru_bd = din("lru_bd", [DEPTH, 128, 16, 128])
    w_br = din("w_branch_r", [DEPTH, 4, 512, D])
    w_out = din("w_out", [DEPTH, D, D])
    ropeC_d = din("ropeC", [128, 1024])
    ropeS_d = din("ropeS", [128, 1024])
    permM_d = din("permM", [128, 128])
    cmask_d = din("cmask", [64, 64])
    ident_d = din("ident", [128, 128])

    y_out = dout("y_out", [NT, D])
    o_nak = dout("o_nak", [DEPTH * 512, 512]).rearrange("(l t) f -> l t f", l=DEPTH)
    o_nav = dout("o_nav", [DEPTH * 512, 512]).rearrange("(l t) f -> l t f", l=DEPTH)
    o_gqk = dout("o_gqk", [DEPTH * 512, 128]).rearrange("(l t) f -> l t f", l=DEPTH)
    o_gqv = dout("o_gqv", [DEPTH * 512, 128]).rearrange("(l t) f -> l t f", l=DEPTH)
    o_lru = dout("o_lru", [DEPTH * 128, 16]).rearrange("(l t) f -> l t f", l=DEPTH)

    ARENA = 207 * 1024
    arena_t = es.enter_context(nc.sbuf_tensor("arena", [128, ARENA], U8))
    A = Arena(arena_t, ARENA)
    ps_t = es.enter_context(nc.psum_tensor("ps", [128, 8, 512], F32))

    def PS(b, rows=slice(0, 128), lo=0, hi=512):
        return R(ps_t[rows, b, lo:hi], ("P", b * 2048 + lo * 4, b * 2048 + hi * 4))

    def mm(out, lhsT, rhs, start, stop, **kw):
        S.op("pe", lambda e: e.matmul(out[0], lhsT=lhsT[0], rhs=rhs[0], start=start, stop=stop, **kw),
             reads=lhsT[1] + rhs[1], writes=out[1])

    def tr(out, in_, ident):
        S.op("pe", lambda e: e.transpose(out[0], in_[0], ident[0]), reads=in_[1] + ident[1], writes=out[1])

    def act(out, in_, func, bias=None, scale=1.0):
        rd = list(in_[1])
        kw = {}
        if bias is not None:
            kw["bias"] = bias[0]
            rd += bias[1]
        if isinstance(scale, tuple):
            rd += scale[1]
            sc = scale[0]
        else:
            sc = scale
        S.op("act", lambda e: e.activation(out=out[0], in_=in_[0], func=func, scale=sc, **kw),
             reads=rd, writes=out[1])

    def tt(en, out, a, b, op):
        S.op(en, lambda e: e.tensor_tensor(out[0], a[0], b[0], op), reads=a[1] + b[1], writes=out[1])

    def ts(en, out, a, s1, s2, op0, op1=None):
        rd = list(a[1])
        v1 = s1
        v2 = s2
        if isinstance(s1, tuple):
            rd += s1[1]
            v1 = s1[0]
        if isinstance(s2, tuple):
            rd += s2[1]
            v2 = s2[0]
        if op1 is None:
            S.op(en, lambda e: e.tensor_scalar(out[0], a[0], v1, None, op0), reads=rd, writes=out[1])
        else:
            S.op(en, lambda e: e.tensor_scalar(out[0], a[0], v1, v2, op0, op1), reads=rd, writes=out[1])

    def stt(out, in0, sc, in1, op0, op1):
        rd = in0[1] + in1[1]
        v = sc
        if isinstance(sc, tuple):
            rd = rd + sc[1]
            v = sc[0]
        S.op("dve", lambda e: e.scalar_tensor_tensor(out[0], in0[0], v, in1[0], op0, op1), reads=rd, writes=out[1])

    def cp(en, out, in_):
        if en == "act":
            S.op("act", lambda e: e.copy(out[0], in_[0]), reads=in_[1], writes=out[1])
        else:
            S.op(en, lambda e: e.tensor_copy(out[0], in_[0]), reads=in_[1], writes=out[1])

    def recip(out, in_):
        S.op("dve", lambda e: e.reciprocal(out[0], in_[0]), reads=in_[1], writes=out[1])

    def scan(out, d0, d1, init):
        rd = d0[1] + d1[1]
        v = init
        if isinstance(init, tuple):
            rd = rd + init[1]
            v = init[0]
        S.op("dve", lambda e: e.tensor_tensor_scan(out[0], d0[0], d1[0], v, ALU.mult, ALU.add), reads=rd, writes=out[1])

    def memset(en, out, val):
        S.op(en, lambda e: e.memset(out[0], val), writes=out[1])

    def dma_in(en, dst, src_ap, group=False):
        S.dma(en, lambda e: e.dma_start(out=dst[0], in_=src_ap), dst[1][0], True, group=group)

    def dma_out(en, dst_ap, src, **kw):
        S.dma(en, lambda e: e.dma_start(out=dst_ap, in_=src[0], **kw), src[1][0], False)

    ident = A.alloc([128], F32)
    identb = A.alloc([128], BF16)
    ones = A.alloc([128], BF16)
    bones = A.alloc([128], BF16)
    ones512 = A.alloc([128], BF16)
    permM = A.alloc([128], BF16)
    cmask = A.alloc([64], F32)
    epsc = A.alloc([1], F32)
    colp = A.alloc([DEPTH, NCOL], F32)
    scT = A.alloc([8, 2], BF16)
    rowb = A.alloc([512], F32)
    ones_f = A.alloc([1], F32)
    modT = [A.alloc([72, 2], F32) for _ in range(2)]
    Atab = [A.alloc([2, 3, 8], F32) for _ in range(2)]
    Btab = [A.alloc([2, 3, 8], F32) for _ in range(2)]

    Ri = R(ident.ap, ident.k())
    Rib = R(identb.ap, identb.k())
    Rones = R(ones.ap, ones.k())
    Rbones = R(bones.ap, bones.k())
    Rones512 = R(ones512.ap, ones512.k())
    Reps = R(epsc.ap, epsc.k())

    def col(l, j):
        return R(colp.ap[:, l, j:j + 1], colp.k())

    dma_in("sp", Ri, ident_d)
    dma_in("sp", R(cmask.ap[0:64, :], cmask.k()), cmask_d)
    for l in range(DEPTH):
        dma_in("sp", R(colp.ap[:, l, :], colp.k()), colp_d[l], group=l > 0)
    dma_in("pool", R(permM.ap, permM.k()), permM_d)
    cp("dve", Rib, Ri)
    memset("dve", Rones, 1.0)
    memset("dve", Rones512, 1.0 / 512.0)
    memset("dve", Rbones, 0.0)
    memset("dve", R(bones.ap[0:64, 0:64], bones.k()), 1.0)
    memset("dve", R(bones.ap[64:128, 64:128], bones.k()), 1.0)
    memset("dve", Reps, EPS)

    xT = A.alloc([8, NT], F32)
    hT = A.alloc([8, NT], BF16)
    NS = 3
    wslots = [A.alloc([4096], BF16) for _ in range(NS)]
    wctr = [0]
    phase_base = A.top

    def X(c, tb):
        return R(xT.ap[:, c, tb * 512:(tb + 1) * 512], xT.k(c * NT + tb * 512, c * NT + (tb + 1) * 512))

    def H(c, tb):
        return R(hT.ap[:, c, tb * 512:(tb + 1) * 512], hT.k(c * NT + tb * 512, c * NT + (tb + 1) * 512))

    def Hc(c, lo, hi):
        tb = lo // 512
        assert (hi - 1) // 512 == tb
        return R(hT.ap[:, c, lo:hi], hT.k(c * NT + tb * 512, c * NT + (tb + 1) * 512))

    def wslot():
        s = wslots[wctr[0] % NS]
        wctr[0] += 1
        return s

    def wload(src_ap, k, n):
        s = wslot()
        dst = s.ap[:, 0:k * n].rearrange("p (k n) -> p k n", k=k)
        dma_in("pool", R(dst, s.k()), src_ap)
        return R(dst, s.k())

    def wview(w2d):
        return w2d.rearrange("(k p) n -> p k n", p=128)

    sqb = [A.alloc([512], BF16) for _ in range(3)]
    rs3 = [A.alloc([512], F32) for _ in range(3)]
    tmpf = [A.alloc([512], F32) for _ in range(3)]
    sqi = [0]
    tfi = [0]
    phase_base = A.top

    def nsq():
        sqi[0] += 1
        t = sqb[sqi[0] % 3]
        return R(t.ap, t.k())

    def ntf():
        tfi[0] += 1
        t = tmpf[tfi[0] % 3]
        return R(t.ap, t.k())

    rsi = [0]

    def nrs():
        rsi[0] += 1
        t = rs3[rsi[0] % 3]
        return R(t.ap, t.k())

    def rstd_from(ps_in, scale):
        r = nrs()
        act(r, ps_in, AF.Ln, bias=Reps, scale=scale)
        act(r, r, AF.Exp, scale=-0.5)
        return r

    A.top = phase_base
    xst = [A.alloc([D], F32) for _ in range(2)]
    for tb in range(3):
        for q in range(4):
            b = tb * 4 + q
            st = xst[b % 2]
            Rst = R(st.ap, st.k())
            dma_in("sp", Rst, x_in[b * 128:(b + 1) * 128, :])
            for c in range(8):
                tr(PS(c, slice(0, 128), q * 128, (q + 1) * 128), R(st.ap[:, c * 128:(c + 1) * 128], st.k()), Ri)
        for c in range(8):
            cp("act" if c % 2 else "dve", X(c, tb), PS(c))

    ccs = xst[0]
    Rcc = R(ccs.ap[0:2, :], ccs.k())
    dma_in("sp", Rcc, cc)
    act(Rcc, Rcc, AF.Silu)
    for k in range(8):
        tr(PS(0, slice(0, 128), k * 2, k * 2 + 2), R(ccs.ap[0:2, k * 128:(k + 1) * 128], ccs.k()), R(ident.ap[0:2, 0:2], ident.k()))
    cp("dve", R(scT.ap, scT.k()), R(ps_t[:, 0, 0:16].rearrange("p (k w) -> p k w", k=8), PS(0, slice(0, 128), 0, 16)[1][0]))

    def adaln(l, par):
        mt = modT[par]
        A_ = Atab[par]
        B_ = Btab[par]
        wv = wview(w_ada[l])
        for g in range(18):
            sl = wload(wv[:, :, g * 512:(g + 1) * 512], 8, 512)
            for k in range(8):
                mm(PS(7, slice(0, 2)), R(scT.ap[:, k, :], scT.k()), R(sl[0][:, k, :], *sl[1]), k == 0, k == 7)
            cp("act", R(rowb.ap[0:2, 0:512], rowb.k()), PS(7, slice(0, 2)))
            for q in range(4):
                tr(PS(6, slice(0, 128), q * 2, q * 2 + 2), R(rowb.ap[0:2, q * 128:(q + 1) * 128], rowb.k()),
                   R(ident.ap[0:2, 0:2], ident.k()))
            cp("dve", R(mt.ap[:, g * 4:(g + 1) * 4, :], mt.k()),
               R(ps_t[:, 6, 0:8].rearrange("p (q w) -> p q w", q=4), PS(6, slice(0, 128), 0, 8)[1][0]))
            yield
        Rm = mt.k()
        for who in range(2):
            tt("dve", R(mt.ap[:, :, who], Rm), R(mt.ap[:, :, who], Rm), R(colp.ap[:, l, C_BADA:C_BADA + 72], colp.k()), ALU.add)
        for who in range(2):
            for sub in range(3):
                stt(R(A_.ap[:, who, sub, :], A_.k()), R(mt.ap[:, sub * 24 + 8:sub * 24 + 16, who], Rm), 1.0,
                    R(colp.ap[:, l, C_G + sub * 16:C_G + sub * 16 + 8], colp.k()), ALU.add, ALU.mult)
                stt(R(B_.ap[:, who, sub, :], B_.k()), R(mt.ap[:, sub * 24 + 16:sub * 24 + 24, who], Rm),
                    0.5 if sub != 1 else 1.0,
                    R(colp.ap[:, l, C_G + sub * 16 + 8:C_G + sub * 16 + 16], colp.k()), ALU.mult, ALU.mult)
        yield

    def run_bg(gen, n=1):
        if gen is None:
            return
        for _ in range(n):
            try:
                next(gen)
            except StopIteration:
                return

    def prenorm(par, sub):
        banks = [6, 7, 6]
        rsl = []
        for tb in range(3):
            for c in range(8):
                sq = nsq()
                act(sq, X(c, tb), AF.Square)
                mm(PS(banks[tb]), Rones, sq, c == 0, c == 7)
            rsl.append(rstd_from(PS(banks[tb]), 1.0 / D))
        for tb in range(3):
            who = 0 if tb == 0 else 1
            for c in range(8):
                t = ntf()
                stt(t, X(c, tb), R(Atab[par].ap[:, who, sub, c:c + 1], Atab[par].k()), rsl[tb], ALU.mult, ALU.mult)
                act(H(c, tb), t, AF.Identity,
                    bias=R(modT[par].ap[:, sub * 24 + c, who:who + 1], modT[par].k()))

    def post_update(par, sub, ybuf, sb):
        rsl = [rstd_from(PS(sb[tb]), 1.0 / D) for tb in range(3)]
        for tb in range(3):
            who = 0 if tb == 0 else 1
            for c in range(8):
                t = ntf()
                Ry = R(ybuf.ap[:, c, tb * 512:(tb + 1) * 512], ybuf.k(c * NT + tb * 512, c * NT + (tb + 1) * 512))
                tt("pool" if c % 2 else "dve", t, Ry, rsl[tb], ALU.mult)
                stt(X(c, tb), t, R(Btab[par].ap[:, who, sub, c:c + 1], Btab[par].k()), X(c, tb), ALU.mult, ALU.add)

    def outproj(get_w, nk, rhs_fn, ybuf, bg=None):
        banks = [0, 1, 2, 3, 4]
        bi = 0
        pend = None
        for c in range(8):
            wsl = get_w(c)
            for tb in range(3):
                b = banks[bi % 5]
                bi += 1
                for k in range(nk):
                    mm(PS(b), wsl(k), rhs_fn(k, tb), k == 0, k == nk - 1)
                if pend is not None:
                    mm(PS(5 + pend[0]), Rones, pend[1], pend[2] == 0, pend[2] == 7)
                Ry = R(ybuf.ap[:, c, tb * 512:(tb + 1) * 512], ybuf.k(c * NT + tb * 512, c * NT + (tb + 1) * 512))
                cp("act", Ry, PS(b))
                sq = nsq()
                act(sq, PS(b), AF.Square)
                pend = (tb, sq, c)
        mm(PS(5 + pend[0]), Rones, pend[1], pend[2] == 0, pend[2] == 7)

    def ffn(l, par, sub, bg=None):
        A.top = phase_base
        actb = A.alloc([22, NT], BF16)
        silb = [A.alloc([512], BF16) for _ in range(2)]
        prenorm(par, sub)
        w1v = wview(ffn_w1[l, sub // 2])
        w2v = wview(ffn_w2[l, sub // 2])
        pi = 0
        for j in range(11):
            s = wslot()
            dsta = s.ap[:, 0:2048].rearrange("p (k n) -> p k n", k=8)
            dstu = s.ap[:, 2048:4096].rearrange("p (k n) -> p k n", k=8)
            dma_in("pool", R(dsta, s.k()), w1v[:, :, j * 256:(j + 1) * 256])
            dma_in("pool", R(dstu, s.k()), w1v[:, :, DFF + j * 256:DFF + (j + 1) * 256], group=True)
            for half in range(2):
                m = 2 * j + half
                for tb in range(3):
                    ba = (pi % 3) * 2
                    pi += 1
                    for k in range(8):
                        mm(PS(ba), R(dsta[:, k, half * 128:(half + 1) * 128], s.k()), H(k, tb), k == 0, k == 7)
                    for k in range(8):
                        mm(PS(ba + 1), R(dstu[:, k, half * 128:(half + 1) * 128], s.k()), H(k, tb), k == 0, k == 7)
                    sb_ = silb[pi % 2]
                    Rs = R(sb_.ap, sb_.k())
                    act(Rs, PS(ba), AF.Silu)
                    Ra = R(actb.ap[:, m, tb * 512:(tb + 1) * 512], actb.k(m * NT + tb * 512, m * NT + (tb + 1) * 512))
                    tt("dve", Ra, PS(ba + 1), Rs, ALU.mult)
            run_bg(bg, 1)
        ybuf = hT

        def get_w(c):
            sl = wload(w2v[:, :, c * 128:(c + 1) * 128], 22, 128)
            return lambda k: R(sl[0][:, k, :], *sl[1])

        def rhs_fn(k, tb):
            return R(actb.ap[:, k, tb * 512:(tb + 1) * 512], actb.k(k * NT + tb * 512, k * NT + (tb + 1) * 512))

        outproj(get_w, 22, rhs_fn, ybuf, bg)
        post_update(par, sub, ybuf, [5, 6, 7])

    def mixer(l, par, bg=None):
        sub = 1
        A.top = phase_base
        Obr = []
        otop = []
        for _ in range(4):
            Obr.append(A.alloc([4, NT], BF16))
            otop.append(A.top)
        mix_base = A.top
        A.top = otop[0]
        prenorm(par, sub)
        wv = wview(w_in[l])
        wr = [0]

        def O(br, c, lo, hi, rows=slice(0, 128)):
            return R(Obr[br].ap[rows, c, lo:hi], Obr[br].k(c * NT, (c + 1) * NT))

        def proj_fm(col0, ncols, handler, banks, tbs=(0, 1, 2)):
            sl = wload(wv[:, :, col0:col0 + ncols], 8, ncols)
            for ci in range(ncols // 128):
                for tb in tbs:
                    b = banks[wr[0] % len(banks)]
                    wr[0] += 1
                    for k in range(8):
                        mm(PS(b), R(sl[0][:, k, ci * 128:(ci + 1) * 128], *sl[1]), H(k, tb), k == 0, k == 7)
                    handler(ci, tb, b)

        S.mark("L%d   na_proj" % l)
        qT = A.alloc([4, NT], BF16)
        kT = A.alloc([4, NT], BF16)
        ckT = A.alloc([4, 512], BF16)
        Vb = A.alloc([12, 512], BF16)
        cVb = A.alloc([4, 512], BF16)
        kst = [A.alloc([512], F32) for _ in range(2)]
        ost = [A.alloc([512], F32) for _ in range(2)]
        Ptl = [A.alloc([512], BF16) for _ in range(6)]
        rden = [A.alloc([512], F32) for _ in range(2)]
        Bq = A.alloc([17, 64], F32)
        Etab = A.alloc([24, 64], BF16)
        na_top = A.top
        cnt = {"k": 0, "o": 0, "p": 0, "r": 0, "s": 0}

        def rot(lst, key):
            cnt[key] += 1
            t = lst[cnt[key] % len(lst)]
            return t

        def QT(t, c, lo, hi, half=None):
            rows = slice(0, 128) if half is None else slice(half * 64, half * 64 + 64)
            return R(t.ap[rows, c, lo:hi], t.k(c * t.shape[1], (c + 1) * t.shape[1]))

        def h_q(ci, tb, b):
            cp("act" if (ci + tb) % 2 else "dve", QT(qT, ci, tb * 512, tb * 512 + 512), PS(b))

        if "pq" in mx:
            proj_fm(0, 512, h_q, [0, 1, 2, 3])

        def h_k(ci, tb, b):
            if tb != 0:
                cp("dve", QT(kT, ci, tb * 512, tb * 512 + 512), PS(b))
            if tb == 0:
                st = rot(kst, "k")
                Rst = R(st.ap, st.k())
                cp("act", Rst, PS(b))
                cp("dve", QT(kT, ci, tb * 512, tb * 512 + 512), Rst)
                for q in range(4):
                    tr(PS(4 + (cnt["k"] % 2), slice(0, 128), q * 128, q * 128 + 128),
                       R(st.ap[:, q * 128:(q + 1) * 128], st.k()), Ri)
                os_ = rot(ost, "o")
                Ros = R(os_.ap, os_.k())
                cp("act", Ros, PS(4 + (cnt["k"] % 2)))
                dma_out("sp", o_nak[l].rearrange("(q p) f -> p q f", p=128)[:, :, ci * 128:(ci + 1) * 128],
                        R(os_.ap.rearrange("p (q f) -> p q f", q=4), os_.k()))

        if "pk" in mx:
            proj_fm(512, 512, h_k, [0, 1, 2, 3])

        if "pv" in mx:
            slv = wload(wv[:, :, 1024:1536], 8, 512)
            for t128 in range(12):
                b = [0, 1, 2, 3][t128 % 4]
                for k in range(8):
                    mm(PS(b), Hc(k, t128 * 128, (t128 + 1) * 128), R(slv[0][:, k, :], *slv[1]), k == 0, k == 7)
                cp("act", R(Vb.ap[:, t128, :], Vb.k(t128 * 512, (t128 + 1) * 512)), PS(b))
                if t128 < 4 and "pvo" in mx:
                    os_ = rot(ost, "o")
                    Ros = R(os_.ap, os_.k())
                    cp("act", Ros, PS(b))
                    dma_out("sp", o_nav[l, t128 * 128:(t128 + 1) * 128, :], Ros)

        if "pc" in mx:
            for q in range(4):
                st = rot(kst, "k")
                Rst = R(st.ap, st.k())
                dma_in("sp", Rst, ck_na[l, q * 128:(q + 1) * 128, :])
                for c in range(4):
                    tr(PS(4 + (q % 2), slice(0, 128), c * 128, c * 128 + 128), R(st.ap[:, c * 128:(c + 1) * 128], st.k()), Ri)
                cp("dve", R(ckT.ap[:, :, q * 128:(q + 1) * 128], ckT.k()),
                   R(ps_t[:, 4 + (q % 2), :].rearrange("p (c t) -> p c t", c=4), PS(4 + (q % 2))[1][0]))
            dma_in("pool", R(cVb.ap, cVb.k()), cv_na[l].rearrange("(q p) f -> p q f", p=128))

        S.mark("L%d   na_ctx" % l)
        def attend(q_fn, keytiles, ncols, out_fn, half, bset):
            spool = bset["s"]
            ob, db = bset["o"], bset["d"]
            n = len(keytiles)
            LA = 3
            Pl = [None] * n
            for i in range(n + LA):
                if i < n:
                    kl, vl, (lo, hi), tab = keytiles[i]
                    sbk = spool[cnt["s"] % len(spool)]
                    cnt["s"] += 1
                    mm(PS(sbk, slice(0, 128), 0, hi - lo), kl, q_fn(lo, hi), True, True)
                    pt = rot(Ptl, "p")
                    Rp = R(pt.ap[:, 0:hi - lo], pt.k())
                    act(Rp, PS(sbk, slice(0, 128), 0, hi - lo), AF.Exp, scale=0.125)
                    if tab is not None:
                        Rp3 = R(pt.ap[:, 0:hi - lo].rearrange("p (e q) -> p e q", q=64), pt.k())
                        tt("dve", Rp3, Rp3, tab, ALU.mult)
                    Pl[i] = Rp
                j = i - LA
                if j >= 0:
                    kl, vl, (lo, hi), tab = keytiles[j]
                    mm(PS(ob, slice(0, 128), lo, hi), vl, Pl[j], j == 0, j == n - 1, skip_group_check=True)
                    mm(PS(db, slice(0, 128), lo, hi), Rones, Pl[j], j == 0, j == n - 1, skip_group_check=True)
            rd = rot(rden, "r")
            rows = slice(half * 64, half * 64 + 64)
            Rr = R(rd.ap[rows, 0:ncols], rd.k())
            act(Rr, PS(db, rows, 0, ncols), AF.Ln)
            act(Rr, Rr, AF.Exp, scale=-1.0)
            tt("dve", out_fn(rows), PS(ob, rows, 0, ncols), Rr, ALU.mult)

        bsets = [{"s": [0, 1, 2, 3], "o": 4, "d": 5}, {"s": [0, 1, 2, 3], "o": 6, "d": 7}]
        hcount = [0]

        def ctx_attention(qt, kt, vfn, br):
            for s in range(2):
                for h in range(8):
                    c, half = (h // 2, h % 2) if br == 0 else (h % 4, h // 4)
                    rows = slice(half * 64, half * 64 + 64)
                    tiles = []
                    for kb in range(2):
                        t0 = s * 256 + kb * 128
                        kl = R(kt[0][rows, kt[1](c), t0:t0 + 128], kt[2])
                        tiles.append((kl, vfn(s * 2 + kb, c), (0, 256), None))
                    bs = bsets[hcount[0] % 2]
                    hcount[0] += 1
                    attend(lambda lo, hi, c=c, rows=rows: R(qt.ap[rows, c, s * 256 + lo:s * 256 + hi], qt.k(c * NT, (c + 1) * NT)),
                           tiles, 256, lambda rws, c=c: O(br, c, s * 256, s * 256 + 256, rws), half, bs)

        if "nactx" in mx:
            ctx_attention(qT, (kT.ap, lambda c: c, kT.k()),
                          lambda t128, c: R(Vb.ap[:, t128, c * 128:(c + 1) * 128], Vb.k(t128 * 512, (t128 + 1) * 512)), 0)

        S.mark("L%d   na_lat" % l)
        if "nalat" in mx:
            def r0(r):
                return min(max(r - 4, 0), 8)

            def inwin(r, kr):
                return r0(r) <= kr <= r0(r) + 7

            Ek = Etab.k()
            RBq15 = R(Bq.ap[0:64, 1:16, :], Bq.k())

            def et_dma(h):
                if h == 0 and l == 0:
                    memset("dve", R(Bq.ap[0:64, :, :], Bq.k()), 0.0)
                src = bass.AP(rpb.tensor, rpb[l, h, 0, 0].offset, [[1, 64], [127, 15], [1, 64]])
                dma_in("sp", RBq15, src)

            def et_exp(h):
                act(RBq15, RBq15, AF.Exp)
                tt("dve", RBq15, RBq15, R(cmask.ap[0:64, :].unsqueeze(1).broadcast_to([64, 15, 64]), cmask.k()), ALU.mult)

            def et_tr(h):
                eb = (h % 2) * 2
                dl = list(range(7, -9, -1))
                for i, d in enumerate(dl):
                    tr(PS(eb + i // 8, slice(0, 128), (i % 8) * 64, (i % 8) * 64 + 64),
                       R(Bq.ap[0:64, d + 8:d + 10, :], Bq.k()), R(ident.ap[0:64, 0:64], ident.k()))
                cp("act", R(Etab.ap[:, 0:8, :], Ek), R(ps_t[:, eb, :].rearrange("p (e q) -> p e q", e=8)[:, :, ::-1], PS(eb)[1][0]))
                cp("act", R(Etab.ap[:, 8:12, :], Ek), R(ps_t[:, eb + 1, 0:256].rearrange("p (e q) -> p e q", e=4)[:, :, ::-1], PS(eb + 1)[1][0]))
                cp("act", R(Etab.ap[:, 12, :], Ek), R(ps_t[:, eb + 1, 256:320][:, ::-1], PS(eb + 1)[1][0]))
                memset("dve", R(Etab.ap[0:64, 12, :], Ek), 0.0)
                cp("act", R(Etab.ap[:, 13, :], Ek), R(ps_t[:, eb, 256:320][:, ::-1], PS(eb)[1][0]))
                memset("dve", R(Etab.ap[64:128, 13, :], Ek), 0.0)
                cp("act", R(Etab.ap[:, 14:17, :], Ek), R(ps_t[:, eb, 320:512].rearrange("p (e q) -> p e q", e=3)[:, :, ::-1], PS(eb)[1][0]))
                cp("act", R(Etab.ap[:, 17:24, :], Ek), R(ps_t[:, eb + 1, 0:448].rearrange("p (e q) -> p e q", e=7)[:, :, ::-1], PS(eb + 1)[1][0]))

            def tab_for(j, ra, rb):
                if j <= 3:
                    i0 = 7 - (2 * j - ra)
                    i1 = 7 - (2 * j - rb)
                    return R(Etab.ap[:, i0:i1 + 1, :], Ek)
                i0 = 13 + (3 - (2 * j - ra))
                i1 = 13 + (3 - (2 * j - rb))
                return R(Etab.ap[:, i0:i1 + 1, :], Ek)

            def na_piece(h, piece):
                c, half = h // 2, h % 2
                rows = slice(half * 64, half * 64 + 64)
                pr0, pr1 = piece * 8, piece * 8 + 7
                tiles = []
                for kb in range(4):
                    kl = R(ckT.ap[rows, c, kb * 128:(kb + 1) * 128], ckT.k())
                    vl = R(cVb.ap[:, kb, c * 128:(c + 1) * 128], cVb.k())
                    tiles.append((kl, vl, (0, 512), None))
                for j in range(8):
                    rr = [r for r in range(16) if inwin(r, 2 * j) or inwin(r, 2 * j + 1)]
                    ra, rb = max(rr[0], pr0), min(rr[-1], pr1)
                    if ra > rb:
                        continue
                    t0 = 512 + j * 128
                    kl = R(kT.ap[rows, c, t0:t0 + 128], kT.k(c * NT, (c + 1) * NT))
                    vl = R(Vb.ap[:, 4 + j, c * 128:(c + 1) * 128], Vb.k((4 + j) * 512, (5 + j) * 512))
                    tiles.append((kl, vl, ((ra - pr0) * 64, (rb - pr0 + 1) * 64), tab_for(j, ra, rb)))
                q0 = 512 + piece * 512
                bs = bsets[hcount[0] % 2]
                hcount[0] += 1
                attend(lambda lo, hi, c=c, rows=rows, q0=q0: R(qT.ap[rows, c, q0 + lo:q0 + hi], qT.k(c * NT, (c + 1) * NT)),
                       tiles, 512, lambda rws, c=c, q0=q0: O(0, c, q0, q0 + 512, rws), half, bs)

            et_dma(0)
            et_exp(0)
            et_tr(0)
            for h in range(8):
                if h < 7:
                    et_dma(h + 1)
                na_piece(h, 0)
                if h < 7:
                    et_exp(h + 1)
                na_piece(h, 1)
                if h < 7:
                    et_tr(h + 1)

        S.mark("L%d   conv" % l)
        if "conv" in mx:
            A.top = otop[1]
            SO = [0, 286, 572]
            ub = A.alloc([4, 1626], BF16)
            cv = A.alloc([4, NT], F32)
            Dg = A.alloc([31, 128], BF16)
            sgb = [A.alloc([512], F32) for _ in range(2)]
            memset("dve", R(ub.ap, ub.k()), 0.0)

            def useg(tb):
                if tb == 0:
                    return [(SO[0] + 15, 0, 256), (SO[1] + 15, 256, 256)]
                return [(SO[2] + 15 + (tb - 1) * 512, tb * 512, 512)]

            sla = wload(wv[:, :, 1536:2048], 8, 512)
            slg = wload(wv[:, :, 2048:2560], 8, 512)
            gi = 0
            for ci in range(4):
                for tb in range(3):
                    ba = (gi % 2) * 2
                    gi += 1
                    for k in range(8):
                        mm(PS(ba), R(sla[0][:, k, ci * 128:(ci + 1) * 128], *sla[1]), H(k, tb), k == 0, k == 7)
                    for k in range(8):
                        mm(PS(ba + 1), R(slg[0][:, k, ci * 128:(ci + 1) * 128], *slg[1]), H(k, tb), k == 0, k == 7)
                    sg = sgb[gi % 2]
                    Rsg = R(sg.ap, sg.k())
                    act(Rsg, PS(ba + 1), AF.Sigmoid)
                    for (u0, t0, n) in useg(tb):
                        o = t0 - tb * 512
                        tt("dve", R(ub.ap[:, ci, u0:u0 + n], ub.k(ci * 1626, (ci + 1) * 1626)),
                           PS(ba, slice(0, 128), o, o + n), R(sg.ap[:, o:o + n], sg.k()), ALU.mult)
            pieces = [(0, SO[0], 0, 256), (0, SO[1], 256, 256), (1, SO[2], 512, 512), (2, SO[2] + 512, 1024, 512)]
            mean_sb = A.alloc([3, 512], F32)
            for ci in range(4):
                for k in range(31):
                    ts("dve", R(Dg.ap[:, k, :], Dg.k()), Rib, col(l, C_DW + ci * 31 + k), None, ALU.mult)
                for pi_, (tb, u0, t0, n) in enumerate(pieces):
                    b = 4 + (pi_ % 2)
                    for k in range(31):
                        mm(PS(b, slice(0, 128), 0, n), R(Dg.ap[:, k, :], Dg.k()),
                           R(ub.ap[:, ci, u0 + k:u0 + k + n], ub.k(ci * 1626, (ci + 1) * 1626)), k == 0, k == 30)
                    cp("act", R(cv.ap[:, ci, t0:t0 + n], cv.k(ci * NT, (ci + 1) * NT)), PS(b, slice(0, 128), 0, n))
            lnb = [(6, 7), (4, 5), (2, 3)]
            for tb in range(3):
                for ci in range(4):
                    Rcv = R(cv.ap[:, ci, tb * 512:(tb + 1) * 512], cv.k(ci * NT, (ci + 1) * NT))
                    s1 = nsq()
                    cp("act", s1, Rcv)
                    mm(PS(lnb[tb][0]), Rones512, s1, ci == 0, ci == 3)
                    s2 = nsq()
                    act(s2, Rcv, AF.Square)
                    mm(PS(lnb[tb][1]), Rones512, s2, ci == 0, ci == 3)
            lnst = []
            for tb in range(3):
                Rmean = R(mean_sb.ap[:, tb, :], mean_sb.k(tb * 512, (tb + 1) * 512))
                cp("act", Rmean, PS(lnb[tb][0]))
                t = ntf()
                tt("dve", t, Rmean, Rmean, ALU.mult)
                tt("dve", t, PS(lnb[tb][1]), t, ALU.subtract)
                lnst.append((Rmean, rstd_from(t, 1.0)))
            for tb in range(3):
                Rmean, Rrstd = lnst[tb]
                for ci in range(4):
                    Rcv = R(cv.ap[:, ci, tb * 512:(tb + 1) * 512], cv.k(ci * NT, (ci + 1) * NT))
                    t = ntf()
                    tt("dve", t, Rcv, Rmean, ALU.subtract)
                    tt("dve", t, t, Rrstd, ALU.mult)
                    act(O(1, ci, tb * 512, (tb + 1) * 512), t, AF.Silu, bias=col(l, C_LNB + ci), scale=col(l, C_LNG + ci))
            run_bg(bg, 2)

        S.mark("L%d   gqa" % l)
        if "gqa" in mx:
            A.top = otop[2]
            gqT = A.alloc([4, NT], BF16)
            ropeC = A.alloc([1024], F32)
            ropeS = A.alloc([1024], F32)
            dma_in("sp", R(ropeC.ap, ropeC.k()), ropeC_d)
            dma_in("sp", R(ropeS.ap, ropeS.k()), ropeS_d)
            gkT = A.alloc([1, NT + 512], BF16)
            gV = A.alloc([16, 128], BF16)
            raw = [A.alloc([512], F32) for _ in range(2)]
            xnb = [A.alloc([512], BF16) for _ in range(2)]
            kst2 = [A.alloc([512], F32) for _ in range(1)]
            ost2 = [A.alloc([512], F32) for _ in range(1)]
            Ptl2 = [A.alloc([512], BF16) for _ in range(6)]
            rden2 = [A.alloc([512], F32) for _ in range(2)]
            Ptl[:] = Ptl2
            rden[:] = rden2
            rc = [0]

            def normrope(dst_fn, gcol, tb, b, want_f32=None):
                rc[0] += 1
                rw = raw[rc[0] % 2]
                Rraw = R(rw.ap, rw.k())
                cp("act", Rraw, PS(b))
                sq = nsq()
                act(sq, Rraw, AF.Square)
                sbank = 6 if rc[0] % 2 else 4
                pbank = 7 if rc[0] % 2 else 5
                mm(PS(sbank), Rbones, sq, True, True)
                RrsB = rstd_from(PS(sbank), 1.0 / 64)
                if tb == 0:
                    if want_f32 is not None:
                        stt(want_f32, Rraw, gcol, RrsB, ALU.mult, ALU.mult)
                        cp("act", dst_fn(), want_f32)
                    else:
                        stt(dst_fn(), Rraw, gcol, RrsB, ALU.mult, ALU.mult)
                    return
                stt(Rraw, Rraw, gcol, RrsB, ALU.mult, ALU.mult)
                xb_ = xnb[rc[0] % 2]
                Rxb = R(xb_.ap, xb_.k())
                cp("act", Rxb, Rraw)
                mm(PS(pbank), R(permM.ap, permM.k()), Rxb, True, True)
                t0 = (tb - 1) * 512
                t = ntf()
                tt("dve", t, PS(pbank), R(ropeS.ap[:, t0:t0 + 512], ropeS.k()), ALU.mult)
                tt("dve", Rraw, Rraw, R(ropeC.ap[:, t0:t0 + 512], ropeC.k()), ALU.mult)
                tt("dve", dst_fn(), Rraw, t, ALU.add)

            def h_gq(ci, tb, b):
                normrope(lambda: QT(gqT, ci, tb * 512, tb * 512 + 512), col(l, C_QN), tb, b)

            proj_fm(2560, 512, h_gq, [0, 1, 2])
            slkv = wload(wv[:, :, 3072:3328], 8, 256)
            for tb in range(3):
                b = tb % 3
                for k in range(8):
                    mm(PS(b), R(slkv[0][:, k, 0:128], *slkv[1]), H(k, tb), k == 0, k == 7)
                Rk = R(gkT.ap[:, 0, tb * 512:(tb + 1) * 512], gkT.k())
                if tb == 0:
                    st = rot(kst2, "k")
                    Rst = R(st.ap, st.k())
                    normrope(lambda: Rk, col(l, C_KN), tb, b, want_f32=Rst)
                    for q in range(4):
                        tr(PS(3, slice(0, 128), q * 128, q * 128 + 128), R(st.ap[:, q * 128:(q + 1) * 128], st.k()), Ri)
                    os_ = rot(ost2, "o")
                    Ros = R(os_.ap, os_.k())
                    cp("act", Ros, PS(3))
                    dma_out("sp", o_gqk[l].rearrange("(q p) f -> p q f", p=128),
                            R(os_.ap.rearrange("p (q f) -> p q f", q=4), os_.k()))
                else:
                    normrope(lambda: Rk, col(l, C_KN), tb, b)
            for t128 in range(12):
                b = t128 % 4
                for k in range(8):
                    mm(PS(b, slice(0, 128), 0, 128), Hc(k, t128 * 128, (t128 + 1) * 128), R(slkv[0][:, k, 128:256], *slkv[1]), k == 0, k == 7)
                cp("act", R(gV.ap[:, t128, :], gV.k()), PS(b, slice(0, 128), 0, 128))
                if t128 < 4:
                    os_ = rot(ost2, "o")
                    Ros = R(os_.ap[:, 0:128], os_.k())
                    cp("act", Ros, PS(b, slice(0, 128), 0, 128))
                    dma_out("sp", o_gqv[l, t128 * 128:(t128 + 1) * 128, :], Ros)
            st = rot(kst2, "k")
            Rst = R(st.ap.rearrange("p (q f) -> p q f", q=4), st.k())
            dma_in("sp", Rst, ck_gq[l].rearrange("(q p) f -> p q f", p=128))
            for q in range(4):
                tr(PS(3, slice(0, 128), q * 128, q * 128 + 128), R(st.ap[:, q * 128:(q + 1) * 128], st.k()), Ri)
            cp("dve", R(gkT.ap[:, 0, NT:NT + 512], gkT.k()), PS(3))
            dma_in("pool", R(gV.ap[:, 12:16, :], gV.k()), cv_gq[l].rearrange("(q p) f -> p q f", p=128))

            RgkT = gkT.k()
            ctx_attention(gqT, (gkT.ap, lambda c: 0, RgkT),
                          lambda t128, c: R(gV.ap[:, t128, :], gV.k()), 2)
            for h in range(8):
                c, half = h % 4, h // 4
                rows = slice(half * 64, half * 64 + 64)
                for piece in range(2):
                    tiles = []
                    for kb in range(12):
                        t0 = 512 + kb * 128 if kb < 8 else NT + (kb - 8) * 128
                        vi = 4 + kb if kb < 8 else 12 + (kb - 8)
                        tiles.append((R(gkT.ap[rows, 0, t0:t0 + 128], RgkT), R(gV.ap[:, vi, :], gV.k()), (0, 512), None))
                    q0 = 512 + piece * 512
                    bs = bsets[hcount[0] % 2]
                    hcount[0] += 1
                    attend(lambda lo, hi, c=c, rows=rows, q0=q0: R(gqT.ap[rows, c, q0 + lo:q0 + hi], gqT.k(c * NT, (c + 1) * NT)),
                           tiles, 512, lambda rws, c=c, q0=q0: O(2, c, q0, q0 + 512, rws), half, bs)
            run_bg(bg, 2)

        S.mark("L%d   lru" % l)
        if "lru" in mx:
            A.top = otop[3]
            LO = [0, 259, 518]
            LW = 1545
            NJ = LW - 3
            xc = A.alloc([LW], F32)
            xcb = A.alloc([LW], BF16)
            bd = A.alloc([4, 128], BF16)
            ga = A.alloc([LW], F32)
            gb = A.alloc([LW], F32)
            g3 = A.alloc([LW], F32)
            g4 = A.alloc([LW], F32)
            fin = A.alloc([16], F32)
            lam8 = A.alloc([8], F32)
            Rlam = R(lam8.ap, lam8.k())
            act(Rlam, R(colp.ap[:, l, C_LAM:C_LAM + 8], colp.k()), AF.Exp, scale=-1.0)
            ts("dve", Rlam, Rlam, 1.0, None, ALU.add)
            act(Rlam, Rlam, AF.Ln)
            ts("dve", Rlam, Rlam, -8.0, None, ALU.mult)

            def lseg(tb):
                if tb == 0:
                    return [(LO[0] + 2, 0, 256), (LO[1] + 2, 256, 256)]
                return [(LO[2] + 2 + (tb - 1) * 512, tb * 512, 512)]

            gpieces = [(LO[0], 256), (LO[1], 256), (LO[2], 512), (LO[2] + 512, 512)]
            sll = wload(wv[:, :, 3328:3840], 8, 512)

            def lru_unit(ci, u):
                rlo, rhi = (0, 518) if u == 0 else (518, LW)
                jlo, jhi = (0, 515) if u == 0 else (518, NJ)
                segs = [(0, LO[0], 0, 256), (1, LO[1], 256, 256)] if u == 0 else [(2, LO[2], 512, 1024)]
                gps = gpieces[0:2] if u == 0 else gpieces[2:4]
                tbs = (0,) if u == 0 else (1, 2)
                bk = 4 if u == 0 else 6

                def K(t):
                    return t.k(rlo, rhi)

                memset("dve", R(g4.ap[:, rlo:rhi], K(g4)), 0.0)
                for tb in tbs:
                    for k in range(8):
                        mm(PS(tb), R(sll[0][:, k, ci * 128:(ci + 1) * 128], *sll[1]), H(k, tb), k == 0, k == 7)
                    for (p0, t0, n) in lseg(tb):
                        o = t0 - tb * 512
                        cp("act", R(g4.ap[:, p0:p0 + n], K(g4)), PS(tb, slice(0, 128), o, o + n))
                yield
                Rxc = R(xc.ap[:, jlo:jhi], K(xc))
                ts("dve", Rxc, R(g4.ap[:, jlo:jhi], K(g4)), col(l, C_LCW + 0 * 4 + ci), col(l, C_LCB + ci), ALU.mult, ALU.add)
                for k in range(1, 4):
                    stt(Rxc, R(g4.ap[:, jlo + k:jhi + k], K(g4)), col(l, C_LCW + k * 4 + ci), Rxc, ALU.mult, ALU.add)
                cp("act", R(xcb.ap[:, jlo:jhi], K(xcb)), Rxc)
                yield
                for dr in range(2):
                    gi_ = g3 if dr == 0 else g4
                    for (g0, n) in gps:
                        mm(PS(bk, slice(0, 128), 0, n), R(bd.ap[:, dr * 2, :], bd.k()), R(xcb.ap[:, g0:g0 + n], K(xcb)), True, True)
                        mm(PS(bk + 1, slice(0, 128), 0, n), R(bd.ap[:, dr * 2 + 1, :], bd.k()), R(xcb.ap[:, g0:g0 + n], K(xcb)), True, True)
                        act(R(ga.ap[:, g0:g0 + n], K(ga)), PS(bk, slice(0, 128), 0, n), AF.Sigmoid, bias=col(l, C_BR + dr * 4 + ci))
                        act(R(gi_.ap[:, g0:g0 + n], K(gi_)), PS(bk + 1, slice(0, 128), 0, n), AF.Sigmoid, bias=col(l, C_BI + dr * 4 + ci))
                    yield
                    for (si, g0, s0, n) in segs:
                        Ra_ = R(ga.ap[:, g0:g0 + n], K(ga))
                        act(Ra_, Ra_, AF.Exp, scale=R(lam8.ap[:, dr * 4 + ci:dr * 4 + ci + 1], lam8.k()))
                    yield
                    for (si, g0, s0, n) in segs:
                        Ra_ = R(ga.ap[:, g0:g0 + n], K(ga))
                        Rb_ = R(gb.ap[:, g0:g0 + n], K(gb))
                        tt("dve", Rb_, Ra_, Ra_, ALU.mult)
                    for (si, g0, s0, n) in segs:
                        Rb_ = R(gb.ap[:, g0:g0 + n], K(gb))
                        act(Rb_, Rb_, AF.Sqrt, bias=R(ones_f.ap, ones_f.k()), scale=-1.0)
                    yield
                    for (si, g0, s0, n) in segs:
                        Rb_ = R(gb.ap[:, g0:g0 + n], K(gb))
                        Ri_ = R(gi_.ap[:, g0:g0 + n], K(gi_))
                        tt("dve", Ri_, Ri_, R(xc.ap[:, g0:g0 + n], K(xc)), ALU.mult)
                        tt("dve", Rb_, Rb_, Ri_, ALU.mult)
                    yield
                    hd = gi_
                    for (si, g0, s0, sn) in segs:
                        init = 0.0 if si < 2 else col(l, C_H0 + dr * 4 + ci)
                        if dr == 0:
                            scan(R(hd.ap[:, g0:g0 + sn], K(hd)), R(ga.ap[:, g0:g0 + sn], K(ga)), R(gb.ap[:, g0:g0 + sn], K(gb)), init)
                        else:
                            scan(R(hd.ap[:, g0:g0 + sn][:, ::-1], K(hd)), R(ga.ap[:, g0:g0 + sn][:, ::-1], K(ga)),
                                 R(gb.ap[:, g0:g0 + sn][:, ::-1], K(gb)), init)
                        if si < 2:
                            pos = g0 + sn - 1 if dr == 0 else g0
                            cp("pool", R(fin.ap[:, (si * 2 + dr) * 4 + ci:(si * 2 + dr) * 4 + ci + 1], fin.k()), R(hd.ap[:, pos:pos + 1], K(hd)))
                    yield
                for (si, g0, s0, sn) in segs:
                    tt("dve", O(3, ci, s0, s0 + sn), R(g3.ap[:, g0:g0 + sn], K(g3)), R(g4.ap[:, g0:g0 + sn], K(g4)), ALU.add)

            for ci in range(4):
                for dr_ in range(2):
                    for gt_ in range(2):
                        dma_in("pool", R(bd.ap[:, dr_ * 2 + gt_, :], bd.k()), lru_bd[l, :, dr_ * 8 + gt_ * 4 + ci, :],
                               group=(dr_ + gt_ > 0))
                gens = [lru_unit(ci, 1), lru_unit(ci, 0)]
                while gens:
                    for g_ in list(gens):
                        try:
                            next(g_)
                        except StopIteration:
                            gens.remove(g_)
            dma_out("sp", o_lru[l], R(fin.ap, fin.k()))
            run_bg(bg, 2)

        S.mark("L%d   merge" % l)
        if "merge" in mx:
            A.top = otop[3]
            mixT = A.alloc([8, NT], BF16)
            sgm = [A.alloc([512], BF16) for _ in range(2)]
            accb = [A.alloc([512], F32) for _ in range(2)]
            mi = 0
            for c in range(8):
                swb = wslot()
                wbd = swb.ap[:, 0:2048].rearrange("p (k kc n) -> p k kc n", k=4, kc=4)
                for k in range(4):
                    dma_in("pool", R(wbd[:, k, :, :], swb.k()), wview(w_br[l, k])[:, :, c * 128:(c + 1) * 128], group=k > 0)
                slg_ = wload(wv[:, :, 3840 + c * 512:3840 + (c + 1) * 512], 8, 512)
                for tb in range(3):
                    acc = accb[(c * 3 + tb) % 2]
                    Racc = R(acc.ap, acc.k())
                    for k in range(4):
                        bp = (mi % 3) * 2
                        mi += 1
                        for kc in range(4):
                            mm(PS(bp), R(wbd[:, k, kc, :], swb.k()),
                               R(Obr[k].ap[:, kc, tb * 512:(tb + 1) * 512], Obr[k].k(kc * NT, (kc + 1) * NT)), kc == 0, kc == 3)
                        for kk in range(8):
                            mm(PS(bp + 1), R(slg_[0][:, kk, k * 128:(k + 1) * 128], *slg_[1]), H(kk, tb), kk == 0, kk == 7)
                        sg = sgm[mi % 2]
                        Rsg = R(sg.ap, sg.k())
                        act(Rsg, PS(bp + 1), AF.Sigmoid)
                        if k == 0:
                            tt("dve", Racc, PS(bp), Rsg, ALU.mult)
                        else:
                            t = ntf()
                            tt("dve", t, PS(bp), Rsg, ALU.mult)
                            if k < 3:
                                tt("dve", Racc, Racc, t, ALU.add)
                            else:
                                tt("dve", R(mixT.ap[:, c, tb * 512:(tb + 1) * 512], mixT.k(c * NT + tb * 512, c * NT + (tb + 1) * 512)),
                                   Racc, t, ALU.add)
                run_bg(bg, 1)
            wov = wview(w_out[l])
            wos = [None, None]

            def get_wo(c):
                if c % 4 == 0:
                    wos[0] = wload(wov[:, :, (c // 4) * 512:(c // 4 + 1) * 512], 8, 512)
                sl = wos[0]
                cc_ = c % 4
                return lambda k: R(sl[0][:, k, cc_ * 128:(cc_ + 1) * 128], *sl[1])

            def rhs_mix(k, tb):
                return R(mixT.ap[:, k, tb * 512:(tb + 1) * 512], mixT.k(k * NT + tb * 512, k * NT + (tb + 1) * 512))

            outproj(get_wo, 8, rhs_mix, hT, bg)
            post_update(par, sub, hT, [5, 6, 7])

    memset("dve", R(ones_f.ap, ones_f.k()), 1.0)

    if adaln_on:
        g0 = adaln(0, 0)
        for _ in g0:
            pass
    for l in range(depth):
        par = l % 2
        bg = adaln(l + 1, 1 - par) if (l + 1 < depth and adaln_on) else None
        S.mark("L%d ffn0" % l)
        if "ffn0" in do:
            ffn(l, par, 0, bg)
        S.mark("L%d mixer" % l)
        if "mixer" in do:
            mixer(l, par, bg)
        S.mark("L%d ffn2" % l)
        if "ffn2" in do:
            ffn(l, par, 2, bg)
        if bg is not None:
            for _ in bg:
                pass

    S.mark("final")
    A.top = phase_base
    yst = [A.alloc([D], F32) for _ in range(2)]
    for t128 in range(12):
        tb, q = t128 // 4, t128 % 4
        st = yst[t128 % 2]
        Rst = R(st.ap, st.k())
        bb = (t128 % 2) * 2
        for c in range(8):
            tr(PS(bb + c // 4, slice(0, 128), (c % 4) * 128, (c % 4) * 128 + 128),
               R(xT.ap[:, c, t128 * 128:(t128 + 1) * 128], xT.k(c * NT + tb * 512, c * NT + (tb + 1) * 512)), Ri)
        cp("act", R(st.ap[:, 0:512], st.k()), PS(bb))
        cp("dve", R(st.ap[:, 512:1024], st.k()), PS(bb + 1))
        dma_out("sp", y_out[t128 * 128:(t128 + 1) * 128, :], Rst)

    S.finish()
    S.emit(nc, es)
    es.close()
    nc._marks = S.marks
    nc._ndsem = len(S.dsem)
    nc._counts = {n: e.count for n, e in S.E.items()}
    return nc


_PROG = {}


def _consts():
    GRID_W = 64
    t = np.arange(1024)
    row = (t // GRID_W).astype(np.float32)
    colv = (t % GRID_W).astype(np.float32)
    half = 32
    freqs = (np.float32(10000.0) ** (-np.arange(0, half, 2, dtype=np.float32) / np.float32(half))).astype(np.float32)
    ar = row[:, None] * freqs
    ac = colv[:, None] * freqs
    C = np.zeros((128, 1024), np.float32)
    Sg = np.zeros((128, 1024), np.float32)
    P = np.zeros((128, 128), np.float32)
    for p in range(128):
        dd = p % 64
        ang = ar if dd < 32 else ac
        f = dd % 16
        C[p] = np.cos(ang[:, f])
        s = np.sin(ang[:, f])
        if dd % 32 < 16:
            Sg[p] = -s
            P[p + 16, p] = 1.0
        else:
            Sg[p] = s
            P[p - 16, p] = 1.0
    cidx = np.arange(64)
    c0 = np.clip(cidx - 8, 0, 48)
    ok = (cidx[None, :] >= c0[:, None]) & (cidx[None, :] < c0[:, None] + 16)
    return C, Sg, P, np.ascontiguousarray(ok.astype(np.float32)[::-1]), np.eye(128, dtype=np.float32)


def kernel(x_prompt, x_sample, c, cache_na_k, cache_na_v, cache_gqa_k, cache_gqa_v, state_lru, c_ctx,
           w_ada, b_ada, norm_g, ffn_w1, ffn_w2, w_in, na_rpb, conv_dw, conv_ln_g, conv_ln_b,
           gqa_q_norm, gqa_k_norm, lru_conv_w, lru_conv_b, lru_wr, lru_br, lru_wi, lru_bi, lru_lambda,
           w_branch, w_out):
    f = lambda a: np.ascontiguousarray(np.asarray(a, dtype=np.float32))
    ncores = 8
    if "nc" not in _PROG:
        _PROG["nc"] = build_program()
    nc = _PROG["nc"]
    ropeC, ropeS, permM, cmask, ident = _consts()

    OFF_NA, OFF_CONV, OFF_GQA, OFF_LRU, OFF_GATE = 0, 1536, 2560, 3328, 3840
    perm = []
    perm += list(range(0, 512))
    perm += list(range(512, 1024))
    perm += list(range(1024, 1536))
    perm += list(range(OFF_CONV, OFF_CONV + 1024))
    for cch in range(4):
        perm += list(range(OFF_GQA + cch * 64, OFF_GQA + cch * 64 + 64))
        perm += list(range(OFF_GQA + (cch + 4) * 64, OFF_GQA + (cch + 4) * 64 + 64))
    perm += list(range(OFF_GQA + 512, OFF_GQA + 768))
    perm += list(range(OFF_LRU, OFF_LRU + 512))
    for cch in range(8):
        for k in range(4):
            perm += list(range(OFF_GATE + k * 1024 + cch * 128, OFF_GATE + k * 1024 + cch * 128 + 128))
    perm = np.array(perm)
    assert perm.shape[0] == 7936
    w_in_r = f(np.asarray(w_in)[:, :, perm])
    wbr = np.asarray(w_branch, dtype=np.float32).copy()
    rp = []
    for cch in range(4):
        rp += list(range(cch * 64, cch * 64 + 64)) + list(range((cch + 4) * 64, (cch + 4) * 64 + 64))
    wbr[:, 2] = wbr[:, 2][:, np.array(rp), :]
    rpb_pad = np.zeros((DEPTH, 8, 15, 127), np.float32)
    rpb_pad[..., 48:79] = np.asarray(na_rpb)
    wr_ = np.asarray(lru_wr, dtype=np.float32)
    wi_ = np.asarray(lru_wi, dtype=np.float32)
    bdm = np.zeros((DEPTH, 16, 128, 128), np.float32)
    for dr in range(2):
        for gt, wsrc in enumerate((wr_, wi_)):
            for cch in range(4):
                m = bdm[:, dr * 8 + gt * 4 + cch]
                m[:, 0:64, 0:64] = wsrc[:, dr, 2 * cch]
                m[:, 64:128, 64:128] = wsrc[:, dr, 2 * cch + 1]
    lru_bd = f(bdm.transpose(0, 2, 1, 3))

    def colsT(v, n):
        return np.asarray(v, dtype=np.float32).reshape(n, 128).T

    shared = {
        "w_ada": f(w_ada), "ffn_w1": f(ffn_w1), "ffn_w2": f(ffn_w2), "w_in_r": w_in_r, "rpb_pad": rpb_pad,
        "lru_bd": lru_bd, "w_branch_r": f(wbr), "w_out": f(w_out), "ropeC": ropeC, "ropeS": ropeS,
        "permM": permM, "cmask": cmask, "ident": ident,
    }
    xp = np.asarray(x_prompt, dtype=np.float32)
    xs = np.asarray(x_sample, dtype=np.float32)
    in_maps = []
    for i in range(ncores):
        colp = np.zeros((DEPTH, 128, NCOL), np.float32)
        for l in range(DEPTH):
            colp[l, :, C_G:C_G + 48] = colsT(np.asarray(norm_g)[l].reshape(-1), 48)
            colp[l, :, C_BADA:C_BADA + 72] = colsT(np.asarray(b_ada)[l], 72)
            dw = np.asarray(conv_dw, dtype=np.float32)[l]
            colp[l, :, C_DW:C_DW + 124] = dw.reshape(31, 4, 128).transpose(2, 1, 0).reshape(128, 124)
            colp[l, :, C_LNG:C_LNG + 4] = colsT(np.asarray(conv_ln_g)[l], 4)
            colp[l, :, C_LNB:C_LNB + 4] = colsT(np.asarray(conv_ln_b)[l], 4)
            colp[l, :, C_QN] = np.tile(np.asarray(gqa_q_norm, dtype=np.float32)[l], 2)
            colp[l, :, C_KN] = np.tile(np.asarray(gqa_k_norm, dtype=np.float32)[l], 2)
            lw = np.asarray(lru_conv_w, dtype=np.float32)[l]
            colp[l, :, C_LCW:C_LCW + 16] = lw.reshape(4, 4, 128).transpose(2, 0, 1).reshape(128, 16)
            colp[l, :, C_LCB:C_LCB + 4] = colsT(np.asarray(lru_conv_b)[l], 4)
            colp[l, :, C_BR:C_BR + 8] = colsT(np.asarray(lru_br)[l].reshape(-1), 8)
            colp[l, :, C_BI:C_BI + 8] = colsT(np.asarray(lru_bi)[l].reshape(-1), 8)
            colp[l, :, C_LAM:C_LAM + 8] = colsT(np.asarray(lru_lambda)[l].reshape(-1), 8)
            colp[l, :, C_H0:C_H0 + 8] = colsT(np.asarray(state_lru)[i, l].reshape(-1), 8)
        m = dict(shared)
        m["x_in"] = f(np.concatenate([xp[2 * i].reshape(256, D), xp[2 * i + 1].reshape(256, D), xs[i]], axis=0))
        m["cc"] = f(np.stack([np.asarray(c_ctx), np.asarray(c)[i]], axis=0))
        m["ck_na"] = f(np.asarray(cache_na_k)[i].reshape(DEPTH, 512, 512))
        m["cv_na"] = f(np.asarray(cache_na_v)[i].reshape(DEPTH, 512, 512))
        m["ck_gq"] = f(np.asarray(cache_gqa_k)[i].reshape(DEPTH, 512, 128))
        m["cv_gq"] = f(np.asarray(cache_gqa_v)[i].reshape(DEPTH, 512, 128))
        m["colp"] = colp
        in_maps.append(m)

    res = run_bass_kernel_spmd(nc, in_maps, core_ids=list(range(ncores)))
    outs = res.results
    y_prompt = np.zeros((16, 256, D), np.float32)
    y_sample = np.zeros((8, 1024, D), np.float32)
    nak = np.zeros((16, DEPTH, 256, 8, 64), np.float32)
    nav = np.zeros((16, DEPTH, 256, 8, 64), np.float32)
    gqk = np.zeros((16, DEPTH, 256, 2, 64), np.float32)
    gqv = np.zeros((16, DEPTH, 256, 2, 64), np.float32)
    lst = np.zeros((16, DEPTH, 2, 512), np.float32)
    for i in range(ncores):
        o = outs[i]
        y = np.asarray(o["y_out"])
        y_prompt[2 * i] = y[0:256]
        y_prompt[2 * i + 1] = y[256:512]
        y_sample[i] = y[512:]
        for s in range(2):
            b = 2 * i + s
            nak[b] = np.asarray(o["o_nak"]).reshape(DEPTH, 512, 512)[:, s * 256:(s + 1) * 256].reshape(DEPTH, 256, 8, 64)
            nav[b] = np.asarray(o["o_nav"]).reshape(DEPTH, 512, 512)[:, s * 256:(s + 1) * 256].reshape(DEPTH, 256, 8, 64)
            gqk[b] = np.asarray(o["o_gqk"]).reshape(DEPTH, 512, 128)[:, s * 256:(s + 1) * 256].reshape(DEPTH, 256, 2, 64)
            gqv[b] = np.asarray(o["o_gqv"]).reshape(DEPTH, 512, 128)[:, s * 256:(s + 1) * 256].reshape(DEPTH, 256, 2, 64)
            fl = np.asarray(o["o_lru"]).reshape(DEPTH, 128, 16)
            for dr in range(2):
                blk = fl[:, :, (s * 2 + dr) * 4:(s * 2 + dr) * 4 + 4]
                lst[b, :, dr] = blk.transpose(0, 2, 1).reshape(DEPTH, 512)
    return (y_prompt, y_sample, nak, nav, gqk, gqv, lst)
```

```python
import numpy as np
from contextlib import ExitStack
import concourse.bass as bass
import concourse.mybir as mybir
from concourse.bass_utils import run_bass_kernel_spmd

F32 = mybir.dt.float32
BF16 = mybir.dt.bfloat16
U8 = mybir.dt.uint8
AF = mybir.ActivationFunctionType
ALU = mybir.AluOpType

DEPTH = 4
D = 1024
NT = 1536
DFF = 2816
EPS = 1e-6
NCOL = 320
C_G = 0
C_BADA = 48
C_DW = 120
C_LNG = 244
C_LNB = 248
C_QN = 252
C_KN = 253
C_LCW = 254
C_LCB = 270
C_BR = 274
C_BI = 282
C_LAM = 290
C_H0 = 298
SEGS = [(0, 256), (256, 256), (512, 1024)]


class Eng:
    def __init__(self, name):
        self.name = name
        self.ops = []
        self.count = 0
        self.seen = {}
        self.semkey = "E_" + name


class Sched:
    def __init__(self):
        self.E = {n: Eng(n) for n in ("pe", "act", "dve", "pool", "sp")}
        self.recs = {}
        self.keys = {"S": [], "P": []}
        self.ov = {}
        self.dsem = {}
        self.marks = []

    def mark(self, label):
        self.marks.append((label, {n: e.count for n, e in self.E.items()}))

    def _reg(self, key):
        if key in self.ov:
            return
        sp, lo, hi = key
        o = []
        for k2 in self.keys[sp]:
            if k2[1] < hi and lo < k2[2]:
                o.append(k2)
                self.ov[k2].append(key)
        self.ov[key] = o
        self.keys[sp].append(key)
        self.recs[key] = [None, {}]

    def _deps(self, eng, reads, writes, is_dma=False, skip_sem=None):
        toks = {}

        def add(t, raw):
            if t is None:
                return
            if t[0] == skip_sem:
                return
            if (not is_dma) and t[2] == eng.name:
                if eng.name == "pe":
                    return
            if toks.get(t[0], 0) < t[1]:
                toks[t[0]] = t[1]

        for r in reads:
            self._reg(r)
            for k in [r] + self.ov[r]:
                add(self.recs[k][0], True)
        for w in writes:
            self._reg(w)
            for k in [w] + self.ov[w]:
                rec = self.recs[k]
                add(rec[0], False)
                for t in rec[1].values():
                    add(t, False)
        return toks

    def _push(self, eng, toks, fn, semkey, inc):
        waits = []
        for k, v in toks.items():
            if eng.seen.get(k, 0) < v:
                eng.seen[k] = v
                waits.append((k, v))
        eng.ops.append((waits, fn, semkey, inc))

    def _record(self, tok, reads, writes):
        for r in reads:
            self.recs[r][1][tok[0]] = tok
        for w in writes:
            rec = self.recs[w]
            rec[0] = tok
            rec[1] = {}

    def op(self, en, fn, reads=(), writes=()):
        eng = self.E[en]
        reads = list(reads)
        writes = list(writes)
        toks = self._deps(eng, reads, writes)
        eng.count += 1
        tok = (eng.semkey, eng.count, eng.name)
        self._push(eng, toks, fn, eng.semkey, 1)
        self._record(tok, reads, writes)

    def dma(self, en, fn, region, write, group=False, reads=(), writes=()):
        eng = self.E[en]
        reads = list(reads) + ([] if write else [region])
        writes = list(writes) + ([region] if write else [])
        dk = (region, en == "pool")
        if dk not in self.dsem:
            self.dsem[dk] = ["D%d" % len(self.dsem), 0]
        ds = self.dsem[dk]
        toks = self._deps(eng, reads, writes, is_dma=True, skip_sem=ds[0] if group else None)
        ds[1] += 16
        tok = (ds[0], ds[1], None)
        self._push(eng, toks, fn, ds[0], 16)
        self._record(tok, reads, writes)

    def finish(self):
        eng = self.E["sp"]
        toks = {ds[0]: ds[1] for ds in self.dsem.values()}
        for e in self.E.values():
            if e.count:
                toks[e.semkey] = e.count
        self._push(eng, toks, None, None, 0)

    def emit(self, nc, es):
        keys = [e.semkey for e in self.E.values()] + [ds[0] for ds in self.dsem.values()]
        sems = {k: es.enter_context(nc.semaphore(k)) for k in keys}
        with nc.Block() as block:
            regs = {"pe": block.tensor, "act": block.scalar, "dve": block.vector,
                    "pool": block.gpsimd, "sp": block.sync}
            for name, reg in regs.items():
                eng = self.E[name]

                def f(e, eng=eng):
                    for waits, fn, sk, inc in eng.ops:
                        for k, v in waits:
                            e.wait_ge(sems[k], v)
                        if fn is not None:
                            fn(e).then_inc(sems[sk], inc)
                reg(f)


def _esz(dt):
    return 4 if dt == F32 else (2 if dt == BF16 else 1)


class T:
    def __init__(self, arena_ap, off, shape, dtype):
        n = int(np.prod(shape))
        self.off = off
        self.esz = _esz(dtype)
        self.nb = n * self.esz
        self.shape = tuple(shape)
        ap = arena_ap[:, off:off + self.nb].bitcast(dtype)
        if len(shape) == 2:
            ap = ap.rearrange("p (a b) -> p a b", a=shape[0])
        elif len(shape) == 3:
            ap = ap.rearrange("p (a b c) -> p a b c", a=shape[0], b=shape[1])
        elif len(shape) == 4:
            ap = ap.rearrange("p (a b c d) -> p a b c d", a=shape[0], b=shape[1], c=shape[2])
        self.ap = ap

    def k(self, lo=None, hi=None):
        if lo is None:
            return ("S", self.off, self.off + self.nb)
        return ("S", self.off + lo * self.esz, self.off + hi * self.esz)


class Arena:
    def __init__(self, ap, size):
        self.ap = ap
        self.size = size
        self.top = 0

    def alloc(self, shape, dtype):
        off = (self.top + 63) // 64 * 64
        t = T(self.ap, off, shape, dtype)
        self.top = off + t.nb
        assert self.top <= self.size, ("arena overflow", self.top, self.size)
        return t


def R(ap, *keys):
    return (ap, list(keys))


def build_program(depth=DEPTH, do=("ffn0", "mixer", "ffn2"), adaln_on=True, mx=("pq", "pk", "pv", "pvo", "pc", "nactx", "nalat", "conv", "gqa", "lru", "merge")):
    nc = bass.Bass("TRN2", target_bir_lowering=False)
    S = Sched()
    es = ExitStack()

    def din(name, shape):
        return nc.dram_tensor(name, list(shape), F32, kind="ExternalInput").ap()

    def dout(name, shape):
        return nc.dram_tensor(name, list(shape), F32, kind="ExternalOutput").ap()

    x_in = din("x_in", [NT, D])
    cc = din("cc", [2, D])
    ck_na = din("ck_na", [DEPTH, 512, 512])
    cv_na = din("cv_na", [DEPTH, 512, 512])
    ck_gq = din("ck_gq", [DEPTH, 512, 128])
    cv_gq = din("cv_gq", [DEPTH, 512, 128])
    colp_d = din("colp", [DEPTH, 128, NCOL])
    w_ada = din("w_ada", [DEPTH, D, 9 * D])
    ffn_w1 = din("ffn_w1", [DEPTH, 2, D, 2 * DFF])
    ffn_w2 = din("ffn_w2", [DEPTH, 2, DFF, D])
    w_in = din("w_in_r", [DEPTH, D, 7936])
    rpb = din("rpb_pad", [DEPTH, 8, 15, 127])
    lru_bd = din("lru_bd", [DEPTH, 128, 16, 128])
    w_br = din("w_branch_r", [DEPTH, 4, 512, D])
    w_out = din("w_out", [DEPTH, D, D])
    ropeC_d = din("ropeC", [128, 1024])
    ropeS_d = din("ropeS", [128, 1024])
    permM_d = din("permM", [128, 128])
    cmask_d = din("cmask", [64, 64])
    ident_d = din("ident", [128, 128])

    y_out = dout("y_out", [NT, D])
    o_nak = dout("o_nak", [DEPTH * 512, 512]).rearrange("(l t) f -> l t f", l=DEPTH)
    o_nav = dout("o_nav", [DEPTH * 512, 512]).rearrange("(l t) f -> l t f", l=DEPTH)
    o_gqk = dout("o_gqk", [DEPTH * 512, 128]).rearrange("(l t) f -> l t f", l=DEPTH)
    o_gqv = dout("o_gqv", [DEPTH * 512, 128]).rearrange("(l t) f -> l t f", l=DEPTH)
    o_lru = dout("o_lru", [DEPTH * 128, 16]).rearrange("(l t) f -> l t f", l=DEPTH)

    ARENA = 207 * 1024
    arena_t = es.enter_context(nc.sbuf_tensor("arena", [128, ARENA], U8))
    A = Arena(arena_t, ARENA)
    ps_t = es.enter_context(nc.psum_tensor("ps", [128, 8, 512], F32))

    def PS(b, rows=slice(0, 128), lo=0, hi=512):
        return R(ps_t[rows, b, lo:hi], ("P", b * 2048 + lo * 4, b * 2048 + hi * 4))

    def mm(out, lhsT, rhs, start, stop, **kw):
        S.op("pe", lambda e: e.matmul(out[0], lhsT=lhsT[0], rhs=rhs[0], start=start, stop=stop, **kw),
             reads=lhsT[1] + rhs[1], writes=out[1])

    def tr(out, in_, ident):
        S.op("pe", lambda e: e.transpose(out[0], in_[0], ident[0]), reads=in_[1] + ident[1], writes=out[1])

    def act(out, in_, func, bias=None, scale=1.0):
        rd = list(in_[1])
        kw = {}
        if bias is not None:
            kw["bias"] = bias[0]
            rd += bias[1]
        if isinstance(scale, tuple):
            rd += scale[1]
            sc = scale[0]
        else:
            sc = scale
        S.op("act", lambda e: e.activation(out=out[0], in_=in_[0], func=func, scale=sc, **kw),
             reads=rd, writes=out[1])

    def tt(en, out, a, b, op):
        S.op(en, lambda e: e.tensor_tensor(out[0], a[0], b[0], op), reads=a[1] + b[1], writes=out[1])

    def ts(en, out, a, s1, s2, op0, op1=None):
        rd = list(a[1])
        v1 = s1
        v2 = s2
        if isinstance(s1, tuple):
            rd += s1[1]
            v1 = s1[0]
        if isinstance(s2, tuple):
            rd += s2[1]
            v2 = s2[0]
        if op1 is None:
            S.op(en, lambda e: e.tensor_scalar(out[0], a[0], v1, None, op0), reads=rd, writes=out[1])
        else:
            S.op(en, lambda e: e.tensor_scalar(out[0], a[0], v1, v2, op0, op1), reads=rd, writes=out[1])

    def stt(out, in0, sc, in1, op0, op1):
        rd = in0[1] + in1[1]
        v = sc
        if isinstance(sc, tuple):
            rd = rd + sc[1]
            v = sc[0]
        S.op("dve", lambda e: e.scalar_tensor_tensor(out[0], in0[0], v, in1[0], op0, op1), reads=rd, writes=out[1])

    def cp(en, out, in_):
        if en == "act":
            S.op("act", lambda e: e.copy(out[0], in_[0]), reads=in_[1], writes=out[1])
        else:
            S.op(en, lambda e: e.tensor_copy(out[0], in_[0]), reads=in_[1], writes=out[1])

    def recip(out, in_):
        S.op("dve", lambda e: e.reciprocal(out[0], in_[0]), reads=in_[1], writes=out[1])

    def scan(out, d0, d1, init):
        rd = d0[1] + d1[1]
        v = init
        if isinstance(init, tuple):
            rd = rd + init[1]
            v = init[0]
        S.op("dve", lambda e: e.tensor_tensor_scan(out[0], d0[0], d1[0], v, ALU.mult, ALU.add), reads=rd, writes=out[1])

    def memset(en, out, val):
        S.op(en, lambda e: e.memset(out[0], val), writes=out[1])

    def dma_in(en, dst, src_ap, group=False):
        S.dma(en, lambda e: e.dma_start(out=dst[0], in_=src_ap), dst[1][0], True, group=group)

    def dma_out(en, dst_ap, src, **kw):
        S.dma(en, lambda e: e.dma_start(out=dst_ap, in_=src[0], **kw), src[1][0], False)

    ident = A.alloc([128], F32)
    identb = A.alloc([128], BF16)
    ones = A.alloc([128], BF16)
    bones = A.alloc([128], BF16)
    ones512 = A.alloc([128], BF16)
    permM = A.alloc([128], BF16)
    cmask = A.alloc([64], F32)
    epsc = A.alloc([1], F32)
    colp = A.alloc([DEPTH, NCOL], F32)
    scT = A.alloc([8, 2], BF16)
    rowb = A.alloc([512], F32)
    ones_f = A.alloc([1], F32)
    modT = [A.alloc([72, 2], F32) for _ in range(2)]
    Atab = [A.alloc([2, 3, 8], F32) for _ in range(2)]
    Btab = [A.alloc([2, 3, 8], F32) for _ in range(2)]

    Ri = R(ident.ap, ident.k())
    Rib = R(identb.ap, identb.k())
    Rones = R(ones.ap, ones.k())
    Rbones = R(bones.ap, bones.k())
    Rones512 = R(ones512.ap, ones512.k())
    Reps = R(epsc.ap, epsc.k())

    def col(l, j):
        return R(colp.ap[:, l, j:j + 1], colp.k())

    dma_in("sp", Ri, ident_d)
    dma_in("sp", R(cmask.ap[0:64, :], cmask.k()), cmask_d)
    for l in range(DEPTH):
        dma_in("sp", R(colp.ap[:, l, :], colp.k()), colp_d[l], group=l > 0)
    dma_in("pool", R(permM.ap, permM.k()), permM_d)
    cp("dve", Rib, Ri)
    memset("dve", Rones, 1.0)
    memset("dve", Rones512, 1.0 / 512.0)
    memset("dve", Rbones, 0.0)
    memset("dve", R(bones.ap[0:64, 0:64], bones.k()), 1.0)
    memset("dve", R(bones.ap[64:128, 64:128], bones.k()), 1.0)
    memset("dve", Reps, EPS)

    xT = A.alloc([8, NT], F32)
    hT = A.alloc([8, NT], BF16)
    NS = 3
    wslots = [A.alloc([4096], BF16) for _ in range(NS)]
    wctr = [0]
    phase_base = A.top

    def X(c, tb):
        return R(xT.ap[:, c, tb * 512:(tb + 1) * 512], xT.k(c * NT + tb * 512, c * NT + (tb + 1) * 512))

    def H(c, tb):
        return R(hT.ap[:, c, tb * 512:(tb + 1) * 512], hT.k(c * NT + tb * 512, c * NT + (tb + 1) * 512))

    def Hc(c, lo, hi):
        tb = lo // 512
        assert (hi - 1) // 512 == tb
        return R(hT.ap[:, c, lo:hi], hT.k(c * NT + tb * 512, c * NT + (tb + 1) * 512))

    def wslot():
        s = wslots[wctr[0] % NS]
        wctr[0] += 1
        return s

    def wload(src_ap, k, n):
        s = wslot()
        dst = s.ap[:, 0:k * n].rearrange("p (k n) -> p k n", k=k)
        dma_in("pool", R(dst, s.k()), src_ap)
        return R(dst, s.k())

    def wview(w2d):
        return w2d.rearrange("(k p) n -> p k n", p=128)

    sqb = [A.alloc([512], BF16) for _ in range(3)]
    rs3 = [A.alloc([512], F32) for _ in range(3)]
    tmpf = [A.alloc([512], F32) for _ in range(3)]
    sqi = [0]
    tfi = [0]
    phase_base = A.top

    def nsq():
        sqi[0] += 1
        t = sqb[sqi[0] % 3]
        return R(t.ap, t.k())

    def ntf():
        tfi[0] += 1
        t = tmpf[tfi[0] % 3]
        return R(t.ap, t.k())

    rsi = [0]

    def nrs():
        rsi[0] += 1
        t = rs3[rsi[0] % 3]
        return R(t.ap, t.k())

    def rstd_from(ps_in, scale):
        r = nrs()
        act(r, ps_in, AF.Ln, bias=Reps, scale=scale)
        act(r, r, AF.Exp, scale=-0.5)
        return r

    A.top = phase_base
    xst = [A.alloc([D], F32) for _ in range(2)]
    for tb in range(3):
        for q in range(4):
            b = tb * 4 + q
            st = xst[b % 2]
            Rst = R(st.ap, st.k())
            dma_in("sp", Rst, x_in[b * 128:(b + 1) * 128, :])
            for c in range(8):
                tr(PS(c, slice(0, 128), q * 128, (q + 1) * 128), R(st.ap[:, c * 128:(c + 1) * 128], st.k()), Ri)
        for c in range(8):
            cp("act" if c % 2 else "dve", X(c, tb), PS(c))

    ccs = xst[0]
    Rcc = R(ccs.ap[0:2, :], ccs.k())
    dma_in("sp", Rcc, cc)
    act(Rcc, Rcc, AF.Silu)
    for k in range(8):
        tr(PS(0, slice(0, 128), k * 2, k * 2 + 2), R(ccs.ap[0:2, k * 128:(k + 1) * 128], ccs.k()), R(ident.ap[0:2, 0:2], ident.k()))
    cp("dve", R(scT.ap, scT.k()), R(ps_t[:, 0, 0:16].rearrange("p (k w) -> p k w", k=8), PS(0, slice(0, 128), 0, 16)[1][0]))

    def adaln(l, par):
        mt = modT[par]
        A_ = Atab[par]
        B_ = Btab[par]
        wv = wview(w_ada[l])
        for g in range(18):
            sl = wload(wv[:, :, g * 512:(g + 1) * 512], 8, 512)
            for k in range(8):
                mm(PS(7, slice(0, 2)), R(scT.ap[:, k, :], scT.k()), R(sl[0][:, k, :], *sl[1]), k == 0, k == 7)
            cp("act", R(rowb.ap[0:2, 0:512], rowb.k()), PS(7, slice(0, 2)))
            for q in range(4):
                tr(PS(6, slice(0, 128), q * 2, q * 2 + 2), R(rowb.ap[0:2, q * 128:(q + 1) * 128], rowb.k()),
                   R(ident.ap[0:2, 0:2], ident.k()))
            cp("dve", R(mt.ap[:, g * 4:(g + 1) * 4, :], mt.k()),
               R(ps_t[:, 6, 0:8].rearrange("p (q w) -> p q w", q=4), PS(6, slice(0, 128), 0, 8)[1][0]))
            yield
        Rm = mt.k()
        for who in range(2):
            tt("dve", R(mt.ap[:, :, who], Rm), R(mt.ap[:, :, who], Rm), R(colp.ap[:, l, C_BADA:C_BADA + 72], colp.k()), ALU.add)
        for who in range(2):
            for sub in range(3):
                stt(R(A_.ap[:, who, sub, :], A_.k()), R(mt.ap[:, sub * 24 + 8:sub * 24 + 16, who], Rm), 1.0,
                    R(colp.ap[:, l, C_G + sub * 16:C_G + sub * 16 + 8], colp.k()), ALU.add, ALU.mult)
                stt(R(B_.ap[:, who, sub, :], B_.k()), R(mt.ap[:, sub * 24 + 16:sub * 24 + 24, who], Rm),
                    0.5 if sub != 1 else 1.0,
                    R(colp.ap[:, l, C_G + sub * 16 + 8:C_G + sub * 16 + 16], colp.k()), ALU.mult, ALU.mult)
        yield

    def run_bg(gen, n=1):
        if gen is None:
            return
        for _ in range(n):
            try:
                next(gen)
            except StopIteration:
                return

    def prenorm(par, sub):
        banks = [6, 7, 6]
        rsl = []
        for tb in range(3):
            for c in range(8):
                sq = nsq()
                act(sq, X(c, tb), AF.Square)
                mm(PS(banks[tb]), Rones, sq, c == 0, c == 7)
            rsl.append(rstd_from(PS(banks[tb]), 1.0 / D))
        for tb in range(3):
            who = 0 if tb == 0 else 1
            for c in range(8):
                t = ntf()
                stt(t, X(c, tb), R(Atab[par].ap[:, who, sub, c:c + 1], Atab[par].k()), rsl[tb], ALU.mult, ALU.mult)
                act(H(c, tb), t, AF.Identity,
                    bias=R(modT[par].ap[:, sub * 24 + c, who:who + 1], modT[par].k()))

    def post_update(par, sub, ybuf, sb):
        rsl = [rstd_from(PS(sb[tb]), 1.0 / D) for tb in range(3)]
        for tb in range(3):
            who = 0 if tb == 0 else 1
            for c in range(8):
                t = ntf()
                Ry = R(ybuf.ap[:, c, tb * 512:(tb + 1) * 512], ybuf.k(c * NT + tb * 512, c * NT + (tb + 1) * 512))
                tt("pool" if c % 2 else "dve", t, Ry, rsl[tb], ALU.mult)
                stt(X(c, tb), t, R(Btab[par].ap[:, who, sub, c:c + 1], Btab[par].k()), X(c, tb), ALU.mult, ALU.add)

    def outproj(get_w, nk, rhs_fn, ybuf, bg=None):
        banks = [0, 1, 2, 3, 4]
        bi = 0
        pend = None
        for c in range(8):
            wsl = get_w(c)
            for tb in range(3):
                b = banks[bi % 5]
                bi += 1
                for k in range(nk):
                    mm(PS(b), wsl(k), rhs_fn(k, tb), k == 0, k == nk - 1)
                if pend is not None:
                    mm(PS(5 + pend[0]), Rones, pend[1], pend[2] == 0, pend[2] == 7)
                Ry = R(ybuf.ap[:, c, tb * 512:(tb + 1) * 512], ybuf.k(c * NT + tb * 512, c * NT + (tb + 1) * 512))
                cp("act", Ry, PS(b))
                sq = nsq()
                act(sq, PS(b), AF.Square)
                pend = (tb, sq, c)
        mm(PS(5 + pend[0]), Rones, pend[1], pend[2] == 0, pend[2] == 7)

    def ffn(l, par, sub, bg=None):
        A.top = phase_base
        actb = A.alloc([22, NT], BF16)
        silb = [A.alloc([512], BF16) for _ in range(2)]
        prenorm(par, sub)
        w1v = wview(ffn_w1[l, sub // 2])
        w2v = wview(ffn_w2[l, sub // 2])
        pi = 0
        for j in range(11):
            s = wslot()
            dsta = s.ap[:, 0:2048].rearrange("p (k n) -> p k n", k=8)
            dstu = s.ap[:, 2048:4096].rearrange("p (k n) -> p k n", k=8)
            dma_in("pool", R(dsta, s.k()), w1v[:, :, j * 256:(j + 1) * 256])
            dma_in("pool", R(dstu, s.k()), w1v[:, :, DFF + j * 256:DFF + (j + 1) * 256], group=True)
            for half in range(2):
                m = 2 * j + half
                for tb in range(3):
                    ba = (pi % 3) * 2
                    pi += 1
                    for k in range(8):
                        mm(PS(ba), R(dsta[:, k, half * 128:(half + 1) * 128], s.k()), H(k, tb), k == 0, k == 7)
                    for k in range(8):
                        mm(PS(ba + 1), R(dstu[:, k, half * 128:(half + 1) * 128], s.k()), H(k, tb), k == 0, k == 7)
                    sb_ = silb[pi % 2]
                    Rs = R(sb_.ap, sb_.k())
                    act(Rs, PS(ba), AF.Silu)
                    Ra = R(actb.ap[:, m, tb * 512:(tb + 1) * 512], actb.k(m * NT + tb * 512, m * NT + (tb + 1) * 512))
                    tt("dve", Ra, PS(ba + 1), Rs, ALU.mult)
            run_bg(bg, 1)
        ybuf = hT

        def get_w(c):
            sl = wload(w2v[:, :, c * 128:(c + 1) * 128], 22, 128)
            return lambda k: R(sl[0][:, k, :], *sl[1])

        def rhs_fn(k, tb):
            return R(actb.ap[:, k, tb * 512:(tb + 1) * 512], actb.k(k * NT + tb * 512, k * NT + (tb + 1) * 512))

        outproj(get_w, 22, rhs_fn, ybuf, bg)
        post_update(par, sub, ybuf, [5, 6, 7])

    def mixer(l, par, bg=None):
        sub = 1
        A.top = phase_base
        Obr = []
        otop = []
        for _ in range(4):
            Obr.append(A.alloc([4, NT], BF16))
            otop.append(A.top)
        mix_base = A.top
        A.top = otop[0]
        prenorm(par, sub)
        wv = wview(w_in[l])
        wr = [0]

        def O(br, c, lo, hi, rows=slice(0, 128)):
            return R(Obr[br].ap[rows, c, lo:hi], Obr[br].k(c * NT, (c + 1) * NT))

        def proj_fm(col0, ncols, handler, banks, tbs=(0, 1, 2)):
            sl = wload(wv[:, :, col0:col0 + ncols], 8, ncols)
            for ci in range(ncols // 128):
                for tb in tbs:
                    b = banks[wr[0] % len(banks)]
                    wr[0] += 1
                    for k in range(8):
                        mm(PS(b), R(sl[0][:, k, ci * 128:(ci + 1) * 128], *sl[1]), H(k, tb), k == 0, k == 7)
                    handler(ci, tb, b)

        S.mark("L%d   na_proj" % l)
        qT = A.alloc([4, NT], BF16)
        kT = A.alloc([4, NT], BF16)
        ckT = A.alloc([4, 512], BF16)
        Vb = A.alloc([12, 512], BF16)
        cVb = A.alloc([4, 512], BF16)
        kst = [A.alloc([512], F32) for _ in range(2)]
        ost = [A.alloc([512], F32) for _ in range(2)]
        Ptl = [A.alloc([512], BF16) for _ in range(6)]
        rden = [A.alloc([512], F32) for _ in range(2)]
        Bq = A.alloc([17, 64], F32)
        Etab = A.alloc([24, 64], BF16)
        na_top = A.top
        cnt = {"k": 0, "o": 0, "p": 0, "r": 0, "s": 0}

        def rot(lst, key):
            cnt[key] += 1
            t = lst[cnt[key] % len(lst)]
            return t

        def QT(t, c, lo, hi, half=None):
            rows = slice(0, 128) if half is None else slice(half * 64, half * 64 + 64)
            return R(t.ap[rows, c, lo:hi], t.k(c * t.shape[1], (c + 1) * t.shape[1]))

        def h_q(ci, tb, b):
            cp("act" if (ci + tb) % 2 else "dve", QT(qT, ci, tb * 512, tb * 512 + 512), PS(b))

        if "pq" in mx:
            proj_fm(0, 512, h_q, [0, 1, 2, 3])

        def h_k(ci, tb, b):
            if tb != 0:
                cp("dve", QT(kT, ci, tb * 512, tb * 512 + 512), PS(b))
            if tb == 0:
                st = rot(kst, "k")
                Rst = R(st.ap, st.k())
                cp("act", Rst, PS(b))
                cp("dve", QT(kT, ci, tb * 512, tb * 512 + 512), Rst)
                for q in range(4):
                    tr(PS(4 + (cnt["k"] % 2), slice(0, 128), q * 128, q * 128 + 128),
                       R(st.ap[:, q * 128:(q + 1) * 128], st.k()), Ri)
                os_ = rot(ost, "o")
                Ros = R(os_.ap, os_.k())
                cp("act", Ros, PS(4 + (cnt["k"] % 2)))
                dma_out("sp", o_nak[l].rearrange("(q p) f -> p q f", p=128)[:, :, ci * 128:(ci + 1) * 128],
                        R(os_.ap.rearrange("p (q f) -> p q f", q=4), os_.k()))

        if "pk" in mx:
            proj_fm(512, 512, h_k, [0, 1, 2, 3])

        if "pv" in mx:
            slv = wload(wv[:, :, 1024:1536], 8, 512)
            for t128 in range(12):
                b = [0, 1, 2, 3][t128 % 4]
                for k in range(8):
                    mm(PS(b), Hc(k, t128 * 128, (t128 + 1) * 128), R(slv[0][:, k, :], *slv[1]), k == 0, k == 7)
                cp("act", R(Vb.ap[:, t128, :], Vb.k(t128 * 512, (t128 + 1) * 512)), PS(b))
                if t128 < 4 and "pvo" in mx:
                    os_ = rot(ost, "o")
                    Ros = R(os_.ap, os_.k())
                    cp("act", Ros, PS(b))
                    dma_out("sp", o_nav[l, t128 * 128:(t128 + 1) * 128, :], Ros)

        if "pc" in mx:
            for q in range(4):
                st = rot(kst, "k")
                Rst = R(st.ap, st.k())
                dma_in("sp", Rst, ck_na[l, q * 128:(q + 1) * 128, :])
                for c in range(4):
                    tr(PS(4 + (q % 2), slice(0, 128), c * 128, c * 128 + 128), R(st.ap[:, c * 128:(c + 1) * 128], st.k()), Ri)
                cp("dve", R(ckT.ap[:, :, q * 128:(q + 1) * 128], ckT.k()),
                   R(ps_t[:, 4 + (q % 2), :].rearrange("p (c t) -> p c t", c=4), PS(4 + (q % 2))[1][0]))
            dma_in("pool", R(cVb.ap, cVb.k()), cv_na[l].rearrange("(q p) f -> p q f", p=128))

        S.mark("L%d   na_ctx" % l)
        pend_norm = [None]

        def flush_norm():
            if pend_norm[0] is not None:
                pend_norm[0]()
                pend_norm[0] = None

        def attend(q_fn, keytiles, ncols, out_fn, half, bset):
            spool = bset["s"]
            ob, db = bset["o"], bset["d"]
            n = len(keytiles)
            LA = 3
            Pl = [None] * n
            for i in range(n + LA):
                if i < n:
                    kl, vl, (lo, hi), tab = keytiles[i]
                    sbk = spool[cnt["s"] % len(spool)]
                    cnt["s"] += 1
                    mm(PS(sbk, slice(0, 128), 0, hi - lo), kl, q_fn(lo, hi), True, True)
                    pt = rot(Ptl, "p")
                    Rp = R(pt.ap[:, 0:hi - lo], pt.k())
                    act(Rp, PS(sbk, slice(0, 128), 0, hi - lo), AF.Exp, scale=0.125)
                    if tab is not None:
                        Rp3 = R(pt.ap[:, 0:hi - lo].rearrange("p (e q) -> p e q", q=64), pt.k())
                        tt("dve", Rp3, Rp3, tab, ALU.mult)
                    Pl[i] = Rp
                if i == min(LA, n) - 1:
                    flush_norm()
                j = i - LA
                if j >= 0:
                    kl, vl, (lo, hi), tab = keytiles[j]
                    mm(PS(ob, slice(0, 128), lo, hi), vl, Pl[j], j == 0, j == n - 1, skip_group_check=True)
                    mm(PS(db, slice(0, 128), lo, hi), Rones, Pl[j], j == 0, j == n - 1, skip_group_check=True)

            def norm():
                rd = rot(rden, "r")
                rows = slice(half * 64, half * 64 + 64)
                Rr = R(rd.ap[rows, 0:ncols], rd.k())
                act(Rr, PS(db, rows, 0, ncols), AF.Ln)
                act(Rr, Rr, AF.Exp, scale=-1.0)
                tt("dve", out_fn(rows), PS(ob, rows, 0, ncols), Rr, ALU.mult)
            pend_norm[0] = norm

        bsets = [{"s": [0, 1, 2, 3], "o": 4, "d": 5}, {"s": [0, 1, 2, 3], "o": 6, "d": 7}]
        hcount = [0]

        def ctx_attention(qt, kt, vfn, br):
            for s in range(2):
                for h in range(8):
                    c, half = (h // 2, h % 2) if br == 0 else (h % 4, h // 4)
                    rows = slice(half * 64, half * 64 + 64)
                    tiles = []
                    for kb in range(2):
                        t0 = s * 256 + kb * 128
                        kl = R(kt[0][rows, kt[1](c), t0:t0 + 128], kt[2])
                        tiles.append((kl, vfn(s * 2 + kb, c), (0, 256), None))
                    bs = bsets[hcount[0] % 2]
                    hcount[0] += 1
                    attend(lambda lo, hi, c=c, rows=rows, s=s: R(qt.ap[rows, c, s * 256 + lo:s * 256 + hi], qt.k(c * NT, (c + 1) * NT)),
                           tiles, 256, lambda rws, c=c, s=s: O(br, c, s * 256, s * 256 + 256, rws), half, bs)

        if "nactx" in mx:
            ctx_attention(qT, (kT.ap, lambda c: c, kT.k()),
                          lambda t128, c: R(Vb.ap[:, t128, c * 128:(c + 1) * 128], Vb.k(t128 * 512, (t128 + 1) * 512)), 0)
        flush_norm()

        S.mark("L%d   na_lat" % l)
        if "nalat" in mx:
            def r0(r):
                return min(max(r - 4, 0), 8)

            def inwin(r, kr):
                return r0(r) <= kr <= r0(r) + 7

            Ek = Etab.k()
            RBq15 = R(Bq.ap[0:64, 1:16, :], Bq.k())

            def et_dma(h):
                if h == 0 and l == 0:
                    memset("dve", R(Bq.ap[0:64, :, :], Bq.k()), 0.0)
                src = bass.AP(rpb.tensor, rpb[l, h, 0, 0].offset, [[1, 64], [127, 15], [1, 64]])
                dma_in("sp", RBq15, src)

            def et_exp(h):
                act(RBq15, RBq15, AF.Exp)
                tt("dve", RBq15, RBq15, R(cmask.ap[0:64, :].unsqueeze(1).broadcast_to([64, 15, 64]), cmask.k()), ALU.mult)

            def et_tr(h):
                eb = (h % 2) * 2
                dl = list(range(7, -9, -1))
                for i, d in enumerate(dl):
                    tr(PS(eb + i // 8, slice(0, 128), (i % 8) * 64, (i % 8) * 64 + 64),
                       R(Bq.ap[0:64, d + 8:d + 10, :], Bq.k()), R(ident.ap[0:64, 0:64], ident.k()))
                cp("act", R(Etab.ap[:, 0:8, :], Ek), R(ps_t[:, eb, :].rearrange("p (e q) -> p e q", e=8)[:, :, ::-1], PS(eb)[1][0]))
                cp("act", R(Etab.ap[:, 8:12, :], Ek), R(ps_t[:, eb + 1, 0:256].rearrange("p (e q) -> p e q", e=4)[:, :, ::-1], PS(eb + 1)[1][0]))
                cp("act", R(Etab.ap[:, 12, :], Ek), R(ps_t[:, eb + 1, 256:320][:, ::-1], PS(eb + 1)[1][0]))
                memset("dve", R(Etab.ap[0:64, 12, :], Ek), 0.0)
                cp("act", R(Etab.ap[:, 13, :], Ek), R(ps_t[:, eb, 256:320][:, ::-1], PS(eb)[1][0]))
                memset("dve", R(Etab.ap[64:128, 13, :], Ek), 0.0)
                cp("act", R(Etab.ap[:, 14:17, :], Ek), R(ps_t[:, eb, 320:512].rearrange("p (e q) -> p e q", e=3)[:, :, ::-1], PS(eb)[1][0]))
                cp("act", R(Etab.ap[:, 17:24, :], Ek), R(ps_t[:, eb + 1, 0:448].rearrange("p (e q) -> p e q", e=7)[:, :, ::-1], PS(eb + 1)[1][0]))

            def tab_for(j, ra, rb):
                if j <= 3:
                    i0 = 7 - (2 * j - ra)
                    i1 = 7 - (2 * j - rb)
                    return R(Etab.ap[:, i0:i1 + 1, :], Ek)
                i0 = 13 + (3 - (2 * j - ra))
                i1 = 13 + (3 - (2 * j - rb))
                return R(Etab.ap[:, i0:i1 + 1, :], Ek)

            def na_piece(h, piece):
                c, half = h // 2, h % 2
                rows = slice(half * 64, half * 64 + 64)
                pr0, pr1 = piece * 8, piece * 8 + 7
                tiles = []
                for kb in range(4):
                    kl = R(ckT.ap[rows, c, kb * 128:(kb + 1) * 128], ckT.k())
                    vl = R(cVb.ap[:, kb, c * 128:(c + 1) * 128], cVb.k())
                    tiles.append((kl, vl, (0, 512), None))
                for j in range(8):
                    rr = [r for r in range(16) if inwin(r, 2 * j) or inwin(r, 2 * j + 1)]
                    ra, rb = max(rr[0], pr0), min(rr[-1], pr1)
                    if ra > rb:
                        continue
                    t0 = 512 + j * 128
                    kl = R(kT.ap[rows, c, t0:t0 + 128], kT.k(c * NT, (c + 1) * NT))
                    vl = R(Vb.ap[:, 4 + j, c * 128:(c + 1) * 128], Vb.k((4 + j) * 512, (5 + j) * 512))
                    tiles.append((kl, vl, ((ra - pr0) * 64, (rb - pr0 + 1) * 64), tab_for(j, ra, rb)))
                q0 = 512 + piece * 512
                bs = bsets[hcount[0] % 2]
                hcount[0] += 1
                attend(lambda lo, hi, c=c, rows=rows, q0=q0: R(qT.ap[rows, c, q0 + lo:q0 + hi], qT.k(c * NT, (c + 1) * NT)),
                       tiles, 512, lambda rws, c=c, q0=q0: O(0, c, q0, q0 + 512, rws), half, bs)

            et_dma(0)
            et_exp(0)
            et_tr(0)
            for h in range(8):
                if h < 7:
                    et_dma(h + 1)
                na_piece(h, 0)
                if h < 7:
                    et_exp(h + 1)
                na_piece(h, 1)
                if h < 7:
                    et_tr(h + 1)
            flush_norm()

        S.mark("L%d   conv" % l)
        if "conv" in mx:
            A.top = otop[1]
            SO = [0, 286, 572]
            ub = A.alloc([4, 1626], BF16)
            cv = A.alloc([4, NT], F32)
            Dg = A.alloc([31, 128], BF16)
            sgb = [A.alloc([512], F32) for _ in range(2)]
            memset("dve", R(ub.ap, ub.k()), 0.0)

            def useg(tb):
                if tb == 0:
                    return [(SO[0] + 15, 0, 256), (SO[1] + 15, 256, 256)]
                return [(SO[2] + 15 + (tb - 1) * 512, tb * 512, 512)]

            sla = wload(wv[:, :, 1536:2048], 8, 512)
            slg = wload(wv[:, :, 2048:2560], 8, 512)
            gi = 0
            for ci in range(4):
                for tb in range(3):
                    ba = (gi % 2) * 2
                    gi += 1
                    for k in range(8):
                        mm(PS(ba), R(sla[0][:, k, ci * 128:(ci + 1) * 128], *sla[1]), H(k, tb), k == 0, k == 7)
                    for k in range(8):
                        mm(PS(ba + 1), R(slg[0][:, k, ci * 128:(ci + 1) * 128], *slg[1]), H(k, tb), k == 0, k == 7)
                    sg = sgb[gi % 2]
                    Rsg = R(sg.ap, sg.k())
                    act(Rsg, PS(ba + 1), AF.Sigmoid)
                    for (u0, t0, n) in useg(tb):
                        o = t0 - tb * 512
                        tt("dve", R(ub.ap[:, ci, u0:u0 + n], ub.k(ci * 1626, (ci + 1) * 1626)),
                           PS(ba, slice(0, 128), o, o + n), R(sg.ap[:, o:o + n], sg.k()), ALU.mult)
            pieces = [(0, SO[0], 0, 256), (0, SO[1], 256, 256), (1, SO[2], 512, 512), (2, SO[2] + 512, 1024, 512)]
            mean_sb = A.alloc([3, 512], F32)
            for ci in range(4):
                for k in range(31):
                    ts("dve", R(Dg.ap[:, k, :], Dg.k()), Rib, col(l, C_DW + ci * 31 + k), None, ALU.mult)
                for pi_, (tb, u0, t0, n) in enumerate(pieces):
                    b = 4 + (pi_ % 2)
                    for k in range(31):
                        mm(PS(b, slice(0, 128), 0, n), R(Dg.ap[:, k, :], Dg.k()),
                           R(ub.ap[:, ci, u0 + k:u0 + k + n], ub.k(ci * 1626, (ci + 1) * 1626)), k == 0, k == 30)
                    cp("act", R(cv.ap[:, ci, t0:t0 + n], cv.k(ci * NT, (ci + 1) * NT)), PS(b, slice(0, 128), 0, n))
            lnb = [(6, 7), (4, 5), (2, 3)]
            for tb in range(3):
                for ci in range(4):
                    Rcv = R(cv.ap[:, ci, tb * 512:(tb + 1) * 512], cv.k(ci * NT, (ci + 1) * NT))
                    s1 = nsq()
                    cp("act", s1, Rcv)
                    mm(PS(lnb[tb][0]), Rones512, s1, ci == 0, ci == 3)
                    s2 = nsq()
                    act(s2, Rcv, AF.Square)
                    mm(PS(lnb[tb][1]), Rones512, s2, ci == 0, ci == 3)
            lnst = []
            for tb in range(3):
                Rmean = R(mean_sb.ap[:, tb, :], mean_sb.k(tb * 512, (tb + 1) * 512))
                cp("act", Rmean, PS(lnb[tb][0]))
                t = ntf()
                tt("dve", t, Rmean, Rmean, ALU.mult)
                tt("dve", t, PS(lnb[tb][1]), t, ALU.subtract)
                lnst.append((Rmean, rstd_from(t, 1.0)))
            for tb in range(3):
                Rmean, Rrstd = lnst[tb]
                for ci in range(4):
                    Rcv = R(cv.ap[:, ci, tb * 512:(tb + 1) * 512], cv.k(ci * NT, (ci + 1) * NT))
                    t = ntf()
                    tt("dve", t, Rcv, Rmean, ALU.subtract)
                    tt("dve", t, t, Rrstd, ALU.mult)
                    act(O(1, ci, tb * 512, (tb + 1) * 512), t, AF.Silu, bias=col(l, C_LNB + ci), scale=col(l, C_LNG + ci))
            run_bg(bg, 2)

        S.mark("L%d   gqa" % l)
        if "gqa" in mx:
            A.top = otop[2]
            gqT = A.alloc([4, NT], BF16)
            ropeC = A.alloc([1024], F32)
            ropeS = A.alloc([1024], F32)
            dma_in("sp", R(ropeC.ap, ropeC.k()), ropeC_d)
            dma_in("sp", R(ropeS.ap, ropeS.k()), ropeS_d)
            gkT = A.alloc([1, NT + 512], BF16)
            gV = A.alloc([16, 128], BF16)
            raw = [A.alloc([512], F32) for _ in range(2)]
            xnb = [A.alloc([512], BF16) for _ in range(2)]
            kst2 = [A.alloc([512], F32) for _ in range(1)]
            ost2 = [A.alloc([512], F32) for _ in range(1)]
            Ptl2 = [A.alloc([512], BF16) for _ in range(6)]
            rden2 = [A.alloc([512], F32) for _ in range(2)]
            Ptl[:] = Ptl2
            rden[:] = rden2
            rc = [0]

            def normrope(dst_fn, gcol, tb, b, want_f32=None):
                rc[0] += 1
                rw = raw[rc[0] % 2]
                Rraw = R(rw.ap, rw.k())
                cp("act", Rraw, PS(b))
                sq = nsq()
                act(sq, Rraw, AF.Square)
                sbank = 6 if rc[0] % 2 else 4
                pbank = 7 if rc[0] % 2 else 5
                mm(PS(sbank), Rbones, sq, True, True)
                RrsB = rstd_from(PS(sbank), 1.0 / 64)
                if tb == 0:
                    if want_f32 is not None:
                        stt(want_f32, Rraw, gcol, RrsB, ALU.mult, ALU.mult)
                        cp("act", dst_fn(), want_f32)
                    else:
                        stt(dst_fn(), Rraw, gcol, RrsB, ALU.mult, ALU.mult)
                    return
                stt(Rraw, Rraw, gcol, RrsB, ALU.mult, ALU.mult)
                xb_ = xnb[rc[0] % 2]
                Rxb = R(xb_.ap, xb_.k())
                cp("act", Rxb, Rraw)
                mm(PS(pbank), R(permM.ap, permM.k()), Rxb, True, True)
                t0 = (tb - 1) * 512
                t = ntf()
                tt("dve", t, PS(pbank), R(ropeS.ap[:, t0:t0 + 512], ropeS.k()), ALU.mult)
                tt("dve", Rraw, Rraw, R(ropeC.ap[:, t0:t0 + 512], ropeC.k()), ALU.mult)
                tt("dve", dst_fn(), Rraw, t, ALU.add)

            def h_gq(ci, tb, b):
                normrope(lambda: QT(gqT, ci, tb * 512, tb * 512 + 512), col(l, C_QN), tb, b)

            proj_fm(2560, 512, h_gq, [0, 1, 2])
            slkv = wload(wv[:, :, 3072:3328], 8, 256)
            for tb in range(3):
                b = tb % 3
                for k in range(8):
                    mm(PS(b), R(slkv[0][:, k, 0:128], *slkv[1]), H(k, tb), k == 0, k == 7)
                Rk = R(gkT.ap[:, 0, tb * 512:(tb + 1) * 512], gkT.k())
                if tb == 0:
                    st = rot(kst2, "k")
                    Rst = R(st.ap, st.k())
                    normrope(lambda: Rk, col(l, C_KN), tb, b, want_f32=Rst)
                    for q in range(4):
                        tr(PS(3, slice(0, 128), q * 128, q * 128 + 128), R(st.ap[:, q * 128:(q + 1) * 128], st.k()), Ri)
                    os_ = rot(ost2, "o")
                    Ros = R(os_.ap, os_.k())
                    cp("act", Ros, PS(3))
                    dma_out("sp", o_gqk[l].rearrange("(q p) f -> p q f", p=128),
                            R(os_.ap.rearrange("p (q f) -> p q f", q=4), os_.k()))
                else:
                    normrope(lambda: Rk, col(l, C_KN), tb, b)
            for t128 in range(12):
                b = t128 % 4
                for k in range(8):
                    mm(PS(b, slice(0, 128), 0, 128), Hc(k, t128 * 128, (t128 + 1) * 128), R(slkv[0][:, k, 128:256], *slkv[1]), k == 0, k == 7)
                cp("act", R(gV.ap[:, t128, :], gV.k()), PS(b, slice(0, 128), 0, 128))
                if t128 < 4:
                    os_ = rot(ost2, "o")
                    Ros = R(os_.ap[:, 0:128], os_.k())
                    cp("act", Ros, PS(b, slice(0, 128), 0, 128))
                    dma_out("sp", o_gqv[l, t128 * 128:(t128 + 1) * 128, :], Ros)
            st = rot(kst2, "k")
            Rst = R(st.ap.rearrange("p (q f) -> p q f", q=4), st.k())
            dma_in("sp", Rst, ck_gq[l].rearrange("(q p) f -> p q f", p=128))
            for q in range(4):
                tr(PS(3, slice(0, 128), q * 128, q * 128 + 128), R(st.ap[:, q * 128:(q + 1) * 128], st.k()), Ri)
            cp("dve", R(gkT.ap[:, 0, NT:NT + 512], gkT.k()), PS(3))
            dma_in("pool", R(gV.ap[:, 12:16, :], gV.k()), cv_gq[l].rearrange("(q p) f -> p q f", p=128))

            RgkT = gkT.k()
            ctx_attention(gqT, (gkT.ap, lambda c: 0, RgkT),
                          lambda t128, c: R(gV.ap[:, t128, :], gV.k()), 2)
            for h in range(8):
                c, half = h % 4, h // 4
                rows = slice(half * 64, half * 64 + 64)
                for piece in range(2):
                    tiles = []
                    for kb in range(12):
                        t0 = 512 + kb * 128 if kb < 8 else NT + (kb - 8) * 128
                        vi = 4 + kb if kb < 8 else 12 + (kb - 8)
                        tiles.append((R(gkT.ap[rows, 0, t0:t0 + 128], RgkT), R(gV.ap[:, vi, :], gV.k()), (0, 512), None))
                    q0 = 512 + piece * 512
                    bs = bsets[hcount[0] % 2]
                    hcount[0] += 1
                    attend(lambda lo, hi, c=c, rows=rows, q0=q0: R(gqT.ap[rows, c, q0 + lo:q0 + hi], gqT.k(c * NT, (c + 1) * NT)),
                           tiles, 512, lambda rws, c=c, q0=q0: O(2, c, q0, q0 + 512, rws), half, bs)
            flush_norm()
            run_bg(bg, 2)

        S.mark("L%d   lru" % l)
        if "lru" in mx:
            A.top = otop[3]
            LO = [0, 259, 518]
            LW = 1545
            NJ = LW - 3
            xc = A.alloc([LW], F32)
            xcb = A.alloc([LW], BF16)
            bd = A.alloc([4, 128], BF16)
            ga = A.alloc([LW], F32)
            gb = A.alloc([LW], F32)
            g3 = A.alloc([LW], F32)
            g4 = A.alloc([LW], F32)
            fin = A.alloc([16], F32)
            lam8 = A.alloc([8], F32)
            Rlam = R(lam8.ap, lam8.k())
            act(Rlam, R(colp.ap[:, l, C_LAM:C_LAM + 8], colp.k()), AF.Exp, scale=-1.0)
            ts("dve", Rlam, Rlam, 1.0, None, ALU.add)
            act(Rlam, Rlam, AF.Ln)
            ts("dve", Rlam, Rlam, -8.0, None, ALU.mult)

            def lseg(tb):
                if tb == 0:
                    return [(LO[0] + 2, 0, 256), (LO[1] + 2, 256, 256)]
                return [(LO[2] + 2 + (tb - 1) * 512, tb * 512, 512)]

            gpieces = [(LO[0], 256), (LO[1], 256), (LO[2], 512), (LO[2] + 512, 512)]
            sll = wload(wv[:, :, 3328:3840], 8, 512)

            def lru_unit(ci, u):
                rlo, rhi = (0, 518) if u == 0 else (518, LW)
                jlo, jhi = (0, 515) if u == 0 else (518, NJ)
                segs = [(0, LO[0], 0, 256), (1, LO[1], 256, 256)] if u == 0 else [(2, LO[2], 512, 1024)]
                gps = gpieces[0:2] if u == 0 else gpieces[2:4]
                tbs = (0,) if u == 0 else (1, 2)
                bk = 4 if u == 0 else 6

                def K(t):
                    return t.k(rlo, rhi)

                memset("dve", R(g4.ap[:, rlo:rhi], K(g4)), 0.0)
                for tb in tbs:
                    for k in range(8):
                        mm(PS(tb), R(sll[0][:, k, ci * 128:(ci + 1) * 128], *sll[1]), H(k, tb), k == 0, k == 7)
                    for (p0, t0, n) in lseg(tb):
                        o = t0 - tb * 512
                        cp("act", R(g4.ap[:, p0:p0 + n], K(g4)), PS(tb, slice(0, 128), o, o + n))
                yield
                Rxc = R(xc.ap[:, jlo:jhi], K(xc))
                ts("dve", Rxc, R(g4.ap[:, jlo:jhi], K(g4)), col(l, C_LCW + 0 * 4 + ci), col(l, C_LCB + ci), ALU.mult, ALU.add)
                for k in range(1, 4):
                    stt(Rxc, R(g4.ap[:, jlo + k:jhi + k], K(g4)), col(l, C_LCW + k * 4 + ci), Rxc, ALU.mult, ALU.add)
                cp("act", R(xcb.ap[:, jlo:jhi], K(xcb)), Rxc)
                yield
                for dr in range(2):
                    gi_ = g3 if dr == 0 else g4
                    for (g0, n) in gps:
                        mm(PS(bk, slice(0, 128), 0, n), R(bd.ap[:, dr * 2, :], bd.k()), R(xcb.ap[:, g0:g0 + n], K(xcb)), True, True)
                        mm(PS(bk + 1, slice(0, 128), 0, n), R(bd.ap[:, dr * 2 + 1, :], bd.k()), R(xcb.ap[:, g0:g0 + n], K(xcb)), True, True)
                        act(R(ga.ap[:, g0:g0 + n], K(ga)), PS(bk, slice(0, 128), 0, n), AF.Sigmoid, bias=col(l, C_BR + dr * 4 + ci))
                        act(R(gi_.ap[:, g0:g0 + n], K(gi_)), PS(bk + 1, slice(0, 128), 0, n), AF.Sigmoid, bias=col(l, C_BI + dr * 4 + ci))
                    yield
                    for (si, g0, s0, n) in segs:
                        Ra_ = R(ga.ap[:, g0:g0 + n], K(ga))
                        act(Ra_, Ra_, AF.Exp, scale=R(lam8.ap[:, dr * 4 + ci:dr * 4 + ci + 1], lam8.k()))
                    yield
                    for (si, g0, s0, n) in segs:
                        Ra_ = R(ga.ap[:, g0:g0 + n], K(ga))
                        Rb_ = R(gb.ap[:, g0:g0 + n], K(gb))
                        tt("dve", Rb_, Ra_, Ra_, ALU.mult)
                    for (si, g0, s0, n) in segs:
                        Rb_ = R(gb.ap[:, g0:g0 + n], K(gb))
                        act(Rb_, Rb_, AF.Sqrt, bias=R(ones_f.ap, ones_f.k()), scale=-1.0)
                    yield
                    for (si, g0, s0, n) in segs:
                        Rb_ = R(gb.ap[:, g0:g0 + n], K(gb))
                        Ri_ = R(gi_.ap[:, g0:g0 + n], K(gi_))
                        tt("dve", Ri_, Ri_, R(xc.ap[:, g0:g0 + n], K(xc)), ALU.mult)
                        tt("dve", Rb_, Rb_, Ri_, ALU.mult)
                    yield
                    hd = gi_
                    for (si, g0, s0, sn) in segs:
                        init = 0.0 if si < 2 else col(l, C_H0 + dr * 4 + ci)
                        if dr == 0:
                            scan(R(hd.ap[:, g0:g0 + sn], K(hd)), R(ga.ap[:, g0:g0 + sn], K(ga)), R(gb.ap[:, g0:g0 + sn], K(gb)), init)
                        else:
                            scan(R(hd.ap[:, g0:g0 + sn][:, ::-1], K(hd)), R(ga.ap[:, g0:g0 + sn][:, ::-1], K(ga)),
                                 R(gb.ap[:, g0:g0 + sn][:, ::-1], K(gb)), init)
                        if si < 2:
                            pos = g0 + sn - 1 if dr == 0 else g0
                            cp("pool", R(fin.ap[:, (si * 2 + dr) * 4 + ci:(si * 2 + dr) * 4 + ci + 1], fin.k()), R(hd.ap[:, pos:pos + 1], K(hd)))
                    yield
                for (si, g0, s0, sn) in segs:
                    tt("dve", O(3, ci, s0, s0 + sn), R(g3.ap[:, g0:g0 + sn], K(g3)), R(g4.ap[:, g0:g0 + sn], K(g4)), ALU.add)

            for ci in range(4):
                for dr_ in range(2):
                    for gt_ in range(2):
                        dma_in("pool", R(bd.ap[:, dr_ * 2 + gt_, :], bd.k()), lru_bd[l, :, dr_ * 8 + gt_ * 4 + ci, :],
                               group=(dr_ + gt_ > 0))
                gens = [lru_unit(ci, 1), lru_unit(ci, 0)]
                while gens:
                    for g_ in list(gens):
                        try:
                            next(g_)
                        except StopIteration:
                            gens.remove(g_)
            dma_out("sp", o_lru[l], R(fin.ap, fin.k()))
            run_bg(bg, 2)

        S.mark("L%d   merge" % l)
        if "merge" in mx:
            A.top = otop[3]
            mixT = A.alloc([8, NT], BF16)
            sgm = [A.alloc([512], BF16) for _ in range(2)]
            accb = [A.alloc([512], F32) for _ in range(2)]
            mi = 0
            for c in range(8):
                swb = wslot()
                wbd = swb.ap[:, 0:2048].rearrange("p (k kc n) -> p k kc n", k=4, kc=4)
                for k in range(4):
                    dma_in("pool", R(wbd[:, k, :, :], swb.k()), wview(w_br[l, k])[:, :, c * 128:(c + 1) * 128], group=k > 0)
                slg_ = wload(wv[:, :, 3840 + c * 512:3840 + (c + 1) * 512], 8, 512)
                for tb in range(3):
                    acc = accb[(c * 3 + tb) % 2]
                    Racc = R(acc.ap, acc.k())
                    for k in range(4):
                        bp = (mi % 3) * 2
                        mi += 1
                        for kc in range(4):
                            mm(PS(bp), R(wbd[:, k, kc, :], swb.k()),
                               R(Obr[k].ap[:, kc, tb * 512:(tb + 1) * 512], Obr[k].k(kc * NT, (kc + 1) * NT)), kc == 0, kc == 3)
                        for kk in range(8):
                            mm(PS(bp + 1), R(slg_[0][:, kk, k * 128:(k + 1) * 128], *slg_[1]), H(kk, tb), kk == 0, kk == 7)
                        sg = sgm[mi % 2]
                        Rsg = R(sg.ap, sg.k())
                        act(Rsg, PS(bp + 1), AF.Sigmoid)
                        if k == 0:
                            tt("dve", Racc, PS(bp), Rsg, ALU.mult)
                        else:
                            t = ntf()
                            tt("dve", t, PS(bp), Rsg, ALU.mult)
                            if k < 3:
                                tt("dve", Racc, Racc, t, ALU.add)
                            else:
                                tt("dve", R(mixT.ap[:, c, tb * 512:(tb + 1) * 512], mixT.k(c * NT + tb * 512, c * NT + (tb + 1) * 512)),
                                   Racc, t, ALU.add)
                run_bg(bg, 1)
            wov = wview(w_out[l])
            wos = [None, None]

            def get_wo(c):
                if c % 4 == 0:
                    wos[0] = wload(wov[:, :, (c // 4) * 512:(c // 4 + 1) * 512], 8, 512)
                sl = wos[0]
                cc_ = c % 4
                return lambda k: R(sl[0][:, k, cc_ * 128:(cc_ + 1) * 128], *sl[1])

            def rhs_mix(k, tb):
                return R(mixT.ap[:, k, tb * 512:(tb + 1) * 512], mixT.k(k * NT + tb * 512, k * NT + (tb + 1) * 512))

            outproj(get_wo, 8, rhs_mix, hT, bg)
            post_update(par, sub, hT, [5, 6, 7])

    memset("dve", R(ones_f.ap, ones_f.k()), 1.0)

    if adaln_on:
        g0 = adaln(0, 0)
        for _ in g0:
            pass
    for l in range(depth):
        par = l % 2
        bg = adaln(l + 1, 1 - par) if (l + 1 < depth and adaln_on) else None
        S.mark("L%d ffn0" % l)
        if "ffn0" in do:
            ffn(l, par, 0, bg)
        S.mark("L%d mixer" % l)
        if "mixer" in do:
            mixer(l, par, bg)
        S.mark("L%d ffn2" % l)
        if "ffn2" in do:
            ffn(l, par, 2, bg)
        if bg is not None:
            for _ in bg:
                pass

    S.mark("final")
    A.top = phase_base
    yst = [A.alloc([D], F32) for _ in range(2)]
    for t128 in range(12):
        tb, q = t128 // 4, t128 % 4
        st = yst[t128 % 2]
        Rst = R(st.ap, st.k())
        bb = (t128 % 2) * 2
        for c in range(8):
            tr(PS(bb + c // 4, slice(0, 128), (c % 4) * 128, (c % 4) * 128 + 128),
               R(xT.ap[:, c, t128 * 128:(t128 + 1) * 128], xT.k(c * NT + tb * 512, c * NT + (tb + 1) * 512)), Ri)
        cp("act", R(st.ap[:, 0:512], st.k()), PS(bb))
        cp("dve", R(st.ap[:, 512:1024], st.k()), PS(bb + 1))
        dma_out("sp", y_out[t128 * 128:(t128 + 1) * 128, :], Rst)

    S.finish()
    S.emit(nc, es)
    es.close()
    nc._marks = S.marks
    nc._ndsem = len(S.dsem)
    nc._counts = {n: e.count for n, e in S.E.items()}
    return nc


_PROG = {}


def _consts():
    GRID_W = 64
    t = np.arange(1024)
    row = (t // GRID_W).astype(np.float32)
    colv = (t % GRID_W).astype(np.float32)
    half = 32
    freqs = (np.float32(10000.0) ** (-np.arange(0, half, 2, dtype=np.float32) / np.float32(half))).astype(np.float32)
    ar = row[:, None] * freqs
    ac = colv[:, None] * freqs
    C = np.zeros((128, 1024), np.float32)
    Sg = np.zeros((128, 1024), np.float32)
    P = np.zeros((128, 128), np.float32)
    for p in range(128):
        dd = p % 64
        ang = ar if dd < 32 else ac
        f = dd % 16
        C[p] = np.cos(ang[:, f])
        s = np.sin(ang[:, f])
        if dd % 32 < 16:
            Sg[p] = -s
            P[p + 16, p] = 1.0
        else:
            Sg[p] = s
            P[p - 16, p] = 1.0
    cidx = np.arange(64)
    c0 = np.clip(cidx - 8, 0, 48)
    ok = (cidx[None, :] >= c0[:, None]) & (cidx[None, :] < c0[:, None] + 16)
    return C, Sg, P, np.ascontiguousarray(ok.astype(np.float32)[::-1]), np.eye(128, dtype=np.float32)


def kernel(x_prompt, x_sample, c, cache_na_k, cache_na_v, cache_gqa_k, cache_gqa_v, state_lru, c_ctx,
           w_ada, b_ada, norm_g, ffn_w1, ffn_w2, w_in, na_rpb, conv_dw, conv_ln_g, conv_ln_b,
           gqa_q_norm, gqa_k_norm, lru_conv_w, lru_conv_b, lru_wr, lru_br, lru_wi, lru_bi, lru_lambda,
           w_branch, w_out):
    f = lambda a: np.ascontiguousarray(np.asarray(a, dtype=np.float32))
    ncores = 8
    if "nc" not in _PROG:
        _PROG["nc"] = build_program()
    nc = _PROG["nc"]
    ropeC, ropeS, permM, cmask, ident = _consts()

    OFF_NA, OFF_CONV, OFF_GQA, OFF_LRU, OFF_GATE = 0, 1536, 2560, 3328, 3840
    perm = []
    perm += list(range(0, 512))
    perm += list(range(512, 1024))
    perm += list(range(1024, 1536))
    perm += list(range(OFF_CONV, OFF_CONV + 1024))
    for cch in range(4):
        perm += list(range(OFF_GQA + cch * 64, OFF_GQA + cch * 64 + 64))
        perm += list(range(OFF_GQA + (cch + 4) * 64, OFF_GQA + (cch + 4) * 64 + 64))
    perm += list(range(OFF_GQA + 512, OFF_GQA + 768))
    perm += list(range(OFF_LRU, OFF_LRU + 512))
    for cch in range(8):
        for k in range(4):
            perm += list(range(OFF_GATE + k * 1024 + cch * 128, OFF_GATE + k * 1024 + cch * 128 + 128))
    perm = np.array(perm)
    assert perm.shape[0] == 7936
    w_in_r = f(np.asarray(w_in)[:, :, perm])
    wbr = np.asarray(w_branch, dtype=np.float32).copy()
    rp = []
    for cch in range(4):
        rp += list(range(cch * 64, cch * 64 + 64)) + list(range((cch + 4) * 64, (cch + 4) * 64 + 64))
    wbr[:, 2] = wbr[:, 2][:, np.array(rp), :]
    rpb_pad = np.zeros((DEPTH, 8, 15, 127), np.float32)
    rpb_pad[..., 48:79] = np.asarray(na_rpb)
    wr_ = np.asarray(lru_wr, dtype=np.float32)
    wi_ = np.asarray(lru_wi, dtype=np.float32)
    bdm = np.zeros((DEPTH, 16, 128, 128), np.float32)
    for dr in range(2):
        for gt, wsrc in enumerate((wr_, wi_)):
            for cch in range(4):
                m = bdm[:, dr * 8 + gt * 4 + cch]
                m[:, 0:64, 0:64] = wsrc[:, dr, 2 * cch]
                m[:, 64:128, 64:128] = wsrc[:, dr, 2 * cch + 1]
    lru_bd = f(bdm.transpose(0, 2, 1, 3))

    def colsT(v, n):
        return np.asarray(v, dtype=np.float32).reshape(n, 128).T

    shared = {
        "w_ada": f(w_ada), "ffn_w1": f(ffn_w1), "ffn_w2": f(ffn_w2), "w_in_r": w_in_r, "rpb_pad": rpb_pad,
        "lru_bd": lru_bd, "w_branch_r": f(wbr), "w_out": f(w_out), "ropeC": ropeC, "ropeS": ropeS,
        "permM": permM, "cmask": cmask, "ident": ident,
    }
    xp = np.asarray(x_prompt, dtype=np.float32)
    xs = np.asarray(x_sample, dtype=np.float32)
    in_maps = []
    for i in range(ncores):
        colp = np.zeros((DEPTH, 128, NCOL), np.float32)
        for l in range(DEPTH):
            colp[l, :, C_G:C_G + 48] = colsT(np.asarray(norm_g)[l].reshape(-1), 48)
            colp[l, :, C_BADA:C_BADA + 72] = colsT(np.asarray(b_ada)[l], 72)
            dw = np.asarray(conv_dw, dtype=np.float32)[l]
            colp[l, :, C_DW:C_DW + 124] = dw.reshape(31, 4, 128).transpose(2, 1, 0).reshape(128, 124)
            colp[l, :, C_LNG:C_LNG + 4] = colsT(np.asarray(conv_ln_g)[l], 4)
            colp[l, :, C_LNB:C_LNB + 4] = colsT(np.asarray(conv_ln_b)[l], 4)
            colp[l, :, C_QN] = np.tile(np.asarray(gqa_q_norm, dtype=np.float32)[l], 2)
            colp[l, :, C_KN] = np.tile(np.asarray(gqa_k_norm, dtype=np.float32)[l], 2)
            lw = np.asarray(lru_conv_w, dtype=np.float32)[l]
            colp[l, :, C_LCW:C_LCW + 16] = lw.reshape(4, 4, 128).transpose(2, 0, 1).reshape(128, 16)
            colp[l, :, C_LCB:C_LCB + 4] = colsT(np.asarray(lru_conv_b)[l], 4)
            colp[l, :, C_BR:C_BR + 8] = colsT(np.asarray(lru_br)[l].reshape(-1), 8)
            colp[l, :, C_BI:C_BI + 8] = colsT(np.asarray(lru_bi)[l].reshape(-1), 8)
            colp[l, :, C_LAM:C_LAM + 8] = colsT(np.asarray(lru_lambda)[l].reshape(-1), 8)
            colp[l, :, C_H0:C_H0 + 8] = colsT(np.asarray(state_lru)[i, l].reshape(-1), 8)
        m = dict(shared)
        m["x_in"] = f(np.concatenate([xp[2 * i].reshape(256, D), xp[2 * i + 1].reshape(256, D), xs[i]], axis=0))
        m["cc"] = f(np.stack([np.asarray(c_ctx), np.asarray(c)[i]], axis=0))
        m["ck_na"] = f(np.asarray(cache_na_k)[i].reshape(DEPTH, 512, 512))
        m["cv_na"] = f(np.asarray(cache_na_v)[i].reshape(DEPTH, 512, 512))
        m["ck_gq"] = f(np.asarray(cache_gqa_k)[i].reshape(DEPTH, 512, 128))
        m["cv_gq"] = f(np.asarray(cache_gqa_v)[i].reshape(DEPTH, 512, 128))
        m["colp"] = colp
        in_maps.append(m)

    res = run_bass_kernel_spmd(nc, in_maps, core_ids=list(range(ncores)))
    outs = res.results
    y_prompt = np.zeros((16, 256, D), np.float32)
    y_sample = np.zeros((8, 1024, D), np.float32)
    nak = np.zeros((16, DEPTH, 256, 8, 64), np.float32)
    nav = np.zeros((16, DEPTH, 256, 8, 64), np.float32)
    gqk = np.zeros((16, DEPTH, 256, 2, 64), np.float32)
    gqv = np.zeros((16, DEPTH, 256, 2, 64), np.float32)
    lst = np.zeros((16, DEPTH, 2, 512), np.float32)
    for i in range(ncores):
        o = outs[i]
        y = np.asarray(o["y_out"])
        y_prompt[2 * i] = y[0:256]
        y_prompt[2 * i + 1] = y[256:512]
        y_sample[i] = y[512:]
        for s in range(2):
            b = 2 * i + s
            nak[b] = np.asarray(o["o_nak"]).reshape(DEPTH, 512, 512)[:, s * 256:(s + 1) * 256].reshape(DEPTH, 256, 8, 64)
            nav[b] = np.asarray(o["o_nav"]).reshape(DEPTH, 512, 512)[:, s * 256:(s + 1) * 256].reshape(DEPTH, 256, 8, 64)
            gqk[b] = np.asarray(o["o_gqk"]).reshape(DEPTH, 512, 128)[:, s * 256:(s + 1) * 256].reshape(DEPTH, 256, 2, 64)
            gqv[b] = np.asarray(o["o_gqv"]).reshape(DEPTH, 512, 128)[:, s * 256:(s + 1) * 256].reshape(DEPTH, 256, 2, 64)
            fl = np.asarray(o["o_lru"]).reshape(DEPTH, 128, 16)
            for dr in range(2):
                blk = fl[:, :, (s * 2 + dr) * 4:(s * 2 + dr) * 4 + 4]
                lst[b, :, dr] = blk.transpose(0, 2, 1).reshape(DEPTH, 512)
    return (y_prompt, y_sample, nak, nav, gqk, gqv, lst)
```

```python
import numpy as np
from contextlib import ExitStack
import concourse.bass as bass
import concourse.mybir as mybir
from concourse.bass_utils import run_bass_kernel_spmd

F32 = mybir.dt.float32
BF16 = mybir.dt.bfloat16
U8 = mybir.dt.uint8
AF = mybir.ActivationFunctionType
ALU = mybir.AluOpType

DEPTH = 4
D = 1024
NT = 1536
DFF = 2816
EPS = 1e-6
NCOL = 320
C_G = 0
C_BADA = 48
C_DW = 120
C_LNG = 244
C_LNB = 248
C_QN = 252
C_KN = 253
C_LCW = 254
C_LCB = 270
C_BR = 274
C_BI = 282
C_LAM = 290
C_H0 = 298
SEGS = [(0, 256), (256, 256), (512, 1024)]


class Eng:
    def __init__(self, name):
        self.name = name
        self.ops = []
        self.count = 0
        self.seen = {}
        self.semkey = "E_" + name


class Sched:
    def __init__(self):
        self.E = {n: Eng(n) for n in ("pe", "act", "dve", "pool", "sp")}
        self.recs = {}
        self.keys = {"S": [], "P": []}
        self.ov = {}
        self.dsem = {}
        self.marks = []

    def mark(self, label):
        self.marks.append((label, {n: e.count for n, e in self.E.items()}))

    def _reg(self, key):
        if key in self.ov:
            return
        sp, lo, hi = key
        o = []
        for k2 in self.keys[sp]:
            if k2[1] < hi and lo < k2[2]:
                o.append(k2)
                self.ov[k2].append(key)
        self.ov[key] = o
        self.keys[sp].append(key)
        self.recs[key] = [None, {}]

    def _deps(self, eng, reads, writes, is_dma=False, skip_sem=None):
        toks = {}

        def add(t, raw):
            if t is None:
                return
            if t[0] == skip_sem:
                return
            if (not is_dma) and t[2] == eng.name:
                if eng.name == "pe":
                    return
            if toks.get(t[0], 0) < t[1]:
                toks[t[0]] = t[1]

        for r in reads:
            self._reg(r)
            for k in [r] + self.ov[r]:
                add(self.recs[k][0], True)
        for w in writes:
            self._reg(w)
            for k in [w] + self.ov[w]:
                rec = self.recs[k]
                add(rec[0], False)
                for t in rec[1].values():
                    add(t, False)
        return toks

    def _push(self, eng, toks, fn, semkey, inc):
        waits = []
        for k, v in toks.items():
            if eng.seen.get(k, 0) < v:
                eng.seen[k] = v
                waits.append((k, v))
        eng.ops.append((waits, fn, semkey, inc))

    def _record(self, tok, reads, writes):
        for r in reads:
            self.recs[r][1][tok[0]] = tok
        for w in writes:
            rec = self.recs[w]
            rec[0] = tok
            rec[1] = {}

    def op(self, en, fn, reads=(), writes=()):
        eng = self.E[en]
        reads = list(reads)
        writes = list(writes)
        toks = self._deps(eng, reads, writes)
        eng.count += 1
        tok = (eng.semkey, eng.count, eng.name)
        self._push(eng, toks, fn, eng.semkey, 1)
        self._record(tok, reads, writes)

    def dma(self, en, fn, region, write, group=False, reads=(), writes=()):
        eng = self.E[en]
        reads = list(reads) + ([] if write else [region])
        writes = list(writes) + ([region] if write else [])
        dk = (region, en == "pool")
        if dk not in self.dsem:
            self.dsem[dk] = ["D%d" % len(self.dsem), 0]
        ds = self.dsem[dk]
        toks = self._deps(eng, reads, writes, is_dma=True, skip_sem=ds[0] if group else None)
        ds[1] += 16
        tok = (ds[0], ds[1], None)
        self._push(eng, toks, fn, ds[0], 16)
        self._record(tok, reads, writes)

    def finish(self):
        eng = self.E["sp"]
        toks = {ds[0]: ds[1] for ds in self.dsem.values()}
        for e in self.E.values():
            if e.count:
                toks[e.semkey] = e.count
        self._push(eng, toks, None, None, 0)

    def emit(self, nc, es):
        keys = [e.semkey for e in self.E.values()] + [ds[0] for ds in self.dsem.values()]
        sems = {k: es.enter_context(nc.semaphore(k)) for k in keys}
        with nc.Block() as block:
            regs = {"pe": block.tensor, "act": block.scalar, "dve": block.vector,
                    "pool": block.gpsimd, "sp": block.sync}
            for name, reg in regs.items():
                eng = self.E[name]

                def f(e, eng=eng):
                    for waits, fn, sk, inc in eng.ops:
                        for k, v in waits:
                            e.wait_ge(sems[k], v)
                        if fn is not None:
                            fn(e).then_inc(sems[sk], inc)
                reg(f)


def _esz(dt):
    return 4 if dt == F32 else (2 if dt == BF16 else 1)


class T:
    def __init__(self, arena_ap, off, shape, dtype):
        n = int(np.prod(shape))
        self.off = off
        self.esz = _esz(dtype)
        self.nb = n * self.esz
        self.shape = tuple(shape)
        ap = arena_ap[:, off:off + self.nb].bitcast(dtype)
        if len(shape) == 2:
            ap = ap.rearrange("p (a b) -> p a b", a=shape[0])
        elif len(shape) == 3:
            ap = ap.rearrange("p (a b c) -> p a b c", a=shape[0], b=shape[1])
        elif len(shape) == 4:
            ap = ap.rearrange("p (a b c d) -> p a b c d", a=shape[0], b=shape[1], c=shape[2])
        self.ap = ap

    def k(self, lo=None, hi=None):
        if lo is None:
            return ("S", self.off, self.off + self.nb)
        return ("S", self.off + lo * self.esz, self.off + hi * self.esz)


class Arena:
    def __init__(self, ap, size):
        self.ap = ap
        self.size = size
        self.top = 0

    def alloc(self, shape, dtype):
        off = (self.top + 63) // 64 * 64
        t = T(self.ap, off, shape, dtype)
        self.top = off + t.nb
        assert self.top <= self.size, ("arena overflow", self.top, self.size)
        return t


def R(ap, *keys):
    return (ap, list(keys))


def build_program(depth=DEPTH, do=("ffn0", "mixer", "ffn2"), adaln_on=True, mx=("pq", "pk", "pv", "pvo", "pc", "nactx", "nalat", "conv", "gqa", "lru", "merge")):
    nc = bass.Bass("TRN2", target_bir_lowering=False)
    S = Sched()
    es = ExitStack()

    def din(name, shape):
        return nc.dram_tensor(name, list(shape), F32, kind="ExternalInput").ap()

    def dout(name, shape):
        return nc.dram_tensor(name, list(shape), F32, kind="ExternalOutput").ap()

    x_in = din("x_in", [NT, D])
    cc = din("cc", [2, D])
    ck_na = din("ck_na", [DEPTH, 512, 512])
    cv_na = din("cv_na", [DEPTH, 512, 512])
    ck_gq = din("ck_gq", [DEPTH, 512, 128])
    cv_gq = din("cv_gq", [DEPTH, 512, 128])
    colp_d = din("colp", [DEPTH, 128, NCOL])
    w_ada = din("w_ada", [DEPTH, D, 9 * D])
    ffn_w1 = din("ffn_w1", [DEPTH, 2, D, 2 * DFF])
    ffn_w2 = din("ffn_w2", [DEPTH, 2, DFF, D])
    w_in = din("w_in_r", [DEPTH, D, 7936])
    rpb = din("rpb_pad", [DEPTH, 8, 15, 127])
    lru_bd = din("lru_bd", [DEPTH, 128, 16, 128])
    w_br = din("w_branch_r", [DEPTH, 4, 512, D])
    w_out = din("w_out", [DEPTH, D, D])
    ropeC_d = din("ropeC", [128, 1024])
    ropeS_d = din("ropeS", [128, 1024])
    permM_d = din("permM", [128, 128])
    cmask_d = din("cmask", [64, 64])
    ident_d = din("ident", [128, 128])

    y_out = dout("y_out", [NT, D])
    o_nak = dout("o_nak", [DEPTH * 512, 512]).rearrange("(l t) f -> l t f", l=DEPTH)
    o_nav = dout("o_nav", [DEPTH * 512, 512]).rearrange("(l t) f -> l t f", l=DEPTH)
    o_gqk = dout("o_gqk", [DEPTH * 512, 128]).rearrange("(l t) f -> l t f", l=DEPTH)
    o_gqv = dout("o_gqv", [DEPTH * 512, 128]).rearrange("(l t) f -> l t f", l=DEPTH)
    o_lru = dout("o_lru", [DEPTH * 128, 16]).rearrange("(l t) f -> l t f", l=DEPTH)

    ARENA = 207 * 1024
    arena_t = es.enter_context(nc.sbuf_tensor("arena", [128, ARENA], U8))
    A = Arena(arena_t, ARENA)
    ps_t = es.enter_context(nc.psum_tensor("ps", [128, 8, 512], F32))

    def PS(b, rows=slice(0, 128), lo=0, hi=512):
        return R(ps_t[rows, b, lo:hi], ("P", b * 2048 + lo * 4, b * 2048 + hi * 4))

    def mm(out, lhsT, rhs, start, stop, **kw):
        S.op("pe", lambda e: e.matmul(out[0], lhsT=lhsT[0], rhs=rhs[0], start=start, stop=stop, **kw),
             reads=lhsT[1] + rhs[1], writes=out[1])

    def tr(out, in_, ident):
        S.op("pe", lambda e: e.transpose(out[0], in_[0], ident[0]), reads=in_[1] + ident[1], writes=out[1])

    def act(out, in_, func, bias=None, scale=1.0):
        rd = list(in_[1])
        kw = {}
        if bias is not None:
            kw["bias"] = bias[0]
            rd += bias[1]
        if isinstance(scale, tuple):
            rd += scale[1]
            sc = scale[0]
        else:
            sc = scale
        S.op("act", lambda e: e.activation(out=out[0], in_=in_[0], func=func, scale=sc, **kw),
             reads=rd, writes=out[1])

    def tt(en, out, a, b, op):
        S.op(en, lambda e: e.tensor_tensor(out[0], a[0], b[0], op), reads=a[1] + b[1], writes=out[1])

    def ts(en, out, a, s1, s2, op0, op1=None):
        rd = list(a[1])
        v1 = s1
        v2 = s2
        if isinstance(s1, tuple):
            rd += s1[1]
            v1 = s1[0]
        if isinstance(s2, tuple):
            rd += s2[1]
            v2 = s2[0]
        if op1 is None:
            S.op(en, lambda e: e.tensor_scalar(out[0], a[0], v1, None, op0), reads=rd, writes=out[1])
        else:
            S.op(en, lambda e: e.tensor_scalar(out[0], a[0], v1, v2, op0, op1), reads=rd, writes=out[1])

    def stt(out, in0, sc, in1, op0, op1):
        rd = in0[1] + in1[1]
        v = sc
        if isinstance(sc, tuple):
            rd = rd + sc[1]
            v = sc[0]
        S.op("dve", lambda e: e.scalar_tensor_tensor(out[0], in0[0], v, in1[0], op0, op1), reads=rd, writes=out[1])

    def cp(en, out, in_):
        if en == "act":
            S.op("act", lambda e: e.copy(out[0], in_[0]), reads=in_[1], writes=out[1])
        else:
            S.op(en, lambda e: e.tensor_copy(out[0], in_[0]), reads=in_[1], writes=out[1])

    def recip(out, in_):
        S.op("dve", lambda e: e.reciprocal(out[0], in_[0]), reads=in_[1], writes=out[1])

    def scan(out, d0, d1, init):
        rd = d0[1] + d1[1]
        v = init
        if isinstance(init, tuple):
            rd = rd + init[1]
            v = init[0]
        S.op("dve", lambda e: e.tensor_tensor_scan(out[0], d0[0], d1[0], v, ALU.mult, ALU.add), reads=rd, writes=out[1])

    def memset(en, out, val):
        S.op(en, lambda e: e.memset(out[0], val), writes=out[1])

    def dma_in(en, dst, src_ap, group=False):
        S.dma(en, lambda e: e.dma_start(out=dst[0], in_=src_ap), dst[1][0], True, group=group)

    def dma_out(en, dst_ap, src, **kw):
        S.dma(en, lambda e: e.dma_start(out=dst_ap, in_=src[0], **kw), src[1][0], False)

    ident = A.alloc([128], F32)
    identb = A.alloc([128], BF16)
    ones = A.alloc([128], BF16)
    bones = A.alloc([128], BF16)
    ones512 = A.alloc([128], BF16)
    permM = A.alloc([128], BF16)
    cmask = A.alloc([64], F32)
    epsc = A.alloc([1], F32)
    colp = A.alloc([DEPTH, NCOL], F32)
    scT = A.alloc([8, 2], BF16)
    rowb = A.alloc([512], F32)
    ones_f = A.alloc([1], F32)
    modT = [A.alloc([72, 2], F32) for _ in range(2)]
    Atab = [A.alloc([2, 3, 8], F32) for _ in range(2)]
    Btab = [A.alloc([2, 3, 8], F32) for _ in range(2)]

    Ri = R(ident.ap, ident.k())
    Rib = R(identb.ap, identb.k())
    Rones = R(ones.ap, ones.k())
    Rbones = R(bones.ap, bones.k())
    Rones512 = R(ones512.ap, ones512.k())
    Reps = R(epsc.ap, epsc.k())

    def col(l, j):
        return R(colp.ap[:, l, j:j + 1], colp.k())

    dma_in("sp", Ri, ident_d)
    dma_in("sp", R(cmask.ap[0:64, :], cmask.k()), cmask_d)
    for l in range(DEPTH):
        dma_in("sp", R(colp.ap[:, l, :], colp.k()), colp_d[l], group=l > 0)
    dma_in("pool", R(permM.ap, permM.k()), permM_d)
    cp("dve", Rib, Ri)
    memset("dve", Rones, 1.0)
    memset("dve", Rones512, 1.0 / 512.0)
    memset("dve", Rbones, 0.0)
    memset("dve", R(bones.ap[0:64, 0:64], bones.k()), 1.0)
    memset("dve", R(bones.ap[64:128, 64:128], bones.k()), 1.0)
    memset("dve", Reps, EPS)

    xT = A.alloc([8, NT], F32)
    hT = A.alloc([8, NT], BF16)
    NS = 3
    wslots = [A.alloc([4096], BF16) for _ in range(NS)]
    wctr = [0]
    phase_base = A.top

    def X(c, tb):
        return R(xT.ap[:, c, tb * 512:(tb + 1) * 512], xT.k(c * NT + tb * 512, c * NT + (tb + 1) * 512))

    def H(c, tb):
        return R(hT.ap[:, c, tb * 512:(tb + 1) * 512], hT.k(c * NT + tb * 512, c * NT + (tb + 1) * 512))

    def Hc(c, lo, hi):
        tb = lo // 512
        assert (hi - 1) // 512 == tb
        return R(hT.ap[:, c, lo:hi], hT.k(c * NT + tb * 512, c * NT + (tb + 1) * 512))

    def wslot():
        s = wslots[wctr[0] % NS]
        wctr[0] += 1
        return s

    def wload(src_ap, k, n):
        s = wslot()
        dst = s.ap[:, 0:k * n].rearrange("p (k n) -> p k n", k=k)
        dma_in("pool", R(dst, s.k()), src_ap)
        return R(dst, s.k())

    def wview(w2d):
        return w2d.rearrange("(k p) n -> p k n", p=128)

    sqb = [A.alloc([512], BF16) for _ in range(3)]
    rs3 = [A.alloc([512], F32) for _ in range(3)]
    tmpf = [A.alloc([512], F32) for _ in range(3)]
    sqi = [0]
    tfi = [0]
    phase_base = A.top

    def nsq():
        sqi[0] += 1
        t = sqb[sqi[0] % 3]
        return R(t.ap, t.k())

    def ntf():
        tfi[0] += 1
        t = tmpf[tfi[0] % 3]
        return R(t.ap, t.k())

    rsi = [0]

    def nrs():
        rsi[0] += 1
        t = rs3[rsi[0] % 3]
        return R(t.ap, t.k())

    def rstd_from(ps_in, scale):
        r = nrs()
        act(r, ps_in, AF.Ln, bias=Reps, scale=scale)
        act(r, r, AF.Exp, scale=-0.5)
        return r

    A.top = phase_base
    xst = [A.alloc([D], F32) for _ in range(2)]
    for tb in range(3):
        for q in range(4):
            b = tb * 4 + q
            st = xst[b % 2]
            Rst = R(st.ap, st.k())
            dma_in("sp", Rst, x_in[b * 128:(b + 1) * 128, :])
            for c in range(8):
                tr(PS(c, slice(0, 128), q * 128, (q + 1) * 128), R(st.ap[:, c * 128:(c + 1) * 128], st.k()), Ri)
        for c in range(8):
            cp("act" if c % 2 else "dve", X(c, tb), PS(c))

    ccs = xst[0]
    Rcc = R(ccs.ap[0:2, :], ccs.k())
    dma_in("sp", Rcc, cc)
    act(Rcc, Rcc, AF.Silu)
    for k in range(8):
        tr(PS(0, slice(0, 128), k * 2, k * 2 + 2), R(ccs.ap[0:2, k * 128:(k + 1) * 128], ccs.k()), R(ident.ap[0:2, 0:2], ident.k()))
    cp("dve", R(scT.ap, scT.k()), R(ps_t[:, 0, 0:16].rearrange("p (k w) -> p k w", k=8), PS(0, slice(0, 128), 0, 16)[1][0]))

    def adaln(l, par):
        mt = modT[par]
        A_ = Atab[par]
        B_ = Btab[par]
        wv = wview(w_ada[l])
        for g in range(18):
            sl = wload(wv[:, :, g * 512:(g + 1) * 512], 8, 512)
            for k in range(8):
                mm(PS(7, slice(0, 2)), R(scT.ap[:, k, :], scT.k()), R(sl[0][:, k, :], *sl[1]), k == 0, k == 7)
            cp("act", R(rowb.ap[0:2, 0:512], rowb.k()), PS(7, slice(0, 2)))
            for q in range(4):
                tr(PS(6, slice(0, 128), q * 2, q * 2 + 2), R(rowb.ap[0:2, q * 128:(q + 1) * 128], rowb.k()),
                   R(ident.ap[0:2, 0:2], ident.k()))
            cp("dve", R(mt.ap[:, g * 4:(g + 1) * 4, :], mt.k()),
               R(ps_t[:, 6, 0:8].rearrange("p (q w) -> p q w", q=4), PS(6, slice(0, 128), 0, 8)[1][0]))
            yield
        Rm = mt.k()
        for who in range(2):
            tt("dve", R(mt.ap[:, :, who], Rm), R(mt.ap[:, :, who], Rm), R(colp.ap[:, l, C_BADA:C_BADA + 72], colp.k()), ALU.add)
        for who in range(2):
            for sub in range(3):
                stt(R(A_.ap[:, who, sub, :], A_.k()), R(mt.ap[:, sub * 24 + 8:sub * 24 + 16, who], Rm), 1.0,
                    R(colp.ap[:, l, C_G + sub * 16:C_G + sub * 16 + 8], colp.k()), ALU.add, ALU.mult)
                stt(R(B_.ap[:, who, sub, :], B_.k()), R(mt.ap[:, sub * 24 + 16:sub * 24 + 24, who], Rm),
                    0.5 if sub != 1 else 1.0,
                    R(colp.ap[:, l, C_G + sub * 16 + 8:C_G + sub * 16 + 16], colp.k()), ALU.mult, ALU.mult)
        yield

    def run_bg(gen, n=1):
        if gen is None:
            return
        for _ in range(n):
            try:
                next(gen)
            except StopIteration:
                return

    def prenorm(par, sub):
        banks = [6, 7, 6]
        rsl = []
        for tb in range(3):
            for c in range(8):
                sq = nsq()
                act(sq, X(c, tb), AF.Square)
                mm(PS(banks[tb]), Rones, sq, c == 0, c == 7)
            rsl.append(rstd_from(PS(banks[tb]), 1.0 / D))
        for tb in range(3):
            who = 0 if tb == 0 else 1
            for c in range(8):
                t = ntf()
                stt(t, X(c, tb), R(Atab[par].ap[:, who, sub, c:c + 1], Atab[par].k()), rsl[tb], ALU.mult, ALU.mult)
                act(H(c, tb), t, AF.Identity,
                    bias=R(modT[par].ap[:, sub * 24 + c, who:who + 1], modT[par].k()))

    def post_update(par, sub, ybuf, sb):
        rsl = [rstd_from(PS(sb[tb]), 1.0 / D) for tb in range(3)]
        for tb in range(3):
            who = 0 if tb == 0 else 1
            for c in range(8):
                t = ntf()
                Ry = R(ybuf.ap[:, c, tb * 512:(tb + 1) * 512], ybuf.k(c * NT + tb * 512, c * NT + (tb + 1) * 512))
                tt("pool" if c % 2 else "dve", t, Ry, rsl[tb], ALU.mult)
                stt(X(c, tb), t, R(Btab[par].ap[:, who, sub, c:c + 1], Btab[par].k()), X(c, tb), ALU.mult, ALU.add)

    def outproj(get_w, nk, rhs_fn, ybuf, bg=None):
        banks = [0, 1, 2, 3, 4]
        bi = 0
        pend = None
        for c in range(8):
            wsl = get_w(c)
            for tb in range(3):
                b = banks[bi % 5]
                bi += 1
                for k in range(nk):
                    mm(PS(b), wsl(k), rhs_fn(k, tb), k == 0, k == nk - 1)
                if pend is not None:
                    mm(PS(5 + pend[0]), Rones, pend[1], pend[2] == 0, pend[2] == 7)
                Ry = R(ybuf.ap[:, c, tb * 512:(tb + 1) * 512], ybuf.k(c * NT + tb * 512, c * NT + (tb + 1) * 512))
                cp("act", Ry, PS(b))
                sq = nsq()
                act(sq, PS(b), AF.Square)
                pend = (tb, sq, c)
        mm(PS(5 + pend[0]), Rones, pend[1], pend[2] == 0, pend[2] == 7)

    def ffn(l, par, sub, bg=None):
        A.top = phase_base
        actb = A.alloc([22, NT], BF16)
        silb = [A.alloc([512], BF16) for _ in range(2)]
        prenorm(par, sub)
        w1v = wview(ffn_w1[l, sub // 2])
        w2v = wview(ffn_w2[l, sub // 2])
        pi = 0
        for j in range(11):
            s = wslot()
            dsta = s.ap[:, 0:2048].rearrange("p (k n) -> p k n", k=8)
            dstu = s.ap[:, 2048:4096].rearrange("p (k n) -> p k n", k=8)
            dma_in("pool", R(dsta, s.k()), w1v[:, :, j * 256:(j + 1) * 256])
            dma_in("pool", R(dstu, s.k()), w1v[:, :, DFF + j * 256:DFF + (j + 1) * 256], group=True)
            for half in range(2):
                m = 2 * j + half
                for tb in range(3):
                    ba = (pi % 3) * 2
                    pi += 1
                    for k in range(8):
                        mm(PS(ba), R(dsta[:, k, half * 128:(half + 1) * 128], s.k()), H(k, tb), k == 0, k == 7)
                    for k in range(8):
                        mm(PS(ba + 1), R(dstu[:, k, half * 128:(half + 1) * 128], s.k()), H(k, tb), k == 0, k == 7)
                    sb_ = silb[pi % 2]
                    Rs = R(sb_.ap, sb_.k())
                    act(Rs, PS(ba), AF.Silu)
                    Ra = R(actb.ap[:, m, tb * 512:(tb + 1) * 512], actb.k(m * NT + tb * 512, m * NT + (tb + 1) * 512))
                    tt("dve", Ra, PS(ba + 1), Rs, ALU.mult)
            run_bg(bg, 1)
        ybuf = hT

        def get_w(c):
            sl = wload(w2v[:, :, c * 128:(c + 1) * 128], 22, 128)
            return lambda k: R(sl[0][:, k, :], *sl[1])

        def rhs_fn(k, tb):
            return R(actb.ap[:, k, tb * 512:(tb + 1) * 512], actb.k(k * NT + tb * 512, k * NT + (tb + 1) * 512))

        outproj(get_w, 22, rhs_fn, ybuf, bg)
        post_update(par, sub, ybuf, [5, 6, 7])

    def mixer(l, par, bg=None):
        sub = 1
        A.top = phase_base
        Obr = []
        otop = []
        for _ in range(4):
            Obr.append(A.alloc([4, NT], BF16))
            otop.append(A.top)
        mix_base = A.top
        A.top = otop[0]
        prenorm(par, sub)
        wv = wview(w_in[l])
        wr = [0]

        def O(br, c, lo, hi, rows=slice(0, 128)):
            return R(Obr[br].ap[rows, c, lo:hi], Obr[br].k(c * NT, (c + 1) * NT))

        def proj_fm(col0, ncols, handler, banks, tbs=(0, 1, 2)):
            sl = wload(wv[:, :, col0:col0 + ncols], 8, ncols)
            for ci in range(ncols // 128):
                for tb in tbs:
                    b = banks[wr[0] % len(banks)]
                    wr[0] += 1
                    for k in range(8):
                        mm(PS(b), R(sl[0][:, k, ci * 128:(ci + 1) * 128], *sl[1]), H(k, tb), k == 0, k == 7)
                    handler(ci, tb, b)

        S.mark("L%d   na_proj" % l)
        qT = A.alloc([4, NT], BF16)
        kT = A.alloc([4, NT], BF16)
        ckT = A.alloc([4, 512], BF16)
        Vb = A.alloc([12, 512], BF16)
        cVb = A.alloc([4, 512], BF16)
        kst = [A.alloc([512], F32) for _ in range(2)]
        ost = [A.alloc([512], F32) for _ in range(2)]
        Ptl = [A.alloc([512], BF16) for _ in range(6)]
        rden = [A.alloc([512], F32) for _ in range(2)]
        Bq = A.alloc([17, 64], F32)
        Etab = A.alloc([24, 64], BF16)
        na_top = A.top
        cnt = {"k": 0, "o": 0, "p": 0, "r": 0, "s": 0}

        def rot(lst, key):
            cnt[key] += 1
            t = lst[cnt[key] % len(lst)]
            return t

        def QT(t, c, lo, hi, half=None):
            rows = slice(0, 128) if half is None else slice(half * 64, half * 64 + 64)
            return R(t.ap[rows, c, lo:hi], t.k(c * t.shape[1], (c + 1) * t.shape[1]))

        def h_q(ci, tb, b):
            cp("act" if (ci + tb) % 2 else "dve", QT(qT, ci, tb * 512, tb * 512 + 512), PS(b))

        if "pq" in mx:
            proj_fm(0, 512, h_q, [0, 1, 2, 3])

        def h_k(ci, tb, b):
            if tb != 0:
                cp("dve", QT(kT, ci, tb * 512, tb * 512 + 512), PS(b))
            if tb == 0:
                st = rot(kst, "k")
                Rst = R(st.ap, st.k())
                cp("act", Rst, PS(b))
                cp("dve", QT(kT, ci, tb * 512, tb * 512 + 512), Rst)
                for q in range(4):
                    tr(PS(4 + (cnt["k"] % 2), slice(0, 128), q * 128, q * 128 + 128),
                       R(st.ap[:, q * 128:(q + 1) * 128], st.k()), Ri)
                os_ = rot(ost, "o")
                Ros = R(os_.ap, os_.k())
                cp("act", Ros, PS(4 + (cnt["k"] % 2)))
                dma_out("sp", o_nak[l].rearrange("(q p) f -> p q f", p=128)[:, :, ci * 128:(ci + 1) * 128],
                        R(os_.ap.rearrange("p (q f) -> p q f", q=4), os_.k()))

        if "pk" in mx:
            proj_fm(512, 512, h_k, [0, 1, 2, 3])

        if "pv" in mx:
            slv = wload(wv[:, :, 1024:1536], 8, 512)
            for t128 in range(12):
                b = [0, 1, 2, 3][t128 % 4]
                for k in range(8):
                    mm(PS(b), Hc(k, t128 * 128, (t128 + 1) * 128), R(slv[0][:, k, :], *slv[1]), k == 0, k == 7)
                cp("act", R(Vb.ap[:, t128, :], Vb.k(t128 * 512, (t128 + 1) * 512)), PS(b))
                if t128 < 4 and "pvo" in mx:
                    os_ = rot(ost, "o")
                    Ros = R(os_.ap, os_.k())
                    cp("act", Ros, PS(b))
                    dma_out("sp", o_nav[l, t128 * 128:(t128 + 1) * 128, :], Ros)

        if "pc" in mx:
            for q in range(4):
                st = rot(kst, "k")
                Rst = R(st.ap, st.k())
                dma_in("sp", Rst, ck_na[l, q * 128:(q + 1) * 128, :])
                for c in range(4):
                    tr(PS(4 + (q % 2), slice(0, 128), c * 128, c * 128 + 128), R(st.ap[:, c * 128:(c + 1) * 128], st.k()), Ri)
                cp("dve", R(ckT.ap[:, :, q * 128:(q + 1) * 128], ckT.k()),
                   R(ps_t[:, 4 + (q % 2), :].rearrange("p (c t) -> p c t", c=4), PS(4 + (q % 2))[1][0]))
            dma_in("pool", R(cVb.ap, cVb.k()), cv_na[l].rearrange("(q p) f -> p q f", p=128))

        S.mark("L%d   na_ctx" % l)
        pend_norm = [None]

        def flush_norm():
            if pend_norm[0] is not None:
                pend_norm[0]()
                pend_norm[0] = None

        def attend(q_fn, keytiles, ncols, out_fn, half, bset):
            spool = bset["s"]
            ob, db = bset["o"], bset["d"]
            n = len(keytiles)
            LA = 3
            Pl = [None] * n
            for i in range(n + LA):
                if i < n:
                    kl, vl, (lo, hi), tab = keytiles[i]
                    sbk = spool[cnt["s"] % len(spool)]
                    cnt["s"] += 1
                    mm(PS(sbk, slice(0, 128), 0, hi - lo), kl, q_fn(lo, hi), True, True)
                    pt = rot(Ptl, "p")
                    Rp = R(pt.ap[:, 0:hi - lo], pt.k())
                    act(Rp, PS(sbk, slice(0, 128), 0, hi - lo), AF.Exp, scale=0.125)
                    if tab is not None:
                        Rp3 = R(pt.ap[:, 0:hi - lo].rearrange("p (e q) -> p e q", q=64), pt.k())
                        tt("dve", Rp3, Rp3, tab, ALU.mult)
                    Pl[i] = Rp
                if i == min(LA, n) - 1:
                    flush_norm()
                j = i - LA
                if j >= 0:
                    kl, vl, (lo, hi), tab = keytiles[j]
                    mm(PS(ob, slice(0, 128), lo, hi), vl, Pl[j], j == 0, j == n - 1, skip_group_check=True)
                    mm(PS(db, slice(0, 128), lo, hi), Rones, Pl[j], j == 0, j == n - 1, skip_group_check=True)

            def norm():
                rd = rot(rden, "r")
                rows = slice(half * 64, half * 64 + 64)
                Rr = R(rd.ap[rows, 0:ncols], rd.k())
                act(Rr, PS(db, rows, 0, ncols), AF.Ln)
                act(Rr, Rr, AF.Exp, scale=-1.0)
                tt("dve", out_fn(rows), PS(ob, rows, 0, ncols), Rr, ALU.mult)
            pend_norm[0] = norm

        bsets = [{"s": [0, 1, 2, 3], "o": 4, "d": 5}, {"s": [0, 1, 2, 3], "o": 6, "d": 7}]
        hcount = [0]

        def ctx_attention(qt, kt, vfn, br):
            for s in range(2):
                for h in range(8):
                    c, half = (h // 2, h % 2) if br == 0 else (h % 4, h // 4)
                    rows = slice(half * 64, half * 64 + 64)
                    tiles = []
                    for kb in range(2):
                        t0 = s * 256 + kb * 128
                        kl = R(kt[0][rows, kt[1](c), t0:t0 + 128], kt[2])
                        tiles.append((kl, vfn(s * 2 + kb, c), (0, 256), None))
                    bs = bsets[hcount[0] % 2]
                    hcount[0] += 1
                    attend(lambda lo, hi, c=c, rows=rows, s=s: R(qt.ap[rows, c, s * 256 + lo:s * 256 + hi], qt.k(c * NT, (c + 1) * NT)),
                           tiles, 256, lambda rws, c=c, s=s: O(br, c, s * 256, s * 256 + 256, rws), half, bs)

        if "nactx" in mx:
            ctx_attention(qT, (kT.ap, lambda c: c, kT.k()),
                          lambda t128, c: R(Vb.ap[:, t128, c * 128:(c + 1) * 128], Vb.k(t128 * 512, (t128 + 1) * 512)), 0)
        flush_norm()

        S.mark("L%d   na_lat" % l)
        if "nalat" in mx:
            def r0(r):
                return min(max(r - 4, 0), 8)

            def inwin(r, kr):
                return r0(r) <= kr <= r0(r) + 7

            Ek = Etab.k()
            RBq15 = R(Bq.ap[0:64, 1:16, :], Bq.k())

            def et_dma(h):
                if h == 0 and l == 0:
                    memset("dve", R(Bq.ap[0:64, :, :], Bq.k()), 0.0)
                src = bass.AP(rpb.tensor, rpb[l, h, 0, 0].offset, [[1, 64], [127, 15], [1, 64]])
                dma_in("sp", RBq15, src)

            def et_exp(h):
                act(RBq15, RBq15, AF.Exp)
                tt("dve", RBq15, RBq15, R(cmask.ap[0:64, :].unsqueeze(1).broadcast_to([64, 15, 64]), cmask.k()), ALU.mult)

            def et_tr(h):
                eb = (h % 2) * 2
                dl = list(range(7, -9, -1))
                for i, d in enumerate(dl):
                    tr(PS(eb + i // 8, slice(0, 128), (i % 8) * 64, (i % 8) * 64 + 64),
                       R(Bq.ap[0:64, d + 8:d + 10, :], Bq.k()), R(ident.ap[0:64, 0:64], ident.k()))
                cp("act", R(Etab.ap[:, 0:8, :], Ek), R(ps_t[:, eb, :].rearrange("p (e q) -> p e q", e=8)[:, :, ::-1], PS(eb)[1][0]))
                cp("act", R(Etab.ap[:, 8:12, :], Ek), R(ps_t[:, eb + 1, 0:256].rearrange("p (e q) -> p e q", e=4)[:, :, ::-1], PS(eb + 1)[1][0]))
                cp("act", R(Etab.ap[:, 12, :], Ek), R(ps_t[:, eb + 1, 256:320][:, ::-1], PS(eb + 1)[1][0]))
                memset("dve", R(Etab.ap[0:64, 12, :], Ek), 0.0)
                cp("act", R(Etab.ap[:, 13, :], Ek), R(ps_t[:, eb, 256:320][:, ::-1], PS(eb)[1][0]))
                memset("dve", R(Etab.ap[64:128, 13, :], Ek), 0.0)
                cp("act", R(Etab.ap[:, 14:17, :], Ek), R(ps_t[:, eb, 320:512].rearrange("p (e q) -> p e q", e=3)[:, :, ::-1], PS(eb)[1][0]))
                cp("act", R(Etab.ap[:, 17:24, :], Ek), R(ps_t[:, eb + 1, 0:448].rearrange("p (e q) -> p e q", e=7)[:, :, ::-1], PS(eb + 1)[1][0]))

            def tab_for(j, ra, rb):
                if j <= 3:
                    i0 = 7 - (2 * j - ra)
                    i1 = 7 - (2 * j - rb)
                    return R(Etab.ap[:, i0:i1 + 1, :], Ek)
                i0 = 13 + (3 - (2 * j - ra))
                i1 = 13 + (3 - (2 * j - rb))
                return R(Etab.ap[:, i0:i1 + 1, :], Ek)

            def na_piece(h, piece):
                c, half = h // 2, h % 2
                rows = slice(half * 64, half * 64 + 64)
                pr0, pr1 = piece * 8, piece * 8 + 7
                tiles = []
                for kb in range(4):
                    kl = R(ckT.ap[rows, c, kb * 128:(kb + 1) * 128], ckT.k())
                    vl = R(cVb.ap[:, kb, c * 128:(c + 1) * 128], cVb.k())
                    tiles.append((kl, vl, (0, 512), None))
                for j in range(8):
                    rr = [r for r in range(16) if inwin(r, 2 * j) or inwin(r, 2 * j + 1)]
                    ra, rb = max(rr[0], pr0), min(rr[-1], pr1)
                    if ra > rb:
                        continue
                    t0 = 512 + j * 128
                    kl = R(kT.ap[rows, c, t0:t0 + 128], kT.k(c * NT, (c + 1) * NT))
                    vl = R(Vb.ap[:, 4 + j, c * 128:(c + 1) * 128], Vb.k((4 + j) * 512, (5 + j) * 512))
                    tiles.append((kl, vl, ((ra - pr0) * 64, (rb - pr0 + 1) * 64), tab_for(j, ra, rb)))
                q0 = 512 + piece * 512
                bs = bsets[hcount[0] % 2]
                hcount[0] += 1
                attend(lambda lo, hi, c=c, rows=rows, q0=q0: R(qT.ap[rows, c, q0 + lo:q0 + hi], qT.k(c * NT, (c + 1) * NT)),
                       tiles, 512, lambda rws, c=c, q0=q0: O(0, c, q0, q0 + 512, rws), half, bs)

            et_dma(0)
            et_exp(0)
            et_tr(0)
            for h in range(8):
                if h < 7:
                    et_dma(h + 1)
                na_piece(h, 0)
                if h < 7:
                    et_exp(h + 1)
                na_piece(h, 1)
                if h < 7:
                    et_tr(h + 1)
            flush_norm()

        S.mark("L%d   conv" % l)
        if "conv" in mx:
            A.top = otop[1]
            SO = [0, 286, 572]
            ub = A.alloc([4, 1626], BF16)
            cv = A.alloc([4, NT], F32)
            Dg = A.alloc([31, 128], BF16)
            sgb = [A.alloc([512], F32) for _ in range(2)]
            memset("dve", R(ub.ap, ub.k()), 0.0)

            def useg(tb):
                if tb == 0:
                    return [(SO[0] + 15, 0, 256), (SO[1] + 15, 256, 256)]
                return [(SO[2] + 15 + (tb - 1) * 512, tb * 512, 512)]

            sla = wload(wv[:, :, 1536:2048], 8, 512)
            slg = wload(wv[:, :, 2048:2560], 8, 512)
            gi = 0
            for ci in range(4):
                for tb in range(3):
                    ba = (gi % 2) * 2
                    gi += 1
                    for k in range(8):
                        mm(PS(ba), R(sla[0][:, k, ci * 128:(ci + 1) * 128], *sla[1]), H(k, tb), k == 0, k == 7)
                    for k in range(8):
                        mm(PS(ba + 1), R(slg[0][:, k, ci * 128:(ci + 1) * 128], *slg[1]), H(k, tb), k == 0, k == 7)
                    sg = sgb[gi % 2]
                    Rsg = R(sg.ap, sg.k())
                    act(Rsg, PS(ba + 1), AF.Sigmoid)
                    for (u0, t0, n) in useg(tb):
                        o = t0 - tb * 512
                        tt("dve", R(ub.ap[:, ci, u0:u0 + n], ub.k(ci * 1626, (ci + 1) * 1626)),
                           PS(ba, slice(0, 128), o, o + n), R(sg.ap[:, o:o + n], sg.k()), ALU.mult)
            pieces = [(0, SO[0], 0, 256), (0, SO[1], 256, 256), (1, SO[2], 512, 512), (2, SO[2] + 512, 1024, 512)]
            mean_sb = A.alloc([3, 512], F32)
            for ci in range(4):
                for k in range(31):
                    ts("dve", R(Dg.ap[:, k, :], Dg.k()), Rib, col(l, C_DW + ci * 31 + k), None, ALU.mult)
                for pi_, (tb, u0, t0, n) in enumerate(pieces):
                    b = 4 + (pi_ % 2)
                    for k in range(31):
                        mm(PS(b, slice(0, 128), 0, n), R(Dg.ap[:, k, :], Dg.k()),
                           R(ub.ap[:, ci, u0 + k:u0 + k + n], ub.k(ci * 1626, (ci + 1) * 1626)), k == 0, k == 30)
                    cp("act", R(cv.ap[:, ci, t0:t0 + n], cv.k(ci * NT, (ci + 1) * NT)), PS(b, slice(0, 128), 0, n))
            lnb = [(6, 7), (4, 5), (2, 3)]
            for tb in range(3):
                for ci in range(4):
                    Rcv = R(cv.ap[:, ci, tb * 512:(tb + 1) * 512], cv.k(ci * NT, (ci + 1) * NT))
                    s1 = nsq()
                    cp("act", s1, Rcv)
                    mm(PS(lnb[tb][0]), Rones512, s1, ci == 0, ci == 3)
                    s2 = nsq()
                    act(s2, Rcv, AF.Square)
                    mm(PS(lnb[tb][1]), Rones512, s2, ci == 0, ci == 3)
            lnst = []
            for tb in range(3):
                Rmean = R(mean_sb.ap[:, tb, :], mean_sb.k(tb * 512, (tb + 1) * 512))
                cp("act", Rmean, PS(lnb[tb][0]))
                t = ntf()
                tt("dve", t, Rmean, Rmean, ALU.mult)
                tt("dve", t, PS(lnb[tb][1]), t, ALU.subtract)
                lnst.append((Rmean, rstd_from(t, 1.0)))
            for tb in range(3):
                Rmean, Rrstd = lnst[tb]
                for ci in range(4):
                    Rcv = R(cv.ap[:, ci, tb * 512:(tb + 1) * 512], cv.k(ci * NT, (ci + 1) * NT))
                    t = ntf()
                    tt("dve", t, Rcv, Rmean, ALU.subtract)
                    tt("dve", t, t, Rrstd, ALU.mult)
                    act(O(1, ci, tb * 512, (tb + 1) * 512), t, AF.Silu, bias=col(l, C_LNB + ci), scale=col(l, C_LNG + ci))
            run_bg(bg, 2)

        S.mark("L%d   gqa" % l)
        if "gqa" in mx:
            A.top = otop[2]
            gqT = A.alloc([4, NT], BF16)
            ropeC = A.alloc([1024], F32)
            ropeS = A.alloc([1024], F32)
            dma_in("sp", R(ropeC.ap, ropeC.k()), ropeC_d)
            dma_in("sp", R(ropeS.ap, ropeS.k()), ropeS_d)
            gkT = A.alloc([1, NT + 512], BF16)
            gV = A.alloc([16, 128], BF16)
            raw = [A.alloc([512], F32) for _ in range(2)]
            xnb = [A.alloc([512], BF16) for _ in range(2)]
            kst2 = [A.alloc([512], F32) for _ in range(1)]
            ost2 = [A.alloc([512], F32) for _ in range(1)]
            Ptl2 = [A.alloc([512], BF16) for _ in range(6)]
            rden2 = [A.alloc([512], F32) for _ in range(1)]
            Ptl[:] = Ptl2
            rden[:] = rden2
            slq = wload(wv[:, :, 2560:3072], 8, 512)
            slkv = wload(wv[:, :, 3072:3328], 8, 256)
            raw3 = raw + [A.alloc([512], F32)]
            calls = [("q", ci, tb) for ci in range(4) for tb in range(3)] + [("k", 0, tb) for tb in range(3)]
            cx = [dict() for _ in calls]

            def dst_of(n):
                kind, ci, tb = calls[n]
                if kind == "q":
                    return QT(gqT, ci, tb * 512, tb * 512 + 512)
                return R(gkT.ap[:, 0, tb * 512:(tb + 1) * 512], gkT.k())

            def P1(n):
                kind, ci, tb = calls[n]
                b = n % 3
                for k in range(8):
                    if kind == "q":
                        w = R(slq[0][:, k, ci * 128:(ci + 1) * 128], *slq[1])
                    else:
                        w = R(slkv[0][:, k, 0:128], *slkv[1])
                    mm(PS(b), w, H(k, tb), k == 0, k == 7)
                rw = raw3[n % 3]
                Rraw = R(rw.ap, rw.k())
                cp("act", Rraw, PS(b))
                sq = nsq()
                act(sq, Rraw, AF.Square)
                cx[n]["raw"] = Rraw
                cx[n]["sq"] = sq

            def P2(n):
                kind, ci, tb = calls[n]
                gcol = col(l, C_QN if kind == "q" else C_KN)
                Rraw = cx[n]["raw"]
                sbank = 6 if n % 2 else 4
                pbank = 7 if n % 2 else 5
                mm(PS(sbank), Rbones, cx[n]["sq"], True, True)
                RrsB = rstd_from(PS(sbank), 1.0 / 64)
                if tb == 0:
                    if kind == "k":
                        st = rot(kst2, "k")
                        Rst = R(st.ap, st.k())
                        stt(Rst, Rraw, gcol, RrsB, ALU.mult, ALU.mult)
                        cp("act", dst_of(n), Rst)
                        for q in range(4):
                            tr(PS(3, slice(0, 128), q * 128, q * 128 + 128), R(st.ap[:, q * 128:(q + 1) * 128], st.k()), Ri)
                        os_ = rot(ost2, "o")
                        Ros = R(os_.ap, os_.k())
                        cp("act", Ros, PS(3))
                        dma_out("sp", o_gqk[l].rearrange("(q p) f -> p q f", p=128),
                                R(os_.ap.rearrange("p (q f) -> p q f", q=4), os_.k()))
                    else:
                        stt(dst_of(n), Rraw, gcol, RrsB, ALU.mult, ALU.mult)
                    return
                stt(Rraw, Rraw, gcol, RrsB, ALU.mult, ALU.mult)
                xb_ = xnb[n % 2]
                Rxb = R(xb_.ap, xb_.k())
                cp("act", Rxb, Rraw)
                mm(PS(pbank), R(permM.ap, permM.k()), Rxb, True, True)
                cx[n]["pbank"] = pbank

            def P3(n):
                kind, ci, tb = calls[n]
                if tb == 0:
                    return
                Rraw = cx[n]["raw"]
                pbank = cx[n]["pbank"]
                t0 = (tb - 1) * 512
                t = ntf()
                tt("dve", t, PS(pbank), R(ropeS.ap[:, t0:t0 + 512], ropeS.k()), ALU.mult)
                tt("dve", Rraw, Rraw, R(ropeC.ap[:, t0:t0 + 512], ropeC.k()), ALU.mult)
                tt("dve", dst_of(n), Rraw, t, ALU.add)

            NCALL = len(calls)
            for n in range(NCALL + 2):
                if n < NCALL:
                    P1(n)
                if 0 <= n - 1 < NCALL:
                    P2(n - 1)
                if 0 <= n - 2 < NCALL:
                    P3(n - 2)
            for t128 in range(12):
                b = t128 % 4
                for k in range(8):
                    mm(PS(b, slice(0, 128), 0, 128), Hc(k, t128 * 128, (t128 + 1) * 128), R(slkv[0][:, k, 128:256], *slkv[1]), k == 0, k == 7)
                cp("act", R(gV.ap[:, t128, :], gV.k()), PS(b, slice(0, 128), 0, 128))
                if t128 < 4:
                    os_ = rot(ost2, "o")
                    Ros = R(os_.ap[:, 0:128], os_.k())
                    cp("act", Ros, PS(b, slice(0, 128), 0, 128))
                    dma_out("sp", o_gqv[l, t128 * 128:(t128 + 1) * 128, :], Ros)
            st = rot(kst2, "k")
            Rst = R(st.ap.rearrange("p (q f) -> p q f", q=4), st.k())
            dma_in("sp", Rst, ck_gq[l].rearrange("(q p) f -> p q f", p=128))
            for q in range(4):
                tr(PS(3, slice(0, 128), q * 128, q * 128 + 128), R(st.ap[:, q * 128:(q + 1) * 128], st.k()), Ri)
            cp("dve", R(gkT.ap[:, 0, NT:NT + 512], gkT.k()), PS(3))
            dma_in("pool", R(gV.ap[:, 12:16, :], gV.k()), cv_gq[l].rearrange("(q p) f -> p q f", p=128))

            RgkT = gkT.k()
            ctx_attention(gqT, (gkT.ap, lambda c: 0, RgkT),
                          lambda t128, c: R(gV.ap[:, t128, :], gV.k()), 2)
            for h in range(8):
                c, half = h % 4, h // 4
                rows = slice(half * 64, half * 64 + 64)
                for piece in range(2):
                    tiles = []
                    for kb in range(12):
                        t0 = 512 + kb * 128 if kb < 8 else NT + (kb - 8) * 128
                        vi = 4 + kb if kb < 8 else 12 + (kb - 8)
                        tiles.append((R(gkT.ap[rows, 0, t0:t0 + 128], RgkT), R(gV.ap[:, vi, :], gV.k()), (0, 512), None))
                    q0 = 512 + piece * 512
                    bs = bsets[hcount[0] % 2]
                    hcount[0] += 1
                    attend(lambda lo, hi, c=c, rows=rows, q0=q0: R(gqT.ap[rows, c, q0 + lo:q0 + hi], gqT.k(c * NT, (c + 1) * NT)),
                           tiles, 512, lambda rws, c=c, q0=q0: O(2, c, q0, q0 + 512, rws), half, bs)
            flush_norm()
            run_bg(bg, 2)

        S.mark("L%d   lru" % l)
        if "lru" in mx:
            A.top = otop[3]
            LO = [0, 259, 518]
            LW = 1545
            NJ = LW - 3
            xc = A.alloc([LW], F32)
            xcb = A.alloc([LW], BF16)
            bd = A.alloc([4, 128], BF16)
            ga = A.alloc([LW], F32)
            gb = A.alloc([LW], F32)
            g3 = A.alloc([LW], F32)
            g4 = A.alloc([LW], F32)
            fin = A.alloc([16], F32)
            lam8 = A.alloc([8], F32)
            Rlam = R(lam8.ap, lam8.k())
            act(Rlam, R(colp.ap[:, l, C_LAM:C_LAM + 8], colp.k()), AF.Exp, scale=-1.0)
            ts("dve", Rlam, Rlam, 1.0, None, ALU.add)
            act(Rlam, Rlam, AF.Ln)
            ts("dve", Rlam, Rlam, -8.0, None, ALU.mult)

            def lseg(tb):
                if tb == 0:
                    return [(LO[0] + 2, 0, 256), (LO[1] + 2, 256, 256)]
                return [(LO[2] + 2 + (tb - 1) * 512, tb * 512, 512)]

            gpieces = [(LO[0], 256), (LO[1], 256), (LO[2], 512), (LO[2] + 512, 512)]
            sll = wload(wv[:, :, 3328:3840], 8, 512)

            def lru_unit(ci, u):
                rlo, rhi = (0, 518) if u == 0 else (518, LW)
                jlo, jhi = (0, 515) if u == 0 else (518, NJ)
                segs = [(0, LO[0], 0, 256), (1, LO[1], 256, 256)] if u == 0 else [(2, LO[2], 512, 1024)]
                gps = gpieces[0:2] if u == 0 else gpieces[2:4]
                tbs = (0,) if u == 0 else (1, 2)
                bk = 4 if u == 0 else 6

                def K(t):
                    return t.k(rlo, rhi)

                memset("dve", R(g4.ap[:, rlo:rhi], K(g4)), 0.0)
                for tb in tbs:
                    for k in range(8):
                        mm(PS(tb), R(sll[0][:, k, ci * 128:(ci + 1) * 128], *sll[1]), H(k, tb), k == 0, k == 7)
                    for (p0, t0, n) in lseg(tb):
                        o = t0 - tb * 512
                        cp("act", R(g4.ap[:, p0:p0 + n], K(g4)), PS(tb, slice(0, 128), o, o + n))
                yield
                Rxc = R(xc.ap[:, jlo:jhi], K(xc))
                ts("dve", Rxc, R(g4.ap[:, jlo:jhi], K(g4)), col(l, C_LCW + 0 * 4 + ci), col(l, C_LCB + ci), ALU.mult, ALU.add)
                for k in range(1, 4):
                    stt(Rxc, R(g4.ap[:, jlo + k:jhi + k], K(g4)), col(l, C_LCW + k * 4 + ci), Rxc, ALU.mult, ALU.add)
                cp("act", R(xcb.ap[:, jlo:jhi], K(xcb)), Rxc)
                yield
                for dr in range(2):
                    gi_ = g3 if dr == 0 else g4
                    for (g0, n) in gps:
                        mm(PS(bk, slice(0, 128), 0, n), R(bd.ap[:, dr * 2, :], bd.k()), R(xcb.ap[:, g0:g0 + n], K(xcb)), True, True)
                        mm(PS(bk + 1, slice(0, 128), 0, n), R(bd.ap[:, dr * 2 + 1, :], bd.k()), R(xcb.ap[:, g0:g0 + n], K(xcb)), True, True)
                        act(R(ga.ap[:, g0:g0 + n], K(ga)), PS(bk, slice(0, 128), 0, n), AF.Sigmoid, bias=col(l, C_BR + dr * 4 + ci))
                        act(R(gi_.ap[:, g0:g0 + n], K(gi_)), PS(bk + 1, slice(0, 128), 0, n), AF.Sigmoid, bias=col(l, C_BI + dr * 4 + ci))
                    yield
                    for (si, g0, s0, n) in segs:
                        Ra_ = R(ga.ap[:, g0:g0 + n], K(ga))
                        act(Ra_, Ra_, AF.Exp, scale=R(lam8.ap[:, dr * 4 + ci:dr * 4 + ci + 1], lam8.k()))
                    yield
                    for (si, g0, s0, n) in segs:
                        Ra_ = R(ga.ap[:, g0:g0 + n], K(ga))
                        Rb_ = R(gb.ap[:, g0:g0 + n], K(gb))
                        tt("dve", Rb_, Ra_, Ra_, ALU.mult)
                    for (si, g0, s0, n) in segs:
                        Rb_ = R(gb.ap[:, g0:g0 + n], K(gb))
                        act(Rb_, Rb_, AF.Sqrt, bias=R(ones_f.ap, ones_f.k()), scale=-1.0)
                    yield
                    for (si, g0, s0, n) in segs:
                        Rb_ = R(gb.ap[:, g0:g0 + n], K(gb))
                        Ri_ = R(gi_.ap[:, g0:g0 + n], K(gi_))
                        tt("dve", Ri_, Ri_, R(xc.ap[:, g0:g0 + n], K(xc)), ALU.mult)
                        tt("dve", Rb_, Rb_, Ri_, ALU.mult)
                    yield
                    hd = gi_
                    for (si, g0, s0, sn) in segs:
                        init = 0.0 if si < 2 else col(l, C_H0 + dr * 4 + ci)
                        if dr == 0:
                            scan(R(hd.ap[:, g0:g0 + sn], K(hd)), R(ga.ap[:, g0:g0 + sn], K(ga)), R(gb.ap[:, g0:g0 + sn], K(gb)), init)
                        else:
                            scan(R(hd.ap[:, g0:g0 + sn][:, ::-1], K(hd)), R(ga.ap[:, g0:g0 + sn][:, ::-1], K(ga)),
                                 R(gb.ap[:, g0:g0 + sn][:, ::-1], K(gb)), init)
                        if si < 2:
                            pos = g0 + sn - 1 if dr == 0 else g0
                            cp("pool", R(fin.ap[:, (si * 2 + dr) * 4 + ci:(si * 2 + dr) * 4 + ci + 1], fin.k()), R(hd.ap[:, pos:pos + 1], K(hd)))
                    yield
                for (si, g0, s0, sn) in segs:
                    tt("dve", O(3, ci, s0, s0 + sn), R(g3.ap[:, g0:g0 + sn], K(g3)), R(g4.ap[:, g0:g0 + sn], K(g4)), ALU.add)

            for ci in range(4):
                for dr_ in range(2):
                    for gt_ in range(2):
                        dma_in("pool", R(bd.ap[:, dr_ * 2 + gt_, :], bd.k()), lru_bd[l, :, dr_ * 8 + gt_ * 4 + ci, :],
                               group=(dr_ + gt_ > 0))
                gens = [lru_unit(ci, 1), lru_unit(ci, 0)]
                while gens:
                    for g_ in list(gens):
                        try:
                            next(g_)
                        except StopIteration:
                            gens.remove(g_)
            dma_out("sp", o_lru[l], R(fin.ap, fin.k()))
            run_bg(bg, 2)

        S.mark("L%d   merge" % l)
        if "merge" in mx:
            A.top = otop[3]
            mixT = A.alloc([8, NT], BF16)
            sgm = [A.alloc([512], BF16) for _ in range(2)]
            accb = [A.alloc([512], F32) for _ in range(2)]
            mi = 0
            for c in range(8):
                swb = wslot()
                wbd = swb.ap[:, 0:2048].rearrange("p (k kc n) -> p k kc n", k=4, kc=4)
                for k in range(4):
                    dma_in("pool", R(wbd[:, k, :, :], swb.k()), wview(w_br[l, k])[:, :, c * 128:(c + 1) * 128], group=k > 0)
                slg_ = wload(wv[:, :, 3840 + c * 512:3840 + (c + 1) * 512], 8, 512)
                for tb in range(3):
                    acc = accb[(c * 3 + tb) % 2]
                    Racc = R(acc.ap, acc.k())
                    for k in range(4):
                        bp = (mi % 3) * 2
                        mi += 1
                        for kc in range(4):
                            mm(PS(bp), R(wbd[:, k, kc, :], swb.k()),
                               R(Obr[k].ap[:, kc, tb * 512:(tb + 1) * 512], Obr[k].k(kc * NT, (kc + 1) * NT)), kc == 0, kc == 3)
                        for kk in range(8):
                            mm(PS(bp + 1), R(slg_[0][:, kk, k * 128:(k + 1) * 128], *slg_[1]), H(kk, tb), kk == 0, kk == 7)
                        sg = sgm[mi % 2]
                        Rsg = R(sg.ap, sg.k())
                        act(Rsg, PS(bp + 1), AF.Sigmoid)
                        if k == 0:
                            tt("dve", Racc, PS(bp), Rsg, ALU.mult)
                        else:
                            t = ntf()
                            tt("dve", t, PS(bp), Rsg, ALU.mult)
                            if k < 3:
                                tt("dve", Racc, Racc, t, ALU.add)
                            else:
                                tt("dve", R(mixT.ap[:, c, tb * 512:(tb + 1) * 512], mixT.k(c * NT + tb * 512, c * NT + (tb + 1) * 512)),
                                   Racc, t, ALU.add)
                run_bg(bg, 1)
            wov = wview(w_out[l])
            wos = [None, None]

            def get_wo(c):
                if c % 4 == 0:
                    wos[0] = wload(wov[:, :, (c // 4) * 512:(c // 4 + 1) * 512], 8, 512)
                sl = wos[0]
                cc_ = c % 4
                return lambda k: R(sl[0][:, k, cc_ * 128:(cc_ + 1) * 128], *sl[1])

            def rhs_mix(k, tb):
                return R(mixT.ap[:, k, tb * 512:(tb + 1) * 512], mixT.k(k * NT + tb * 512, k * NT + (tb + 1) * 512))

            outproj(get_wo, 8, rhs_mix, hT, bg)
            post_update(par, sub, hT, [5, 6, 7])

    memset("dve", R(ones_f.ap, ones_f.k()), 1.0)

    if adaln_on:
        g0 = adaln(0, 0)
        for _ in g0:
            pass
    for l in range(depth):
        par = l % 2
        bg = adaln(l + 1, 1 - par) if (l + 1 < depth and adaln_on) else None
        S.mark("L%d ffn0" % l)
        if "ffn0" in do:
            ffn(l, par, 0, bg)
        S.mark("L%d mixer" % l)
        if "mixer" in do:
            mixer(l, par, bg)
        S.mark("L%d ffn2" % l)
        if "ffn2" in do:
            ffn(l, par, 2, bg)
        if bg is not None:
            for _ in bg:
                pass

    S.mark("final")
    A.top = phase_base
    yst = [A.alloc([D], F32) for _ in range(2)]
    for t128 in range(12):
        tb, q = t128 // 4, t128 % 4
        st = yst[t128 % 2]
        Rst = R(st.ap, st.k())
        bb = (t128 % 2) * 2
        for c in range(8):
            tr(PS(bb + c // 4, slice(0, 128), (c % 4) * 128, (c % 4) * 128 + 128),
               R(xT.ap[:, c, t128 * 128:(t128 + 1) * 128], xT.k(c * NT + tb * 512, c * NT + (tb + 1) * 512)), Ri)
        cp("act", R(st.ap[:, 0:512], st.k()), PS(bb))
        cp("dve", R(st.ap[:, 512:1024], st.k()), PS(bb + 1))
        dma_out("sp", y_out[t128 * 128:(t128 + 1) * 128, :], Rst)

    S.finish()
    S.emit(nc, es)
    es.close()
    nc._marks = S.marks
    nc._ndsem = len(S.dsem)
    nc._counts = {n: e.count for n, e in S.E.items()}
    return nc


_PROG = {}


def _consts():
    GRID_W = 64
    t = np.arange(1024)
    row = (t // GRID_W).astype(np.float32)
    colv = (t % GRID_W).astype(np.float32)
    half = 32
    freqs = (np.float32(10000.0) ** (-np.arange(0, half, 2, dtype=np.float32) / np.float32(half))).astype(np.float32)
    ar = row[:, None] * freqs
    ac = colv[:, None] * freqs
    C = np.zeros((128, 1024), np.float32)
    Sg = np.zeros((128, 1024), np.float32)
    P = np.zeros((128, 128), np.float32)
    for p in range(128):
        dd = p % 64
        ang = ar if dd < 32 else ac
        f = dd % 16
        C[p] = np.cos(ang[:, f])
        s = np.sin(ang[:, f])
        if dd % 32 < 16:
            Sg[p] = -s
            P[p + 16, p] = 1.0
        else:
            Sg[p] = s
            P[p - 16, p] = 1.0
    cidx = np.arange(64)
    c0 = np.clip(cidx - 8, 0, 48)
    ok = (cidx[None, :] >= c0[:, None]) & (cidx[None, :] < c0[:, None] + 16)
    return C, Sg, P, np.ascontiguousarray(ok.astype(np.float32)[::-1]), np.eye(128, dtype=np.float32)


def kernel(x_prompt, x_sample, c, cache_na_k, cache_na_v, cache_gqa_k, cache_gqa_v, state_lru, c_ctx,
           w_ada, b_ada, norm_g, ffn_w1, ffn_w2, w_in, na_rpb, conv_dw, conv_ln_g, conv_ln_b,
           gqa_q_norm, gqa_k_norm, lru_conv_w, lru_conv_b, lru_wr, lru_br, lru_wi, lru_bi, lru_lambda,
           w_branch, w_out):
    f = lambda a: np.ascontiguousarray(np.asarray(a, dtype=np.float32))
    ncores = 8
    if "nc" not in _PROG:
        _PROG["nc"] = build_program()
    nc = _PROG["nc"]
    ropeC, ropeS, permM, cmask, ident = _consts()

    OFF_NA, OFF_CONV, OFF_GQA, OFF_LRU, OFF_GATE = 0, 1536, 2560, 3328, 3840
    perm = []
    perm += list(range(0, 512))
    perm += list(range(512, 1024))
    perm += list(range(1024, 1536))
    perm += list(range(OFF_CONV, OFF_CONV + 1024))
    for cch in range(4):
        perm += list(range(OFF_GQA + cch * 64, OFF_GQA + cch * 64 + 64))
        perm += list(range(OFF_GQA + (cch + 4) * 64, OFF_GQA + (cch + 4) * 64 + 64))
    perm += list(range(OFF_GQA + 512, OFF_GQA + 768))
    perm += list(range(OFF_LRU, OFF_LRU + 512))
    for cch in range(8):
        for k in range(4):
            perm += list(range(OFF_GATE + k * 1024 + cch * 128, OFF_GATE + k * 1024 + cch * 128 + 128))
    perm = np.array(perm)
    assert perm.shape[0] == 7936
    w_in_r = f(np.asarray(w_in)[:, :, perm])
    wbr = np.asarray(w_branch, dtype=np.float32).copy()
    rp = []
    for cch in range(4):
        rp += list(range(cch * 64, cch * 64 + 64)) + list(range((cch + 4) * 64, (cch + 4) * 64 + 64))
    wbr[:, 2] = wbr[:, 2][:, np.array(rp), :]
    rpb_pad = np.zeros((DEPTH, 8, 15, 127), np.float32)
    rpb_pad[..., 48:79] = np.asarray(na_rpb)
    wr_ = np.asarray(lru_wr, dtype=np.float32)
    wi_ = np.asarray(lru_wi, dtype=np.float32)
    bdm = np.zeros((DEPTH, 16, 128, 128), np.float32)
    for dr in range(2):
        for gt, wsrc in enumerate((wr_, wi_)):
            for cch in range(4):
                m = bdm[:, dr * 8 + gt * 4 + cch]
                m[:, 0:64, 0:64] = wsrc[:, dr, 2 * cch]
                m[:, 64:128, 64:128] = wsrc[:, dr, 2 * cch + 1]
    lru_bd = f(bdm.transpose(0, 2, 1, 3))

    def colsT(v, n):
        return np.asarray(v, dtype=np.float32).reshape(n, 128).T

    shared = {
        "w_ada": f(w_ada), "ffn_w1": f(ffn_w1), "ffn_w2": f(ffn_w2), "w_in_r": w_in_r, "rpb_pad": rpb_pad,
        "lru_bd": lru_bd, "w_branch_r": f(wbr), "w_out": f(w_out), "ropeC": ropeC, "ropeS": ropeS,
        "permM": permM, "cmask": cmask, "ident": ident,
    }
    xp = np.asarray(x_prompt, dtype=np.float32)
    xs = np.asarray(x_sample, dtype=np.float32)
    in_maps = []
    for i in range(ncores):
        colp = np.zeros((DEPTH, 128, NCOL), np.float32)
        for l in range(DEPTH):
            colp[l, :, C_G:C_G + 48] = colsT(np.asarray(norm_g)[l].reshape(-1), 48)
            colp[l, :, C_BADA:C_BADA + 72] = colsT(np.asarray(b_ada)[l], 72)
            dw = np.asarray(conv_dw, dtype=np.float32)[l]
            colp[l, :, C_DW:C_DW + 124] = dw.reshape(31, 4, 128).transpose(2, 1, 0).reshape(128, 124)
            colp[l, :, C_LNG:C_LNG + 4] = colsT(np.asarray(conv_ln_g)[l], 4)
            colp[l, :, C_LNB:C_LNB + 4] = colsT(np.asarray(conv_ln_b)[l], 4)
            colp[l, :, C_QN] = np.tile(np.asarray(gqa_q_norm, dtype=np.float32)[l], 2)
            colp[l, :, C_KN] = np.tile(np.asarray(gqa_k_norm, dtype=np.float32)[l], 2)
            lw = np.asarray(lru_conv_w, dtype=np.float32)[l]
            colp[l, :, C_LCW:C_LCW + 16] = lw.reshape(4, 4, 128).transpose(2, 0, 1).reshape(128, 16)
            colp[l, :, C_LCB:C_LCB + 4] = colsT(np.asarray(lru_conv_b)[l], 4)
            colp[l, :, C_BR:C_BR + 8] = colsT(np.asarray(lru_br)[l].reshape(-1), 8)
            colp[l, :, C_BI:C_BI + 8] = colsT(np.asarray(lru_bi)[l].reshape(-1), 8)
            colp[l, :, C_LAM:C_LAM + 8] = colsT(np.asarray(lru_lambda)[l].reshape(-1), 8)
            colp[l, :, C_H0:C_H0 + 8] = colsT(np.asarray(state_lru)[i, l].reshape(-1), 8)
        m = dict(shared)
        m["x_in"] = f(np.concatenate([xp[2 * i].reshape(256, D), xp[2 * i + 1].reshape(256, D), xs[i]], axis=0))
        m["cc"] = f(np.stack([np.asarray(c_ctx), np.asarray(c)[i]], axis=0))
        m["ck_na"] = f(np.asarray(cache_na_k)[i].reshape(DEPTH, 512, 512))
        m["cv_na"] = f(np.asarray(cache_na_v)[i].reshape(DEPTH, 512, 512))
        m["ck_gq"] = f(np.asarray(cache_gqa_k)[i].reshape(DEPTH, 512, 128))
        m["cv_gq"] = f(np.asarray(cache_gqa_v)[i].reshape(DEPTH, 512, 128))
        m["colp"] = colp
        in_maps.append(m)

    res = run_bass_kernel_spmd(nc, in_maps, core_ids=list(range(ncores)))
    outs = res.results
    y_prompt = np.zeros((16, 256, D), np.float32)
    y_sample = np.zeros((8, 1024, D), np.float32)
    nak = np.zeros((16, DEPTH, 256, 8, 64), np.float32)
    nav = np.zeros((16, DEPTH, 256, 8, 64), np.float32)
    gqk = np.zeros((16, DEPTH, 256, 2, 64), np.float32)
    gqv = np.zeros((16, DEPTH, 256, 2, 64), np.float32)
    lst = np.zeros((16, DEPTH, 2, 512), np.float32)
    for i in range(ncores):
        o = outs[i]
        y = np.asarray(o["y_out"])
        y_prompt[2 * i] = y[0:256]
        y_prompt[2 * i + 1] = y[256:512]
        y_sample[i] = y[512:]
        for s in range(2):
            b = 2 * i + s
            nak[b] = np.asarray(o["o_nak"]).reshape(DEPTH, 512, 512)[:, s * 256:(s + 1) * 256].reshape(DEPTH, 256, 8, 64)
            nav[b] = np.asarray(o["o_nav"]).reshape(DEPTH, 512, 512)[:, s * 256:(s + 1) * 256].reshape(DEPTH, 256, 8, 64)
            gqk[b] = np.asarray(o["o_gqk"]).reshape(DEPTH, 512, 128)[:, s * 256:(s + 1) * 256].reshape(DEPTH, 256, 2, 64)
            gqv[b] = np.asarray(o["o_gqv"]).reshape(DEPTH, 512, 128)[:, s * 256:(s + 1) * 256].reshape(DEPTH, 256, 2, 64)
            fl = np.asarray(o["o_lru"]).reshape(DEPTH, 128, 16)
            for dr in range(2):
                blk = fl[:, :, (s * 2 + dr) * 4:(s * 2 + dr) * 4 + 4]
                lst[b, :, dr] = blk.transpose(0, 2, 1).reshape(DEPTH, 512)
    return (y_prompt, y_sample, nak, nav, gqk, gqv, lst)
```
